# Optimizing a Trainium2 kernel written in Bass

```python
import math
import jax, jax.numpy as jnp
from jax import lax
import numpy as np

D_MODEL = 1024
BATCH = 2
SEQ = 16384
DEPTH = 1

N_MEM = 256
HEAD_DIM = 64
DIFF_HEADS = D_MODEL // (2 * HEAD_DIM)
DIFF_V_DIM = 2 * HEAD_DIM
MOBA_HEADS = D_MODEL // (2 * HEAD_DIM)
MOBA_BLOCK = 256
MOBA_TOPK = 3
Q_BLOCK = 128
XATTN_HEADS = 4
XATTN_HEAD_DIM = D_MODEL // XATTN_HEADS
D_FF = 4 * D_MODEL
REL_BUCKETS = 32
REL_MAX_DIST = 128
N_SELF_HEADS = DIFF_HEADS + MOBA_HEADS
LN_EPS = 1e-5
DEEPNORM_ALPHA = (2.0 * DEPTH) ** 0.25
DEEPNORM_BETA = (8.0 * DEPTH) ** -0.25

COL_DQ = DIFF_HEADS * 2 * HEAD_DIM
COL_DK = DIFF_HEADS * 2 * HEAD_DIM
COL_DV = DIFF_HEADS * DIFF_V_DIM
COL_MQ = MOBA_HEADS * HEAD_DIM
COL_MK = MOBA_HEADS * HEAD_DIM
COL_MV = MOBA_HEADS * HEAD_DIM
COL_GATE = 2 * D_MODEL
W_IN_COLS = COL_DQ + COL_DK + COL_DV + COL_MQ + COL_MK + COL_MV + COL_GATE

kernel_name = "hybrid_diffattn_moba_gated_deepnorm"


def _split_points():
    widths = [COL_DQ, COL_DK, COL_DV, COL_MQ, COL_MK, COL_MV]
    return [int(v) for v in np.cumsum(widths)]


def layer_norm(x, g, b):
    xf = x.astype(jnp.float32)
    mu = jnp.mean(xf, axis=-1, keepdims=True)
    var = jnp.mean(jnp.square(xf - mu), axis=-1, keepdims=True)
    return ((xf - mu) * lax.rsqrt(var + LN_EPS) * g.astype(jnp.float32) + b.astype(jnp.float32)).astype(x.dtype)


def rel_bucket(n):
    max_exact = REL_BUCKETS // 2
    nf = jnp.maximum(n, 1).astype(jnp.float32)
    large = max_exact + (jnp.log(nf / max_exact) / math.log(REL_MAX_DIST / max_exact)
                         * (REL_BUCKETS - max_exact)).astype(jnp.int32)
    large = jnp.minimum(large, REL_BUCKETS - 1)
    return jnp.where(n < max_exact, n, large)


def diff_attention(q, k, v, bias_by_dist, lam, sub_g, lam_init):
    B, H, _, S, Dh = q.shape
    n_qb = S // Q_BLOCK
    kpos = jnp.arange(S)
    scale = HEAD_DIM ** -0.5

    def block(i):
        start = i * Q_BLOCK
        qb = lax.dynamic_slice_in_dim(q, start, Q_BLOCK, axis=3)
        qpos = start + jnp.arange(Q_BLOCK)
        dist = qpos[:, None] - kpos[None, :]
        bias = bias_by_dist[:, jnp.clip(dist, 0, S - 1)]
        logits = jnp.einsum('bhmqd,bhmkd->bhmqk', qb, k,
                            preferred_element_type=jnp.float32) * scale + bias[None, :, None]
        logits = jnp.where(dist >= 0, logits, -jnp.inf)
        p = jax.nn.softmax(logits, axis=-1)
        a = p[:, :, 0] - lam * p[:, :, 1]
        return jnp.einsum('bhqk,bhkd->bhqd', a.astype(v.dtype), v)

    out = lax.map(block, jnp.arange(n_qb))
    out = out.transpose(1, 2, 0, 3, 4).reshape(B, H, S, 2 * Dh)
    of = out.astype(jnp.float32)
    of = of * lax.rsqrt(jnp.mean(of * of, axis=-1, keepdims=True) + LN_EPS) * sub_g.astype(jnp.float32)
    of = of * (1.0 - lam_init)
    return of.transpose(0, 2, 1, 3).reshape(B, S, H * 2 * Dh).astype(v.dtype)


def moba_attention(q, k, v, bias_by_dist):
    B, H, S, Dh = q.shape
    nb = max(-(-S // MOBA_BLOCK), MOBA_TOPK)
    pad = nb * MOBA_BLOCK - S
    kp = jnp.pad(k, ((0, 0), (0, 0), (0, pad), (0, 0))).reshape(B, H, nb, MOBA_BLOCK, Dh)
    vp = jnp.pad(v, ((0, 0), (0, 0), (0, pad), (0, 0))).reshape(B, H, nb, MOBA_BLOCK, Dh)
    k_mean = jnp.mean(kp.astype(jnp.float32), axis=3)
    scale = Dh ** -0.5
    blk_ids = jnp.arange(nb)
    offs = jnp.arange(MOBA_BLOCK)
    b_idx = jnp.arange(B)[:, None, None, None]
    h_idx = jnp.arange(H)[None, :, None, None]
    h_idx5 = jnp.arange(H)[None, :, None, None, None]
    rank_valid_ids = jnp.arange(MOBA_TOPK)

    def chunk(i):
        start = i * Q_BLOCK
        cur = start // MOBA_BLOCK
        qc = lax.dynamic_slice_in_dim(q, start, Q_BLOCK, axis=2)
        qpos = start + jnp.arange(Q_BLOCK)
        gate = jnp.einsum('bhqd,bhnd->bhqn', qc.astype(jnp.float32), k_mean)
        gate = jnp.where(blk_ids < cur, gate, -jnp.inf)
        _, sel = lax.top_k(gate, MOBA_TOPK)
        sel_valid = rank_valid_ids < cur
        ks = kp[b_idx, h_idx, sel]
        vs = vp[b_idx, h_idx, sel]
        l_sel = jnp.einsum('bhqd,bhqnkd->bhqnk', qc, ks,
                           preferred_element_type=jnp.float32) * scale
        kpos_sel = sel[..., None] * MOBA_BLOCK + offs
        dist_sel = jnp.clip(qpos[None, None, :, None, None] - kpos_sel, 0, S - 1)
        l_sel = l_sel + bias_by_dist[h_idx5, dist_sel]
        l_sel = jnp.where(sel_valid[None, None, None, :, None], l_sel, -jnp.inf)
        ko = lax.dynamic_index_in_dim(kp, cur, axis=2, keepdims=False)
        vo = lax.dynamic_index_in_dim(vp, cur, axis=2, keepdims=False)
        kpos_own = cur * MOBA_BLOCK + offs
        dist_own = qpos[:, None] - kpos_own[None, :]
        l_own = jnp.einsum('bhqd,bhkd->bhqk', qc, ko,
                           preferred_element_type=jnp.float32) * scale
        l_own = l_own + bias_by_dist[:, jnp.clip(dist_own, 0, S - 1)][None]
        l_own = jnp.where(dist_own >= 0, l_own, -jnp.inf)
        logits = jnp.concatenate([l_sel.reshape(B, H, Q_BLOCK, MOBA_TOPK * MOBA_BLOCK), l_own], axis=-1)
        p = jax.nn.softmax(logits, axis=-1).astype(v.dtype)
        p_sel = p[..., :MOBA_TOPK * MOBA_BLOCK].reshape(B, H, Q_BLOCK, MOBA_TOPK, MOBA_BLOCK)
        p_own = p[..., MOBA_TOPK * MOBA_BLOCK:]
        return (jnp.einsum('bhqnk,bhqnkd->bhqd', p_sel, vs)
                + jnp.einsum('bhqk,bhkd->bhqd', p_own, vo))

    out = lax.map(chunk, jnp.arange(S // Q_BLOCK))
    return out.transpose(1, 0, 3, 2, 4).reshape(B, S, H * Dh)


def memory_cross_attention(h, mem, wq, wk, wv, wo):
    B, S, D = h.shape
    M = mem.shape[1]
    q = (h @ wq).reshape(B, S, XATTN_HEADS, XATTN_HEAD_DIM)
    k = (mem @ wk).reshape(B, M, XATTN_HEADS, XATTN_HEAD_DIM)
    v = (mem @ wv).reshape(B, M, XATTN_HEADS, XATTN_HEAD_DIM)
    logits = jnp.einsum('bshd,bmhd->bhsm', q, k, preferred_element_type=jnp.float32) * (XATTN_HEAD_DIM ** -0.5)
    p = jax.nn.softmax(logits, axis=-1).astype(v.dtype)
    o = jnp.einsum('bhsm,bmhd->bshd', p, v).reshape(B, S, D)
    return o @ wo


def setup_inputs(seed: int = 0) -> dict:
    key = jax.random.key(seed)
    ks = list(jax.random.split(key, 40))
    nrm = jax.random.normal
    D = D_MODEL
    sd = D ** -0.5
    beta = DEEPNORM_BETA
    widths = [COL_DQ, COL_DK, COL_DV, COL_MQ, COL_MK, COL_MV, COL_GATE]
    scales = [sd, sd, sd * beta, sd, sd, sd * beta, sd]
    w_in = jnp.concatenate([nrm(ks[20 + i], (DEPTH, D, w), jnp.float32) * s
                            for i, (w, s) in enumerate(zip(widths, scales))], axis=-1)

    def gain(k, n):
        return 1.0 + 0.02 * nrm(k, n, jnp.float32)

    def bias(k, n):
        return 0.02 * nrm(k, n, jnp.float32)

    return {
        "x": nrm(ks[0], (BATCH, SEQ, D), jnp.float32),
        "mem": nrm(ks[1], (BATCH, N_MEM, D), jnp.float32),
        "ln_in_g": gain(ks[2], (D,)),
        "ln_in_b": bias(ks[3], (D,)),
        "rel_table": 0.3 * nrm(ks[4], (REL_BUCKETS, N_SELF_HEADS), jnp.float32),
        "w_in": w_in,
        "b_gate": 0.01 * nrm(ks[5], (DEPTH, COL_GATE), jnp.float32),
        "lam_q1": 0.1 * nrm(ks[6], (DEPTH, HEAD_DIM), jnp.float32),
        "lam_k1": 0.1 * nrm(ks[7], (DEPTH, HEAD_DIM), jnp.float32),
        "lam_q2": 0.1 * nrm(ks[8], (DEPTH, HEAD_DIM), jnp.float32),
        "lam_k2": 0.1 * nrm(ks[9], (DEPTH, HEAD_DIM), jnp.float32),
        "diff_sub_g": gain(ks[10], (DEPTH, DIFF_V_DIM)),
        "w_br_diff": nrm(ks[11], (DEPTH, COL_DV, D), jnp.float32) * COL_DV ** -0.5,
        "w_br_moba": nrm(ks[12], (DEPTH, COL_MV, D), jnp.float32) * COL_MV ** -0.5,
        "w_out": nrm(ks[13], (DEPTH, D, D), jnp.float32) * sd * beta,
        "ln1_g": gain(ks[14], (DEPTH, D)),
        "ln1_b": bias(ks[15], (DEPTH, D)),
        "wq_x": nrm(ks[16], (DEPTH, D, D), jnp.float32) * sd,
        "wk_x": nrm(ks[17], (DEPTH, D, D), jnp.float32) * sd,
        "wv_x": nrm(ks[18], (DEPTH, D, D), jnp.float32) * sd * beta,
        "wo_x": nrm(ks[19], (DEPTH, D, D), jnp.float32) * sd * beta,
        "ln2_g": gain(ks[30], (DEPTH, D)),
        "ln2_b": bias(ks[31], (DEPTH, D)),
        "w_ff1": nrm(ks[32], (DEPTH, D, D_FF), jnp.float32) * sd * beta,
        "w_ff2": nrm(ks[33], (DEPTH, D_FF, D), jnp.float32) * D_FF ** -0.5 * beta,
        "ln3_g": gain(ks[34], (DEPTH, D)),
        "ln3_b": bias(ks[35], (DEPTH, D)),
    }


def reference(x, mem, ln_in_g, ln_in_b, rel_table, w_in, b_gate, lam_q1, lam_k1, lam_q2, lam_k2,
              diff_sub_g, w_br_diff, w_br_moba, w_out, ln1_g, ln1_b, wq_x, wk_x, wv_x, wo_x,
              ln2_g, ln2_b, w_ff1, w_ff2, ln3_g, ln3_b):
    B, S, D = x.shape
    bias_by_dist = rel_table[rel_bucket(jnp.arange(S))].T.astype(jnp.float32)
    bias_diff = bias_by_dist[:DIFF_HEADS]
    bias_moba = bias_by_dist[DIFF_HEADS:]
    splits = _split_points()

    h = layer_norm(x, ln_in_g, ln_in_b)
    for l in range(DEPTH):
        lam_init = 0.8 - 0.6 * math.exp(-0.3 * l)
        proj = h @ w_in[l]
        dq, dk, dv, mq, mk, mv, g_logit = jnp.split(proj, splits, axis=-1)
        dq = dq.reshape(B, S, DIFF_HEADS, 2, HEAD_DIM).transpose(0, 2, 3, 1, 4)
        dk = dk.reshape(B, S, DIFF_HEADS, 2, HEAD_DIM).transpose(0, 2, 3, 1, 4)
        dv = dv.reshape(B, S, DIFF_HEADS, DIFF_V_DIM).transpose(0, 2, 1, 3)
        lam = (jnp.exp(jnp.sum(lam_q1[l].astype(jnp.float32) * lam_k1[l].astype(jnp.float32)))
               - jnp.exp(jnp.sum(lam_q2[l].astype(jnp.float32) * lam_k2[l].astype(jnp.float32)))
               + lam_init)
        y_diff = diff_attention(dq, dk, dv, bias_diff, lam, diff_sub_g[l], lam_init) @ w_br_diff[l]
        mq = mq.reshape(B, S, MOBA_HEADS, HEAD_DIM).transpose(0, 2, 1, 3)
        mk = mk.reshape(B, S, MOBA_HEADS, HEAD_DIM).transpose(0, 2, 1, 3)
        mv = mv.reshape(B, S, MOBA_HEADS, HEAD_DIM).transpose(0, 2, 1, 3)
        y_moba = moba_attention(mq, mk, mv, bias_moba) @ w_br_moba[l]
        gates = jax.nn.sigmoid(g_logit + b_gate[l])
        g_diff, g_moba = jnp.split(gates, 2, axis=-1)
        mixed = (g_diff * y_diff + g_moba * y_moba) @ w_out[l]
        h = layer_norm(DEEPNORM_ALPHA * h + mixed, ln1_g[l], ln1_b[l])
        xa = memory_cross_attention(h, mem, wq_x[l], wk_x[l], wv_x[l], wo_x[l])
        h = layer_norm(DEEPNORM_ALPHA * h + xa, ln2_g[l], ln2_b[l])
        ff = jnp.square(jax.nn.relu(h @ w_ff1[l])) @ w_ff2[l]
        h = layer_norm(DEEPNORM_ALPHA * h + ff, ln3_g[l], ln3_b[l])
    return h
```

```python
import math
from contextlib import ExitStack
import numpy as np
import ml_dtypes
import concourse.bass as bass
import concourse.mybir as mybir
from concourse.bass_utils import run_bass_kernel_spmd

F32 = mybir.dt.float32
BF16 = mybir.dt.bfloat16
AF = mybir.ActivationFunctionType
ALU = mybir.AluOpType
AX = mybir.AxisListType

D = 1024
NT = 16384
NO = 4096
NEG = -30000.0
ALPHA = 2.0 ** 0.25
LAM_INIT = 0.2
EPS = 1e-5
C_DQ, C_DK, C_DV, C_MQ, C_MK, C_MV, C_G = 0, 1024, 2048, 3072, 3584, 4096, 4608


class Buf:
    __slots__ = ("name", "w", "r", "dsem", "dcnt")

    def __init__(self, name):
        self.name = name
        self.w = None
        self.r = {}
        self.dsem = None
        self.dcnt = 0


class Sched:
    def __init__(self, nc, es):
        self.nc = nc
        self.es = es
        self.E = {}
        for name, eng in (("pe", nc.tensor), ("act", nc.scalar), ("dve", nc.vector),
                          ("pool", nc.gpsimd), ("sp", nc.sync)):
            sem = es.enter_context(nc.semaphore("s_" + name))
            self.E[name] = dict(eng=eng, sem=sem, cnt=0, seen={})
        self.dbufs = []
        self.nwaits = 0

    def buf(self, name):
        return Buf(name)

    def bufs(self, name, n):
        return [Buf(f"{name}{i}") for i in range(n)]

    def _wait(self, en, ev):
        e = self.E[en]
        if e["seen"].get(ev[0], 0) >= ev[2]:
            return
        e["eng"].wait_ge(ev[1], ev[2])
        e["seen"][ev[0]] = ev[2]
        self.nwaits += 1

    def _deps(self, en, reads, writes):
        for b in reads:
            if b.w is not None:
                self._wait(en, b.w)
        for b in writes:
            if b.w is not None:
                self._wait(en, b.w)
            for ev in b.r.values():
                self._wait(en, ev)

    def op(self, en, fn, reads=(), writes=(), sig=True):
        e = self.E[en]
        if en == "pe":
            e["seen"]["pe"] = 1 << 60
        self._deps(en, reads, writes)
        ins = fn(e["eng"])
        if sig:
            e["cnt"] += 1
            ins.then_inc(e["sem"], 1)
            ev = (en, e["sem"], e["cnt"])
        else:
            ev = (en, e["sem"], e["cnt"] + 1)
        for b in reads:
            b.r[en] = ev
        for b in writes:
            b.w = ev
            b.r = {}
        return ins

    def dma(self, qn, out, in_, reads=(), writes=(), **kw):
        q = self.E[qn]
        self._deps(qn, reads, writes)
        b = writes[0]
        if b.dsem is None:
            b.dsem = self.es.enter_context(self.nc.semaphore("d_" + b.name))
            self.dbufs.append(b)
        ins = q["eng"].dma_start(out=out, in_=in_, **kw)
        b.dcnt += 16
        ins.then_inc(b.dsem, 16)
        key = ("d", id(b))
        ev = (key, b.dsem, b.dcnt)
        for r in reads:
            r.r[key] = ev
        for w in writes:
            w.w = ev
            w.r = {}
        return ins

    def barrier(self):
        for en, e in self.E.items():
            for on, o in self.E.items():
                if on != en and o["cnt"] > 0:
                    self._wait(en, (on, o["sem"], o["cnt"]))
            for b in self.dbufs:
                self._wait(en, (("d", id(b)), b.dsem, b.dcnt))


def build_program(stop_after=None, dbg=False):
    nc = bass.Bass("TRN2", target_bir_lowering=False)

    def din(name, shape, dt=F32):
        return nc.dram_tensor(name, list(shape), dt, kind="ExternalInput").ap()

    def dscr(name, shape, dt, kind="Internal"):
        return nc.dram_tensor(name, list(shape), dt, kind=kind).ap()

    _decl = {}
    _decl["x_b"] = ("x_b", lambda: [NT, D], F32, "ExternalInput")
    _decl["x_own"] = ("x_own", lambda: [NO, D], F32, "ExternalInput")
    _decl["mem_b"] = ("mem_b", lambda: [256, D], F32, "ExternalInput")
    _decl["ln_in_g"] = ("ln_in_g", lambda: [D], F32, "ExternalInput")
    _decl["ln_in_b"] = ("ln_in_b", lambda: [D], F32, "ExternalInput")
    _decl["rel_table"] = ("rel_table", lambda: [32, 16], F32, "ExternalInput")
    _decl["w_in"] = ("w_in", lambda: [D, 6656], F32, "ExternalInput")
    _decl["b_gate"] = ("b_gate", lambda: [2048], F32, "ExternalInput")
    _decl["lam_q1"] = ("lam_q1", lambda: [64], F32, "ExternalInput")
    _decl["lam_k1"] = ("lam_k1", lambda: [64], F32, "ExternalInput")
    _decl["lam_q2"] = ("lam_q2", lambda: [64], F32, "ExternalInput")
    _decl["lam_k2"] = ("lam_k2", lambda: [64], F32, "ExternalInput")
    _decl["diff_sub_g"] = ("diff_sub_g", lambda: [128], F32, "ExternalInput")
    _decl["w_br_diff"] = ("w_br_diff", lambda: [1024, D], F32, "ExternalInput")
    _decl["w_br_moba"] = ("w_br_moba", lambda: [512, D], F32, "ExternalInput")
    _decl["w_out"] = ("w_out", lambda: [D, D], F32, "ExternalInput")
    _decl["ln1_g"] = ("ln1_g", lambda: [D], F32, "ExternalInput")
    _decl["ln1_b"] = ("ln1_b", lambda: [D], F32, "ExternalInput")
    _decl["wq_x"] = ("wq_x", lambda: [D, D], F32, "ExternalInput")
    _decl["wk_x"] = ("wk_x", lambda: [D, D], F32, "ExternalInput")
    _decl["wv_x"] = ("wv_x", lambda: [D, D], F32, "ExternalInput")
    _decl["wo_x"] = ("wo_x", lambda: [D, D], F32, "ExternalInput")
    _decl["ln2_g"] = ("ln2_g", lambda: [D], F32, "ExternalInput")
    _decl["ln2_b"] = ("ln2_b", lambda: [D], F32, "ExternalInput")
    _decl["w_ff1"] = ("w_ff1", lambda: [D, 4096], F32, "ExternalInput")
    _decl["w_ff2"] = ("w_ff2", lambda: [4096, D], F32, "ExternalInput")
    _decl["ln3_g"] = ("ln3_g", lambda: [D], F32, "ExternalInput")
    _decl["ln3_b"] = ("ln3_b", lambda: [D], F32, "ExternalInput")
    _decl["t2"] = ("t2", lambda: [16, 128, 2560], F32, "ExternalInput")
    _decl["eind"] = ("eind", lambda: [64, NT], BF16, "ExternalInput")
    _decl["pastbias"] = ("pastbias", lambda: [8, 4, 128, 64], F32, "ExternalInput")
    _decl["pastvalid"] = ("pastvalid", lambda: [8, 4, 128, 64], F32, "ExternalInput")
    _decl["ownterm"] = ("ownterm", lambda: [8, 4, 128, 64], F32, "ExternalInput")
    _decl["ident_in"] = ("ident", lambda: [128, 128], BF16, "ExternalInput")
    _decl["out_d"] = ("out", lambda: [NO, D], F32, "ExternalOutput")
    _decl["KT"] = ("KT", lambda: [12, 128, NT], BF16, "dk")
    _decl["VSd"] = ("VSd", lambda: [8, 128, 128, 129], BF16, "dk")
    _decl["VSm"] = ("VSm", lambda: [8, 128, 128, 65], BF16, "dk")
    _decl["QT"] = ("QT", lambda: [8, 128, NO], BF16, "dk")
    _decl["QA"] = ("QA", lambda: [8, 128, NO], BF16, "dk")
    _decl["Gs"] = ("Gs", lambda: [16, 128, NO], BF16, "dk")
    _decl["Hs"] = ("Hs", lambda: [NO, D], F32, "dk")
    _decl["ATT"] = ("ATT", lambda: [12, 128, NO], BF16, "dk")
    _decl["wb_brd"] = ("wb_brd", lambda: [1024, D], BF16, "Internal")
    _decl["wb_brm"] = ("wb_brm", lambda: [512, D], BF16, "Internal")
    _decl["wb_out"] = ("wb_out", lambda: [D, D], BF16, "Internal")
    _decl["wb_q"] = ("wb_q", lambda: [D, D], BF16, "Internal")
    _decl["wb_k"] = ("wb_k", lambda: [D, D], BF16, "Internal")
    _decl["wb_v"] = ("wb_v", lambda: [D, D], BF16, "Internal")
    _decl["wb_o"] = ("wb_o", lambda: [D, D], BF16, "Internal")
    _decl["wb_f1"] = ("wb_f1", lambda: [D, 4096], BF16, "Internal")
    _decl["wb_f2"] = ("wb_f2", lambda: [4096, D], BF16, "Internal")
    dk = "ExternalOutput" if dbg else "Internal"
    _made = {}
    used_inputs = []

    def T(var):
        if var not in _made:
            name, shp, dt, kind = _decl[var]
            if kind == "dk":
                kind = dk
            _made[var] = nc.dram_tensor(name, list(shp()), dt, kind=kind).ap()
            if kind == "ExternalInput":
                used_inputs.append(name)
        return _made[var]

    with ExitStack() as es:
        S = Sched(nc, es)
        op, dma = S.op, S.dma
        evq = [0]

        def ev_eng():
            evq[0] += 1
            return "act" if evq[0] % 2 else "dve"

        def sbt(st, name, shape, dt):
            return st.enter_context(nc.sbuf_tensor(name, list(shape), dt))

        def pst(st, name, shape, dt):
            return st.enter_context(nc.psum_tensor(name, list(shape), dt))

        ident = sbt(es, "ident_sb", [128, 128], BF16); identB = S.buf("ident")
        dma("sp", ident[:], T("ident_in"), writes=[identB])
        kmT = sbt(es, "kmT", [128, 4, 64], F32); kmB = S.buf("kmT")
        epsc = sbt(es, "epsc", [128, 1], F32); epsB = S.buf("epsc")
        op("dve", lambda e: e.memset(epsc[:], EPS), writes=[epsB])
        scrB = {n: S.buf(n) for n in ("KT", "VSd", "VSm", "QT", "QA", "Gs", "Hs", "ATT", "out", "wb")}
        for dst, src in () if stop_after == "A1" else ((T("wb_brd"), T("w_br_diff")), (T("wb_brm"), T("w_br_moba")), (T("wb_out"), T("w_out")), (T("wb_q"), T("wq_x")), (T("wb_k"), T("wk_x")),
                         (T("wb_v"), T("wv_x")), (T("wb_o"), T("wo_x"))):
            dma("pool", dst, src, writes=[scrB["wb"]])
        for c in range(0 if stop_after == "A1" else 4):
            dma("pool", T("wb_f1")[c * 256:(c + 1) * 256, :], T("w_ff1")[c * 256:(c + 1) * 256, :], writes=[scrB["wb"]])
            dma("pool", T("wb_f2")[c * 1024:(c + 1) * 1024, :], T("w_ff2")[c * 1024:(c + 1) * 1024, :], writes=[scrB["wb"]])

        if stop_after == "A0":
            S.barrier()
            return nc, S, used_inputs
        def layer_norm(st, xt, xtB, gt, bt, gbB, out32, out32B, outbf, outbfB, nsub=4):
            s1, s2, mu, var, rstd, nmr, junk = st
            sB = s1[1]
            op("dve", lambda e: e.reduce_sum(out=s1[0][:, 0:nsub], in_=xt[:, 0:nsub, :], axis=AX.X), reads=[xtB], writes=[sB])
            for n in range(nsub):
                op("act", lambda e: e.activation(out=junk[0][:], in_=xt[:, n, :], func=AF.Square,
                                                 accum_out=s2[0][:, n:n + 1]), reads=[xtB], writes=[junk[1], s2[1]])
            op("dve", lambda e: e.tensor_scalar_mul(out=mu[0][:, 0:nsub], in0=s1[0][:, 0:nsub], scalar1=1.0 / D), reads=[sB], writes=[mu[1]])
            op("dve", lambda e: e.tensor_tensor(out=var[0][:, 0:nsub], in0=mu[0][:, 0:nsub], in1=mu[0][:, 0:nsub], op=ALU.mult),
               reads=[mu[1]], writes=[var[1]])
            op("dve", lambda e: e.scalar_tensor_tensor(out=var[0][:, 0:nsub], in0=s2[0][:, 0:nsub], scalar=1.0 / D, in1=var[0][:, 0:nsub],
                                                       op0=ALU.mult, op1=ALU.subtract), reads=[s2[1], var[1]], writes=[var[1]])
            op("dve", lambda e: e.tensor_scalar_add(out=var[0][:, 0:nsub], in0=var[0][:, 0:nsub], scalar1=EPS), reads=[var[1]], writes=[var[1]])
            op("act", lambda e: e.activation(out=rstd[0][:, 0:nsub], in_=var[0][:, 0:nsub], func=AF.Sqrt), reads=[var[1]], writes=[rstd[1]])
            op("dve", lambda e: e.reciprocal(out=rstd[0][:, 0:nsub], in_=rstd[0][:, 0:nsub]), reads=[rstd[1]], writes=[rstd[1]])
            op("dve", lambda e: e.scalar_tensor_tensor(out=nmr[0][:, 0:nsub], in0=mu[0][:, 0:nsub], scalar=-1.0, in1=rstd[0][:, 0:nsub],
                                                       op0=ALU.mult, op1=ALU.mult), reads=[mu[1], rstd[1]], writes=[nmr[1]])
            for n in range(nsub):
                op("act", lambda e: e.activation(out=xt[:, n, :], in_=xt[:, n, :], func=AF.Identity,
                                                 bias=nmr[0][:, n:n + 1], scale=rstd[0][:, n:n + 1]),
                   reads=[xtB, nmr[1], rstd[1]], writes=[xtB])
            for n in range(nsub):
                op("dve", lambda e: e.tensor_tensor(out=xt[:, n, :], in0=xt[:, n, :], in1=gt[:], op=ALU.mult),
                   reads=[xtB, gbB], writes=[xtB])
            for n in range(nsub):
                op("pool", lambda e: e.tensor_tensor(out=out32[:, n, :], in0=xt[:, n, :], in1=bt[:], op=ALU.add),
                   reads=[xtB, gbB], writes=[out32B])
            if outbf is not None:
                for n in range(nsub):
                    op(("dve", "pool")[n % 2], lambda e: e.tensor_copy(out=outbf[:, n, :], in_=out32[:, n, :]),
                       reads=[out32B], writes=[outbfB])

        def layer_norm_ap(st, xt, xtB, gt, bt, gbB, outbf, outbfB, epsc, epsB, nsub=4):
            s1, s2, mu, var, rstd, nmr, junk = st
            for n in range(nsub):
                op("act", lambda e: e.activation(out=junk[0][:], in_=xt[:, n, :], func=AF.Identity,
                                                 accum_out=s1[0][:, n:n + 1]), reads=[xtB], writes=[junk[1], s1[1]])
                op("act", lambda e: e.activation(out=junk[0][:], in_=xt[:, n, :], func=AF.Square,
                                                 accum_out=s2[0][:, n:n + 1]), reads=[xtB], writes=[junk[1], s2[1]])
            op("pool", lambda e: e.tensor_scalar_mul(out=mu[0][:, 0:nsub], in0=s1[0][:, 0:nsub], scalar1=1.0 / D), reads=[s1[1]], writes=[mu[1]])
            op("pool", lambda e: e.tensor_tensor(out=var[0][:, 0:nsub], in0=mu[0][:, 0:nsub], in1=mu[0][:, 0:nsub], op=ALU.mult),
               reads=[mu[1]], writes=[var[1]])
            op("pool", lambda e: e.tensor_scalar_mul(out=s2[0][:, 0:nsub], in0=s2[0][:, 0:nsub], scalar1=1.0 / D), reads=[s2[1]], writes=[s2[1]])
            op("pool", lambda e: e.tensor_tensor(out=var[0][:, 0:nsub], in0=s2[0][:, 0:nsub], in1=var[0][:, 0:nsub], op=ALU.subtract),
               reads=[s2[1], var[1]], writes=[var[1]])
            op("act", lambda e: e.activation(out=rstd[0][:, 0:nsub], in_=var[0][:, 0:nsub], func=AF.Ln, bias=epsc[:]), reads=[var[1], epsB], writes=[rstd[1]])
            op("act", lambda e: e.activation(out=rstd[0][:, 0:nsub], in_=rstd[0][:, 0:nsub], func=AF.Exp, scale=-0.5), reads=[rstd[1]], writes=[rstd[1]])
            op("pool", lambda e: e.tensor_tensor(out=nmr[0][:, 0:nsub], in0=mu[0][:, 0:nsub], in1=rstd[0][:, 0:nsub], op=ALU.mult),
               reads=[mu[1], rstd[1]], writes=[nmr[1]])
            op("pool", lambda e: e.tensor_scalar_mul(out=nmr[0][:, 0:nsub], in0=nmr[0][:, 0:nsub], scalar1=-1.0), reads=[nmr[1]], writes=[nmr[1]])
            for n in range(nsub):
                op("act", lambda e: e.activation(out=xt[:, n, :], in_=xt[:, n, :], func=AF.Identity,
                                                 bias=nmr[0][:, n:n + 1], scale=rstd[0][:, n:n + 1]),
                   reads=[xtB, nmr[1], rstd[1]], writes=[xtB])
                op("pool", lambda e: e.tensor_tensor(out=xt[:, n, :], in0=xt[:, n, :], in1=gt[:], op=ALU.mult),
                   reads=[xtB, gbB], writes=[xtB])
                op("pool", lambda e: e.tensor_tensor(out=outbf[:, n, :], in0=xt[:, n, :], in1=bt[:], op=ALU.add),
                   reads=[xtB, gbB], writes=[outbfB])

        def ln_state(st_, pfx):
            res = []
            for nm in ("s1", "s2", "mu", "var", "rstd", "nmr"):
                res.append((sbt(st_, pfx + nm, [128, 4], F32), S.buf(pfx + nm)))
            res.append((sbt(st_, pfx + "junk", [128, 1024], F32), S.buf(pfx + "junk")))
            return res

        def transpose_to(hb, hbB, hT, hTB, tps, tpB, tcnt, nsub=4, eng=None):
            for c in range(8):
                i = tcnt[0] % len(tps); tcnt[0] += 1
                for n in range(nsub):
                    op("pe", lambda e: e.transpose(out=tps[i][:, n * 128:(n + 1) * 128], in_=hb[:, n, c * 128:(c + 1) * 128],
                                                   identity=ident[:]),
                       reads=[hbB, identB], writes=[tpB[i]], sig=(n == nsub - 1))
                op(eng or ev_eng(), lambda e: cp(e, hT[:, c, 0:nsub * 128], tps[i][:, 0:nsub * 128]),
                   reads=[tpB[i]], writes=[hTB])

        def cp(e, out, in_):
            return e.copy(out=out, in_=in_) if e is nc.scalar else e.tensor_copy(out=out, in_=in_)

        with ExitStack() as ph:
            gt = sbt(ph, "gt", [128, D], F32); bt = sbt(ph, "bt", [128, D], F32); gbB = S.buf("gb")
            dma("sp", gt[:], T("ln_in_g").partition_broadcast(128), writes=[gbB])
            dma("sp", bt[:], T("ln_in_b").partition_broadcast(128), writes=[gbB])
            Wk = sbt(ph, "Wk", [128, 8, 1536], BF16); WkB = S.buf("Wk")
            Wv = sbt(ph, "Wv", [128, 8, 1536], BF16); WvB = S.buf("Wv")
            for c in range(8):
                rows = slice(c * 128, (c + 1) * 128)
                dma("pool", Wk[:, c, 0:1024], T("w_in")[rows, C_DK:C_DK + 1024], writes=[WkB])
                dma("pool", Wk[:, c, 1024:1536], T("w_in")[rows, C_MK:C_MK + 512], writes=[WkB])
                dma("pool", Wv[:, c, 0:1024], T("w_in")[rows, C_DV:C_DV + 1024], writes=[WvB])
                dma("pool", Wv[:, c, 1024:1536], T("w_in")[rows, C_MV:C_MV + 512], writes=[WvB])
            xts = [sbt(ph, f"xt{i}", [128, 4, D], F32) for i in range(2)]; xtBs = S.bufs("xt", 2)
            hbs = [sbt(ph, f"hb{i}", [128, 4, D], BF16) for i in range(2)]; hbBs = S.bufs("hb", 2)
            hTs = [sbt(ph, f"hT{i}", [128, 8, 512], BF16) for i in range(2)]; hTBs = S.bufs("hT", 2)
            kst = sbt(ph, "kst", [128, 12, 512], BF16); kstB = S.buf("kst")
            vstd = sbt(ph, "vstd", [128, 8, 4, 129], BF16); vstdB = S.buf("vstd")
            vstm = sbt(ph, "vstm", [128, 8, 4, 65], BF16); vstmB = S.buf("vstm")
            lst = ln_state(ph, "a_")
            tps = [pst(ph, f"tpa{i}", [128, 1024], BF16) for i in range(2)]; tpB = S.bufs("tpa", 2)
            mps = [pst(ph, f"mpa{i}", [128, 512], F32) for i in range(4)]; mpB = S.bufs("mpa", 4)
            op("dve", lambda e: e.memset(vstd[:, :, :, 128:129], 1.0), writes=[vstdB])
            op("dve", lambda e: e.memset(vstm[:, :, :, 64:65], 1.0), writes=[vstmB])
            tcnt = [0]; mcnt = [0]
            nta = 32 if stop_after != "A1" else 1
            def front_a(Tt):
                xt, xtB = xts[Tt % 2], xtBs[Tt % 2]
                dma("sp", xt[:], T("x_b")[Tt * 512:(Tt + 1) * 512, :].rearrange("(n p) d -> p n d", p=128), writes=[xtB])
                layer_norm_ap(lst, xt, xtB, gt, bt, gbB, hbs[Tt % 2], hbBs[Tt % 2], epsc, epsB)
                transpose_to(hbs[Tt % 2], hbBs[Tt % 2], hTs[Tt % 2], hTBs[Tt % 2], tps, tpB, tcnt, eng="dve")

            front_a(0)
            for Tt in range(nta):
                if Tt + 1 < nta:
                    front_a(Tt + 1)
                hT, hTB = hTs[Tt % 2], hTBs[Tt % 2]
                for g in range(12):
                    i = mcnt[0] % 4; mcnt[0] += 1
                    for c in range(8):
                        op("pe", lambda e: e.matmul(mps[i][:], lhsT=Wk[:, c, g * 128:(g + 1) * 128], rhs=hT[:, c, :],
                                                    start=(c == 0), stop=(c == 7)),
                           reads=[WkB, hTB], writes=[mpB[i]], sig=(c == 7))
                    op("dve", lambda e: cp(e, kst[:, g, :], mps[i][:]), reads=[mpB[i]], writes=[kstB])
                    if g >= 8:
                        op("dve", lambda e: e.reduce_sum(out=kmT[:, g - 8, 2 * Tt:2 * Tt + 2],
                                                         in_=mps[i][:].rearrange("p (a b) -> p a b", b=256), axis=AX.X),
                           reads=[mpB[i]], writes=[kmB])
                for g4 in range(3):
                    dma("pool", T("KT")[4 * g4:4 * g4 + 4, :, Tt * 512:(Tt + 1) * 512].rearrange("g p t -> p g t"), kst[:, 4 * g4:4 * g4 + 4, :],
                        reads=[kstB], writes=[scrB["KT"]])
                for n in range(4):
                    for cg in range(3):
                        i = mcnt[0] % 4; mcnt[0] += 1
                        for c in range(8):
                            op("pe", lambda e: e.matmul(mps[i][:], lhsT=hT[:, c, n * 128:(n + 1) * 128],
                                                        rhs=Wv[:, c, cg * 512:(cg + 1) * 512], start=(c == 0), stop=(c == 7)),
                               reads=[WvB, hTB], writes=[mpB[i]], sig=(c == 7))
                        if cg < 2:
                            op("dve", lambda e: cp(e, vstd[:, 4 * cg:4 * cg + 4, n, 0:128],
                                                      mps[i][:].rearrange("p (h d) -> p h d", d=128)),
                               reads=[mpB[i]], writes=[vstdB])
                        else:
                            op("dve", lambda e: cp(e, vstm[:, :, n, 0:64], mps[i][:].rearrange("p (h d) -> p h d", d=64)),
                               reads=[mpB[i]], writes=[vstmB])
                for h in range(8):
                    dma("pool", T("VSd")[h, :, 4 * Tt:4 * Tt + 4, :], vstd[:, h, :, :], reads=[vstdB], writes=[scrB["VSd"]])
                    dma("pool", T("VSm")[h, :, 4 * Tt:4 * Tt + 4, :], vstm[:, h, :, :], reads=[vstmB], writes=[scrB["VSm"]])
            S.barrier()
        if stop_after in ("A", "A1"):
            return nc, S, used_inputs

        with ExitStack() as ph:
            gt = sbt(ph, "gtb", [128, D], F32); bt = sbt(ph, "btb", [128, D], F32); gbB = S.buf("gbb")
            dma("sp", gt[:], T("ln_in_g").partition_broadcast(128), writes=[gbB])
            dma("sp", bt[:], T("ln_in_b").partition_broadcast(128), writes=[gbB])
            Wq = sbt(ph, "Wq", [128, 8, 1536], BF16); WqB = S.buf("Wq")
            Wg = sbt(ph, "Wg", [128, 8, 2048], BF16); WgB = S.buf("Wg")
            for c in range(8):
                rows = slice(c * 128, (c + 1) * 128)
                dma("pool", Wq[:, c, 0:1024], T("w_in")[rows, C_DQ:C_DQ + 1024], writes=[WqB])
                dma("pool", Wq[:, c, 1024:1536], T("w_in")[rows, C_MQ:C_MQ + 512], writes=[WqB])
                dma("pool", Wg[:, c, :], T("w_in")[rows, C_G:C_G + 2048], writes=[WgB])
            bgT = sbt(ph, "bgT", [128, 16], F32); bgB = S.buf("bgT")
            dma("sp", bgT[:], T("b_gate").rearrange("(g p) -> p g", p=128), writes=[bgB], allow_slow_non_contiguous=True)
            kml = sbt(ph, "kml", [128, 4, 64], F32); kmh = sbt(ph, "kmh", [128, 4, 64], F32); kmlB = S.buf("kml")
            op("dve", lambda e: e.memset(kml[:], 0.0), writes=[kmlB])
            op("dve", lambda e: e.memset(kmh[:], 0.0), writes=[kmlB])
            op("dve", lambda e: e.tensor_scalar_mul(out=kml[0:64, :, :], in0=kmT[0:64, :, :], scalar1=1.0 / 256), reads=[kmB], writes=[kmlB])
            op("dve", lambda e: e.tensor_scalar_mul(out=kmh[64:128, :, :], in0=kmT[64:128, :, :], scalar1=1.0 / 256), reads=[kmB], writes=[kmlB])
            xts = [sbt(ph, f"xtb{i}", [128, 4, D], F32) for i in range(2)]; xtBs = S.bufs("xtb", 2)
            hb = sbt(ph, "hbb", [128, 4, D], BF16); hbB = S.buf("hbb")
            hT = sbt(ph, "hTb", [128, 8, 512], BF16); hTB = S.buf("hTb")
            qst = sbt(ph, "qst", [128, 8, 512], BF16); qstB = S.buf("qst")
            qf = sbt(ph, "qf", [128, 4, 512], F32); qfB = S.buf("qf")
            qa = sbt(ph, "qa", [128, 8, 512], BF16); qaB = S.buf("qa")
            gst = sbt(ph, "gst", [128, 16, 512], BF16); gstB = S.buf("gst")
            pbt = sbt(ph, "pbt", [128, 4, 64], F32); pvt = sbt(ph, "pvt", [128, 4, 64], F32); ott = sbt(ph, "ott", [128, 4, 64], F32)
            pcB = S.buf("pc")
            gm = sbt(ph, "gm", [128, 64], F32); gmB = S.buf("gm")
            m8 = sbt(ph, "m8", [128, 8], F32); m8B = S.buf("m8")
            sel = sbt(ph, "sel", [128, 64], F32); selB = S.buf("sel")
            mpad = sbt(ph, "mpad", [128, 4, 128], BF16); mpadB = S.buf("mpad")
            op("dve", lambda e: e.memset(mpad[:], 0.0), writes=[mpadB])
            lst = ln_state(ph, "b_")
            tps = [pst(ph, f"tpb{i}", [128, 1024], BF16) for i in range(2)]; tpB = S.bufs("tpb", 2)
            mps = [pst(ph, f"mpb{i}", [128, 512], F32) for i in range(4)]; mpB = S.bufs("mpb", 4)
            gps = pst(ph, "gps", [128, 512], F32); gpsB = S.buf("gps")
            tcnt = [0]; mcnt = [0]
            for r in range(8):
                xt, xtB = xts[r % 2], xtBs[r % 2]
                tsl = slice(r * 512, (r + 1) * 512)
                dma("sp", xt[:], T("x_own")[tsl, :].rearrange("(n p) d -> p n d", p=128), writes=[xtB])
                dma("sp", pbt[:], T("pastbias")[r].rearrange("s p n -> p s n"), writes=[pcB])
                dma("sp", pvt[:], T("pastvalid")[r].rearrange("s p n -> p s n"), writes=[pcB])
                dma("sp", ott[:], T("ownterm")[r].rearrange("s p n -> p s n"), writes=[pcB])
                layer_norm(lst, xt, xtB, gt, bt, gbB, xt, xtB, hb, hbB)
                dma("pool", T("Hs")[tsl, :].rearrange("(n p) d -> p n d", p=128), xt[:], reads=[xtB], writes=[scrB["Hs"]])
                transpose_to(hb, hbB, hT, hTB, tps, tpB, tcnt)

                def proj(col0, M, dst_fn, eng=None):
                    i = mcnt[0] % 4; mcnt[0] += 1
                    for c in range(8):
                        op("pe", lambda e: e.matmul(mps[i][0:M, :], lhsT=W_[:, c, col0:col0 + M], rhs=hT[:, c, :],
                                                    start=(c == 0), stop=(c == 7)),
                           reads=[W_B, hTB], writes=[mpB[i]], sig=(c == 7))
                    dst_fn(mps[i], mpB[i])

                W_, W_B = Wq, WqB
                for g in range(8):
                    proj(g * 128, 128, lambda ps, psB: op("dve", lambda e: e.tensor_scalar_mul(out=qst[:, g, :], in0=ps[:], scalar1=0.125),
                                                          reads=[psB], writes=[qstB]))
                dma("pool", T("QT")[:, :, tsl].rearrange("g p t -> p g t"), qst[:], reads=[qstB], writes=[scrB["QT"]])
                for i4 in range(4):
                    proj(1024 + i4 * 128, 128, lambda ps, psB: op("act", lambda e: e.copy(out=qf[:, i4, :], in_=ps[:]),
                                                                  reads=[psB], writes=[qfB]))
                for h in range(8):
                    proj(1024 + h * 64, 64, lambda ps, psB: op("dve", lambda e: e.tensor_scalar_mul(out=qa[0:64, h, :], in0=ps[0:64, :], scalar1=0.125),
                                                               reads=[psB], writes=[qaB]))
                for h in range(8):
                    km = kml if h % 2 == 0 else kmh
                    for s in range(4):
                        op("pe", lambda e: e.matmul(gps[:, s * 64:(s + 1) * 64], lhsT=qf[:, h // 2, s * 128:(s + 1) * 128],
                                                    rhs=km[:, h // 2, :], start=True, stop=True),
                           reads=[qfB, kmlB], writes=[gpsB])
                    for s in range(4):
                        op("dve", lambda e: e.tensor_tensor(out=gm[:], in0=gps[:, s * 64:(s + 1) * 64], in1=pbt[:, s, :], op=ALU.add),
                           reads=[gpsB, pcB], writes=[gmB])
                        op("dve", lambda e: e.max(out=m8[:], in_=gm[:]), reads=[gmB], writes=[m8B])
                        op("dve", lambda e: e.scalar_tensor_tensor(out=sel[:], in0=gm[:], scalar=m8[:, 2:3], in1=pvt[:, s, :],
                                                                   op0=ALU.is_ge, op1=ALU.mult), reads=[gmB, m8B, pcB], writes=[selB])
                        op("dve", lambda e: e.scalar_tensor_tensor(out=mpad[:, s, 64:128], in0=sel[:], scalar=-NEG, in1=ott[:, s, :],
                                                                   op0=ALU.mult, op1=ALU.add), reads=[selB, pcB], writes=[mpadB])
                    i = tcnt[0] % 2; tcnt[0] += 1
                    for s in range(4):
                        op("pe", lambda e: e.transpose(out=tps[i][:, s * 128:(s + 1) * 128], in_=mpad[:, s, :], identity=ident[:]),
                           reads=[mpadB, identB], writes=[tpB[i]], sig=(s == 3))
                    op("act", lambda e: e.copy(out=qa[64:128, h, :], in_=tps[i][64:128, 0:512]), reads=[tpB[i]], writes=[qaB])
                dma("pool", T("QA")[:, :, tsl].rearrange("h p t -> p h t"), qa[:], reads=[qaB], writes=[scrB["QA"]])
                W_, W_B = Wg, WgB
                for gg in range(16):
                    proj(gg * 128, 128, lambda ps, psB: op("act", lambda e: e.activation(out=gst[:, gg, :], in_=ps[:], func=AF.Sigmoid,
                                                                                         bias=bgT[:, gg:gg + 1]),
                                                           reads=[psB, bgB], writes=[gstB]))
                dma("pool", T("Gs")[:, :, tsl].rearrange("g p t -> p g t"), gst[:], reads=[gstB], writes=[scrB["Gs"]])
            S.barrier()
        if stop_after == "B":
            return nc, S, used_inputs

        with ExitStack() as ph:
            KTb = sbt(ph, "KTb", [128, NT], BF16); KTB = S.bufs("KTc", 8)
            VTb = sbt(ph, "VTb", [128, 128, 129], BF16); VTB = S.bufs("VTc", 8)
            Bb = [sbt(ph, f"Bb{i}", [128, 17, 512], BF16) for i in range(2)]; BB = S.bufs("Bb", 2)
            Qb = [sbt(ph, f"Qb{i}", [128, NO], BF16) for i in range(2)]; QB = S.bufs("Qb", 2)
            chc = [sbt(ph, f"chc{i}", [128, 1], F32) for i in range(2)]; chB = S.bufs("chc", 2)
            Btmp = [sbt(ph, f"Btmp{i}", [128, 4, 512], F32) for i in range(2)]; BtB = S.bufs("Btmp", 2); bcnt = [0]
            zc = sbt(ph, "zc", [128, 1], F32); zcB = S.buf("zc")
            op("dve", lambda e: e.memset(zc[:], 0.0), writes=[zcB])
            PT = [sbt(ph, f"PT{i}", [128, 1024], BF16) for i in range(3)]; PTB = S.bufs("PT", 3)
            lq = sbt(ph, "lq", [128, 4, 64], F32); lqB = S.buf("lq")
            for i, a in enumerate((T("lam_q1"), T("lam_k1"), T("lam_q2"), T("lam_k2"))):
                dma("sp", lq[:, i, :], a.partition_broadcast(128), writes=[lqB])
            lw = sbt(ph, "lw", [128, 2, 64], F32); ls = sbt(ph, "ls", [128, 4], F32); lwB = S.buf("lw")
            op("dve", lambda e: e.tensor_tensor(out=lw[:, 0, :], in0=lq[:, 0, :], in1=lq[:, 1, :], op=ALU.mult), reads=[lqB], writes=[lwB])
            op("dve", lambda e: e.tensor_tensor(out=lw[:, 1, :], in0=lq[:, 2, :], in1=lq[:, 3, :], op=ALU.mult), reads=[lqB], writes=[lwB])
            op("dve", lambda e: e.reduce_sum(out=ls[:, 0:2], in_=lw[:], axis=AX.X), reads=[lwB], writes=[lwB])
            op("act", lambda e: e.activation(out=ls[:, 0:2], in_=ls[:, 0:2], func=AF.Exp), reads=[lwB], writes=[lwB])
            op("dve", lambda e: e.tensor_tensor(out=ls[:, 2:3], in0=ls[:, 1:2], in1=ls[:, 0:1], op=ALU.subtract), reads=[lwB], writes=[lwB])
            op("dve", lambda e: e.tensor_scalar_add(out=ls[:, 2:3], in0=ls[:, 2:3], scalar1=-LAM_INIT), reads=[lwB], writes=[lwB])
            nlam = ls[:, 2:3]
            sg = sbt(ph, "sg", [128, 128], F32); sgB = S.buf("sg")
            dma("sp", sg[:], T("diff_sub_g").partition_broadcast(128), writes=[sgB])
            op("dve", lambda e: e.tensor_scalar_mul(out=sg[:], in0=sg[:], scalar1=1.0 - LAM_INIT), reads=[sgB], writes=[sgB])
            rc = sbt(ph, "rc", [128, 8], F32); rcB = S.buf("rc")
            oa = sbt(ph, "oa", [128, 128], F32); oaB = S.buf("oa")
            od = sbt(ph, "od", [128, 128], F32); odB = S.buf("od")
            ojk = sbt(ph, "ojk", [128, 128], F32); ojB = S.buf("ojk")
            ob = sbt(ph, "ob", [128, 4, 128], BF16); obB = S.buf("ob")
            osb = sbt(ph, "osb", [128, 3, 387], F32); osbB = S.buf("osb")
            od4 = sbt(ph, "od4", [128, 4, 128], F32)
            rq = sbt(ph, "rq", [128, 16], F32); rqB = S.buf("rq")
            ast = [sbt(ph, f"ast{i}", [128, 512], BF16) for i in range(2)]; astB = S.bufs("ast", 2)
            Sps = [pst(ph, f"Sps{i}", [128, 1024], F32) for i in range(2)]; SpB = S.bufs("Sps", 2)
            Ops = [pst(ph, f"Ops{i}", [128, 512], F32) for i in range(3)]; OpB = S.buf("Ops")
            tpc = pst(ph, "tpc", [128, 1024], BF16); tpcB = S.buf("tpc")
            scnt = [0]; pcnt = [0]; acnt = [0]
            heads = range(16) if stop_after != "C1" else [0, 8]
            nslots = 8 if stop_after != "C1" else 2
            for hi, hh in enumerate(heads):
                moba = hh >= 8
                hm = hh - 8
                par = hi % 2
                dma("sp", chc[par][:], T("rel_table")[31, hh:hh + 1].partition_broadcast(128), writes=[chB[par]])
                if not moba:
                    dma("sp", Qb[par][:], T("QT")[hh], reads=[scrB["QT"]], writes=[QB[par]])
                else:
                    dma("sp", Qb[par][:], T("QA")[hm], reads=[scrB["QA"]], writes=[QB[par]])
                for ui, u0 in enumerate(range(0, 17, 4)):
                    u1 = min(17, u0 + 4)
                    src = bass.AP(tensor=T("t2").tensor, offset=T("t2")[hh, 0, u0 * 128].offset, ap=[[2560, 128], [128, u1 - u0], [1, 512]])
                    bi = bcnt[0] % 2; bcnt[0] += 1
                    dma("sp", Btmp[bi][:, 0:u1 - u0, :], src, writes=[BtB[bi]])
                    op("dve", lambda e: e.tensor_scalar(out=Btmp[bi][:, 0:u1 - u0, :], in0=Btmp[bi][:, 0:u1 - u0, :], scalar1=chc[par][:, 0:1],
                                                        scalar2=None, op0=ALU.subtract), reads=[BtB[bi], chB[par]], writes=[BtB[bi]])
                    op("act", lambda e: e.activation(out=Bb[par][:, u0:u1, :], in_=Btmp[bi][:, 0:u1 - u0, :], func=AF.Exp),
                       reads=[BtB[bi]], writes=[BB[par]])
                for c in range(nslots):
                    ksl = slice(c * 2048, (c + 1) * 2048)
                    if not moba:
                        dma("sp", KTb[:, ksl], T("KT")[hh, :, ksl], reads=[scrB["KT"]], writes=[KTB[c]])
                        dma("sp", VTb[:, 16 * c:16 * c + 16, :], T("VSd")[hh, :, 16 * c:16 * c + 16, :], reads=[scrB["VSd"]], writes=[VTB[c]])
                    else:
                        dma("sp", KTb[0:64, ksl], T("KT")[8 + hm // 2, (hm % 2) * 64:(hm % 2) * 64 + 64, ksl], reads=[scrB["KT"]], writes=[KTB[c]])
                        dma("sp", KTb[64:128, ksl], T("eind")[:, ksl], writes=[KTB[c]])
                        dma("sp", VTb[:, 16 * c:16 * c + 16, 0:65], T("VSm")[hm, :, 16 * c:16 * c + 16, :], reads=[scrB["VSm"]], writes=[VTB[c]])
                nm = 1 if moba else 2
                vw = 65 if moba else 129
                dv = vw - 1
                def acc_ap(a):
                    return Ops[a // 3][:, (a % 3) * 129:(a % 3) * 129 + vw]

                def osb_ap(a):
                    return osb[:, a // 3, (a % 3) * 129:(a % 3) * 129 + vw]

                def parts_of(r, st):
                    if not moba:
                        return [(st, slice(0, 64), 0), (st, slice(64, 128), 4)]
                    return [(2 * st, slice(0, 128), 0), (2 * st + 1, slice(0, 128), 0)]

                def emit_qk(r, st):
                    qsl = slice(r * 512, (r + 1) * 512)
                    si = scnt[0] % 2; scnt[0] += 1
                    Sp = Sps[si]
                    for p, (kt, rows, _) in enumerate(parts_of(r, st)):
                        c = kt // 16
                        op("pe", lambda e: e.matmul(Sp[:, p * 512:(p + 1) * 512], lhsT=KTb[rows, kt * 128:(kt + 1) * 128],
                                                    rhs=Qb[par][rows, qsl], start=True, stop=True),
                           reads=[KTB[c], QB[par]], writes=[SpB[si]], sig=(p == 1))
                    return (si,)

                def emit_exp_av(r, st, si):
                    nkt = 16 * r + 16
                    Sp = Sps[si]
                    pi = pcnt[0] % 3; pcnt[0] += 1
                    op("act", lambda e: e.activation(out=PT[pi][:, 0:1024], in_=Sp[:, 0:1024], func=AF.Exp, bias=chc[par][:]),
                       reads=[SpB[si], chB[par]], writes=[PTB[pi]])
                    parts = parts_of(r, st)
                    for p, (kt, rows, _) in enumerate(parts):
                        ktp = kt - (16 * r - 1)
                        if ktp >= 0:
                            u = 16 - ktp
                            op("dve", lambda e: e.tensor_tensor(out=PT[pi][:, p * 512:(p + 1) * 512], in0=PT[pi][:, p * 512:(p + 1) * 512],
                                                                in1=Bb[par][:, u, :], op=ALU.mult),
                               reads=[PTB[pi], BB[par]], writes=[PTB[pi]])
                    for p, (kt, rows, abase) in enumerate(parts):
                        c = kt // 16
                        for s in range(4):
                            a = abase + s
                            bank = a // 3
                            st_flag = (kt == 0) and (bank not in slot_banks)
                            slot_banks.add(bank)
                            op("pe", lambda e: e.matmul(acc_ap(a), lhsT=PT[pi][:, p * 512 + s * 128:p * 512 + (s + 1) * 128],
                                                        rhs=VTb[:, kt, 0:vw], start=st_flag, stop=(kt == nkt - 1),
                                                        skip_group_check=True),
                               reads=[PTB[pi], VTB[c]], writes=[OpB], sig=(p == 1 and s == 3))

                def emit_epi_dve(r):
                    if not moba:
                        for bk, wdt in ((0, 387), (1, 387), (2, 258)):
                            op("dve", lambda e: e.tensor_copy(out=osb[:, bk, 0:wdt], in_=Ops[bk][:, 0:wdt]), reads=[OpB], writes=[osbB])
                    else:
                        for a in range(4):
                            op("dve", lambda e: e.tensor_copy(out=osb_ap(a), in_=acc_ap(a)), reads=[OpB], writes=[osbB])
                    if not moba:
                        for s in range(4):
                            a0, a1 = osb_ap(s), osb_ap(4 + s)
                            op("dve", lambda e: e.reciprocal(out=rc[:, 0:1], in_=a0[:, 128:129]), reads=[osbB], writes=[rcB])
                            op("dve", lambda e: e.reciprocal(out=rc[:, 1:2], in_=a1[:, 128:129]), reads=[osbB], writes=[rcB])
                            op("dve", lambda e: e.tensor_tensor(out=rc[:, 2:3], in0=rc[:, 1:2], in1=nlam, op=ALU.mult), reads=[rcB, lwB], writes=[rcB])
                            op("dve", lambda e: e.tensor_scalar_mul(out=oa[:], in0=a0[:, 0:128], scalar1=rc[:, 0:1]), reads=[osbB, rcB], writes=[oaB])
                            op("dve", lambda e: e.scalar_tensor_tensor(out=od4[:, s, :], in0=a1[:, 0:128], scalar=rc[:, 2:3], in1=oa[:],
                                                                       op0=ALU.mult, op1=ALU.add), reads=[osbB, rcB, oaB], writes=[odB])
                            op("dve", lambda e: e.tensor_tensor(out=ojk[:], in0=od4[:, s, :], in1=od4[:, s, :], op=ALU.mult), reads=[odB], writes=[ojB])
                            op("dve", lambda e: e.reduce_sum(out=rq[:, s:s + 1], in_=ojk[:], axis=AX.X), reads=[ojB], writes=[rqB])
                        op("dve", lambda e: e.tensor_scalar(out=rq[:, 4:8], in0=rq[:, 0:4], scalar1=1.0 / 128, scalar2=EPS,
                                                            op0=ALU.mult, op1=ALU.add), reads=[rqB], writes=[rqB])
                        op("act", lambda e: e.activation(out=rq[:, 8:12], in_=rq[:, 4:8], func=AF.Sqrt), reads=[rqB], writes=[rqB])
                        op("dve", lambda e: e.reciprocal(out=rq[:, 12:16], in_=rq[:, 8:12]), reads=[rqB], writes=[rqB])
                        for s in range(4):
                            op("dve", lambda e: e.scalar_tensor_tensor(out=ob[:, s, :], in0=od4[:, s, :], scalar=rq[:, 12 + s:13 + s], in1=sg[:],
                                                                       op0=ALU.mult, op1=ALU.mult), reads=[odB, rqB, sgB], writes=[obB])
                    else:
                        for s in range(4):
                            a0 = osb_ap(s)
                            op("dve", lambda e: e.reciprocal(out=rc[:, 0:1], in_=a0[:, 64:65]), reads=[osbB], writes=[rcB])
                            op("dve", lambda e: e.tensor_scalar_mul(out=ob[:, s, 0:64], in0=a0[:, 0:64], scalar1=rc[:, 0:1]),
                               reads=[osbB, rcB], writes=[obB])

                def emit_epi_tail(r):
                    qsl = slice(r * 512, (r + 1) * 512)
                    ai = acnt[0] % 2; acnt[0] += 1
                    for s in range(4):
                        op("pe", lambda e: e.transpose(out=tpc[0:dv, s * 128:(s + 1) * 128], in_=ob[:, s, 0:dv], identity=ident[:]),
                           reads=[obB, identB], writes=[tpcB], sig=(s == 3))
                    op("dve", lambda e: e.tensor_copy(out=ast[ai][0:dv, :], in_=tpc[0:dv, 0:512]), reads=[tpcB], writes=[astB[ai]])
                    if not moba:
                        dma("pool", T("ATT")[hh, :, qsl], ast[ai][:], reads=[astB[ai]], writes=[scrB["ATT"]])
                    else:
                        dma("pool", T("ATT")[8 + hm // 2, (hm % 2) * 64:(hm % 2) * 64 + 64, qsl], ast[ai][0:64, :], reads=[astB[ai]],
                            writes=[scrB["ATT"]])

                spk = 2 if moba else 1
                flat = [(r, st) for r in range(nslots) for st in range((16 * r + 16) // spk)]
                slot_banks = set()
                nxt = emit_qk(*flat[0])
                deferred = None
                for idx, (r, st) in enumerate(flat):
                    cur = nxt
                    if idx + 1 < len(flat):
                        nxt = emit_qk(*flat[idx + 1])
                    if st == 0:
                        slot_banks.clear()
                    emit_exp_av(r, st, *cur)
                    if deferred is not None and deferred[1] == idx:
                        emit_epi_tail(deferred[0]); deferred = None
                    if st == (16 * r + 16) // spk - 1:
                        emit_epi_dve(r)
                        if idx + 1 < len(flat):
                            deferred = (r, idx + 4)
                        else:
                            emit_epi_tail(r)
            S.barrier()
        if stop_after in ("C", "C1"):
            return nc, S, used_inputs

        with ExitStack() as ph:
            lng = [sbt(ph, f"lng{i}", [128, D], F32) for i in range(6)]; lngB = S.buf("lng")
            for i, a in enumerate((T("ln1_g"), T("ln1_b"), T("ln2_g"), T("ln2_b"), T("ln3_g"), T("ln3_b"))):
                dma("sp", lng[i][:], a.partition_broadcast(128), writes=[lngB])
            ones = sbt(ph, "ones", [128, 128], BF16); onesB = S.buf("ones")
            op("dve", lambda e: e.memset(ones[:], 1.0), writes=[onesB])
            NW = 4
            wr = [sbt(ph, f"wr{i}", [128, 4096], BF16) for i in range(NW)]; wrB = S.bufs("wr", NW)
            wcnt = [0]

            def wload(src, kc, col0, ncol, row0=0):
                i = wcnt[0] % NW; wcnt[0] += 1
                v = wr[i][:, 0:kc * ncol].rearrange("p (c n) -> p c n", n=ncol)
                dma("sp", v, src[row0:row0 + kc * 128, col0:col0 + ncol].rearrange("(c p) n -> p c n", p=128),
                    reads=[scrB["wb"]], writes=[wrB[i]])
                return v, wrB[i]

            kxT = sbt(ph, "kxT", [128, 8, 256], BF16); kxB = S.buf("kxT")
            vx = sbt(ph, "vx", [128, 2, D], BF16); vxB = S.buf("vx")
            lst = ln_state(ph, "d_")
            ph2 = ExitStack()
            memf = sbt(ph2, "memf", [128, 2, D], F32); memB = S.buf("memf")
            memb = sbt(ph2, "memb", [128, 2, D], BF16); membB = S.buf("memb")
            memT = sbt(ph2, "memT", [128, 8, 256], BF16); memTB = S.buf("memT")
            tps = [pst(ph, f"tpd{i}", [128, 1024], BF16) for i in range(2)]; tpB = S.bufs("tpd", 2)
            mps = [pst(ph, f"mpd{i}", [128, 512], F32) for i in range(6)]; mpB = S.bufs("mpd", 6)
            tcnt = [0]; mcnt = [0]

            def nextps():
                i = mcnt[0] % 2; mcnt[0] += 1
                return mps[i], mpB[i]

            dma("sp", memf[:], T("mem_b").rearrange("(n p) d -> p n d", p=128), writes=[memB])
            op("dve", lambda e: e.tensor_copy(out=memb[:], in_=memf[:]), reads=[memB], writes=[membB])
            transpose_to(memb, membB, memT, memTB, tps, tpB, tcnt, nsub=2)
            for half in range(2):
                wv_, wB_ = wload(T("wb_k"), 8, half * 512, 512)
                for g4 in range(4):
                    ps, psB = nextps()
                    for c in range(8):
                        op("pe", lambda e: e.matmul(ps[:, 0:256], lhsT=wv_[:, c, g4 * 128:(g4 + 1) * 128], rhs=memT[:, c, :],
                                                    start=(c == 0), stop=(c == 7)), reads=[wB_, memTB], writes=[psB], sig=(c == 7))
                    op(ev_eng(), lambda e: cp(e, kxT[:, half * 4 + g4, :], ps[:, 0:256]), reads=[psB], writes=[kxB])
            for half in range(2):
                wv_, wB_ = wload(T("wb_v"), 8, half * 512, 512)
                for mt in range(2):
                    ps, psB = nextps()
                    for c in range(8):
                        op("pe", lambda e: e.matmul(ps[:], lhsT=memT[:, c, mt * 128:(mt + 1) * 128], rhs=wv_[:, c, :],
                                                    start=(c == 0), stop=(c == 7)), reads=[wB_, memTB], writes=[psB], sig=(c == 7))
                    op(ev_eng(), lambda e: cp(e, vx[:, mt, half * 512:(half + 1) * 512], ps[:]), reads=[psB], writes=[vxB])

            S.barrier()
            ph2.close()
            attT = sbt(ph, "attT", [128, 12, 512], BF16); attB = S.buf("attT")
            gT = sbt(ph, "gT", [128, 16, 512], BF16); gTB = S.buf("gT")
            ydT = sbt(ph, "ydT", [128, 8, 512], BF16); ydB = S.buf("ydT")
            mixT = sbt(ph, "mixT", [128, 8, 512], BF16); mixB = S.buf("mixT")
            tmpb = sbt(ph, "tmpb", [128, 512], BF16); tmpB = S.buf("tmpb")
            ha = sbt(ph, "ha", [128, 4, D], F32); haB = S.buf("ha")
            hbf = sbt(ph, "hbf", [128, 4, D], F32); hbfB = S.buf("hbf")
            hbb = sbt(ph, "hbb2", [128, 4, D], BF16); hbbB = S.buf("hbb2")
            hT = sbt(ph, "hTd", [128, 8, 512], BF16); hTB = S.buf("hTd")
            qxT, qxB = ydT, ydB
            pTs = [sbt(ph, f"pTx{i}", [128, 512], BF16) for i in range(2)]; pTB = S.bufs("pTx", 2)
            rs = sbt(ph, "rs", [128, 512], F32); rsB = S.buf("rs")
            oT, oTB = mixT, mixB
            rl = sbt(ph, "rl", [128, 512], BF16); rlB = S.buf("rl")
            actT = sbt(ph, "actT", [128, 32, 512], BF16); actB = S.buf("actT")
            def fm_proj(wsrc, kc, ncolgroups, src, srcB, evac, row0=0, gpw=None):
                gpw = gpw or (4096 // (kc * 128))
                for g0 in range(0, ncolgroups, gpw):
                    n = min(gpw, ncolgroups - g0)
                    wv_, wB_ = wload(wsrc, kc, g0 * 128, n * 128, row0=row0)
                    for gi in range(n):
                        ps, psB = nextps()
                        for c in range(kc):
                            op("pe", lambda e: e.matmul(ps[:], lhsT=wv_[:, c, gi * 128:(gi + 1) * 128], rhs=src[:, c, :],
                                                        start=(c == 0), stop=(c == kc - 1)), reads=[wB_, srcB], writes=[psB], sig=(c == kc - 1))
                        evac(g0 + gi, ps, psB)

            def tm_proj(wsrc, nk, src, srcB, evac):
                for half in range(2):
                    for kg in range(nk // 8):
                        wv_, wB_ = wload(wsrc, 8, half * 512, 512, row0=kg * 1024)
                        for sub in range(4):
                            for c in range(8):
                                k = kg * 8 + c
                                op("pe", lambda e: e.matmul(mps[2 + sub][:], lhsT=src[:, k, sub * 128:(sub + 1) * 128], rhs=wv_[:, c, :],
                                                            start=(k == 0), stop=(k == nk - 1)),
                                   reads=[wB_, srcB], writes=[mpB[2 + sub]], sig=(c == 7))
                    for sub in range(4):
                        evac(half, sub, mps[2 + sub], mpB[2 + sub])

            def resid_evac(hin, hinB, hout, houtB):
                def f(half, sub, ps, psB):
                    cs = slice(half * 512, (half + 1) * 512)
                    op("dve", lambda e: e.scalar_tensor_tensor(out=hout[:, sub, cs], in0=hin[:, sub, cs], scalar=ALPHA, in1=ps[:],
                                                               op0=ALU.mult, op1=ALU.add), reads=[hinB, psB], writes=[houtB])
                return f

            ntile = 8 if stop_after != "D1" else 1
            for t in range(ntile):
                tsl = slice(t * 512, (t + 1) * 512)
                dma("sp", attT[:], T("ATT")[:, :, tsl].rearrange("g p t -> p g t"), reads=[scrB["ATT"]], writes=[attB])
                dma("sp", gT[:], T("Gs")[:, :, tsl].rearrange("g p t -> p g t"), reads=[scrB["Gs"]], writes=[gTB])
                dma("sp", ha[:], T("Hs")[tsl, :].rearrange("(n p) d -> p n d", p=128), reads=[scrB["Hs"]], writes=[haB])
                fm_proj(T("wb_brd"), 8, 8, attT, attB,
                        lambda g, ps, psB: op("dve", lambda e: e.tensor_tensor(out=ydT[:, g, :], in0=ps[:], in1=gT[:, g, :], op=ALU.mult),
                                              reads=[psB, gTB], writes=[ydB]))

                def ev_m(g, ps, psB):
                    op("dve", lambda e: e.tensor_tensor(out=tmpb[:], in0=ps[:], in1=gT[:, 8 + g, :], op=ALU.mult),
                       reads=[psB, gTB], writes=[tmpB])
                    op("pool", lambda e: e.tensor_tensor(out=mixT[:, g, :], in0=tmpb[:], in1=ydT[:, g, :], op=ALU.add),
                       reads=[tmpB, ydB], writes=[mixB])
                fm_proj(T("wb_brm"), 4, 8, attT[:, 8:12, :], attB, ev_m)
                tm_proj(T("wb_out"), 8, mixT, mixB, resid_evac(ha, haB, hbf, hbfB))
                layer_norm(lst, hbf, hbfB, lng[0], lng[1], lngB, hbf, hbfB, hbb, hbbB)
                transpose_to(hbb, hbbB, hT, hTB, tps, tpB, tcnt)
                fm_proj(T("wb_q"), 8, 8, hT, hTB, lambda g, ps, psB: op(ev_eng(), lambda e: cp(e, qxT[:, g, :], ps[:]), reads=[psB], writes=[qxB]))
                for hd in range(4):
                    for mt in range(2):
                        ps, psB = nextps()
                        for c in range(2):
                            op("pe", lambda e: e.matmul(ps[:], lhsT=kxT[:, 2 * hd + c, mt * 128:(mt + 1) * 128], rhs=qxT[:, 2 * hd + c, :],
                                                        start=(c == 0), stop=(c == 1)), reads=[kxB, qxB], writes=[psB], sig=(c == 1))
                        op("act", lambda e: e.activation(out=pTs[mt][:], in_=ps[:], func=AF.Exp, scale=1.0 / 16), reads=[psB], writes=[pTB[mt]])
                    ps, psB = nextps()
                    for mt in range(2):
                        op("pe", lambda e: e.matmul(ps[:], lhsT=ones[:], rhs=pTs[mt][:], start=(mt == 0), stop=(mt == 1)),
                           reads=[onesB, pTB[mt]], writes=[psB], sig=(mt == 1))
                    op("dve", lambda e: e.reciprocal(out=rs[:], in_=ps[:]), reads=[psB], writes=[rsB])
                    for dc in range(2):
                        ps, psB = nextps()
                        for mt in range(2):
                            op("pe", lambda e: e.matmul(ps[:], lhsT=vx[:, mt, hd * 256 + dc * 128:hd * 256 + (dc + 1) * 128], rhs=pTs[mt][:],
                                                        start=(mt == 0), stop=(mt == 1)), reads=[vxB, pTB[mt]], writes=[psB], sig=(mt == 1))
                        op("dve", lambda e: e.tensor_tensor(out=oT[:, 2 * hd + dc, :], in0=ps[:], in1=rs[:], op=ALU.mult),
                           reads=[psB, rsB], writes=[oTB])
                tm_proj(T("wb_o"), 8, oT, oTB, resid_evac(hbf, hbfB, ha, haB))
                layer_norm(lst, ha, haB, lng[2], lng[3], lngB, ha, haB, hbb, hbbB)
                transpose_to(hbb, hbbB, hT, hTB, tps, tpB, tcnt)

                def ev_f(g, ps, psB):
                    op("act", lambda e: e.activation(out=rl[:], in_=ps[:], func=AF.Relu), reads=[psB], writes=[rlB])
                    op(("dve", "pool")[g % 2], lambda e: e.tensor_tensor(out=actT[:, g, :], in0=rl[:], in1=rl[:], op=ALU.mult),
                       reads=[rlB], writes=[actB])
                fm_proj(T("wb_f1"), 8, 32, hT, hTB, ev_f)
                tm_proj(T("wb_f2"), 32, actT, actB, resid_evac(ha, haB, hbf, hbfB))
                layer_norm(lst, hbf, hbfB, lng[4], lng[5], lngB, hbf, hbfB, None, None)
                dma("sp", T("out_d")[tsl, :].rearrange("(n p) d -> p n d", p=128), hbf[:], reads=[hbfB], writes=[scrB["out"]])
            S.barrier()
        return nc, S, used_inputs


def _rel_bucket(n):
    n = np.asarray(n)
    nf = np.maximum(n, 1).astype(np.float32)
    large = 16 + (np.log(nf / np.float32(16)) / np.float32(math.log(128 / 16)) * np.float32(16)).astype(np.int32)
    large = np.minimum(large, 31)
    return np.where(n < 16, n, large)


def make_inputs(i, inputs):
    b, j = i // 4, i % 4
    f = lambda a: np.ascontiguousarray(np.asarray(a, dtype=np.float32))
    x = np.asarray(inputs["x"], dtype=np.float32)
    qts = [4 * r + j for r in range(8)]
    m = {}
    m["x_b"] = f(x[b])
    m["x_own"] = f(np.concatenate([x[b, 512 * q:512 * q + 512] for q in qts], axis=0))
    m["mem_b"] = f(inputs["mem"][b])
    for k in ("ln_in_g", "ln_in_b", "rel_table"):
        m[k] = f(inputs[k])
    for k in ("w_in", "b_gate", "lam_q1", "lam_k1", "lam_q2", "lam_k2", "diff_sub_g", "w_br_diff", "w_br_moba", "w_out",
              "ln1_g", "ln1_b", "wq_x", "wk_x", "wv_x", "wo_x", "ln2_g", "ln2_b", "w_ff1", "w_ff2", "ln3_g", "ln3_b"):
        m[k] = f(np.asarray(inputs[k])[0])
    rel = np.asarray(inputs["rel_table"], dtype=np.float32)
    m0 = 512 * j - 1920
    mm = np.arange(2560)[None, :] + m0
    kk = np.arange(128)[:, None]
    dist = mm - kk
    bk = _rel_bucket(np.maximum(dist, 0))
    tab = rel[bk]
    tab = np.where((dist >= 0)[:, :, None], tab, np.float32(NEG))
    m["t2"] = f(np.transpose(tab, (2, 0, 1)))
    blk = np.arange(NT) // 256
    m["eind"] = (blk[None, :] == np.arange(64)[:, None]).astype(ml_dtypes.bfloat16)
    pb = np.zeros((8, 4, 128, 64), np.float32); pv = np.zeros_like(pb); ot = np.full_like(pb, NEG)
    for r in range(8):
        for s in range(4):
            cur = 2 * (4 * r + j) + s // 2
            pb[r, s, :, cur:] = -1e30
            pv[r, s, :, :cur] = 1.0
            ot[r, s, :, cur] = 0.0
    m["pastbias"], m["pastvalid"], m["ownterm"] = pb, pv, ot
    m["ident"] = np.eye(128).astype(ml_dtypes.bfloat16)
    return m


def kernel(**inputs):
    nc, S, used = build_program()
    in_maps = [{k: v for k, v in make_inputs(i, inputs).items() if k in used} for i in range(8)]
    res = run_bass_kernel_spmd(nc, in_maps, core_ids=list(range(8)))
    out = np.zeros((2, NT, D), np.float32)
    for i in range(8):
        b, j = i // 4, i % 4
        o = res.results[i]["out"]
        for r in range(8):
            q = 4 * r + j
            out[b, 512 * q:512 * q + 512] = o[r * 512:(r + 1) * 512]
    return out
```

```python
import math
from contextlib import ExitStack
import numpy as np
import ml_dtypes
import concourse.bass as bass
import concourse.mybir as mybir
from concourse.bass_utils import run_bass_kernel_spmd

F32 = mybir.dt.float32
BF16 = mybir.dt.bfloat16
AF = mybir.ActivationFunctionType
ALU = mybir.AluOpType
AX = mybir.AxisListType

D = 1024
NT = 16384
NO = 4096
NEG = -30000.0
ALPHA = 2.0 ** 0.25
LAM_INIT = 0.2
EPS = 1e-5
C_DQ, C_DK, C_DV, C_MQ, C_MK, C_MV, C_G = 0, 1024, 2048, 3072, 3584, 4096, 4608


class Buf:
    __slots__ = ("name", "w", "r", "dsem", "dcnt")

    def __init__(self, name):
        self.name = name
        self.w = None
        self.r = {}
        self.dsem = None
        self.dcnt = 0


class Sched:
    def __init__(self, nc, es):
        self.nc = nc
        self.es = es
        self.E = {}
        for name, eng in (("pe", nc.tensor), ("act", nc.scalar), ("dve", nc.vector),
                          ("pool", nc.gpsimd), ("sp", nc.sync)):
            sem = es.enter_context(nc.semaphore("s_" + name))
            self.E[name] = dict(eng=eng, sem=sem, cnt=0, seen={})
        self.dbufs = []
        self.nwaits = 0

    def buf(self, name):
        return Buf(name)

    def bufs(self, name, n):
        return [Buf(f"{name}{i}") for i in range(n)]

    def _wait(self, en, ev):
        e = self.E[en]
        if e["seen"].get(ev[0], 0) >= ev[2]:
            return
        e["eng"].wait_ge(ev[1], ev[2])
        e["seen"][ev[0]] = ev[2]
        self.nwaits += 1

    def _deps(self, en, reads, writes):
        for b in reads:
            if b.w is not None:
                self._wait(en, b.w)
        for b in writes:
            if b.w is not None:
                self._wait(en, b.w)
            for ev in b.r.values():
                self._wait(en, ev)

    def op(self, en, fn, reads=(), writes=(), sig=True):
        e = self.E[en]
        if en == "pe":
            e["seen"]["pe"] = 1 << 60
        self._deps(en, reads, writes)
        ins = fn(e["eng"])
        if sig:
            e["cnt"] += 1
            ins.then_inc(e["sem"], 1)
            ev = (en, e["sem"], e["cnt"])
        else:
            ev = (en, e["sem"], e["cnt"] + 1)
        for b in reads:
            b.r[en] = ev
        for b in writes:
            b.w = ev
            b.r = {}
        return ins

    def dma(self, qn, out, in_, reads=(), writes=(), **kw):
        q = self.E[qn]
        self._deps(qn, reads, writes)
        b = writes[0]
        if b.dsem is None:
            b.dsem = self.es.enter_context(self.nc.semaphore("d_" + b.name))
            self.dbufs.append(b)
        ins = q["eng"].dma_start(out=out, in_=in_, **kw)
        b.dcnt += 16
        ins.then_inc(b.dsem, 16)
        key = ("d", id(b))
        ev = (key, b.dsem, b.dcnt)
        for r in reads:
            r.r[key] = ev
        for w in writes:
            w.w = ev
            w.r = {}
        return ins

    def barrier(self):
        for en, e in self.E.items():
            for on, o in self.E.items():
                if on != en and o["cnt"] > 0:
                    self._wait(en, (on, o["sem"], o["cnt"]))
            for b in self.dbufs:
                self._wait(en, (("d", id(b)), b.dsem, b.dcnt))


def build_program(stop_after=None, dbg=False):
    nc = bass.Bass("TRN2", target_bir_lowering=False)

    def din(name, shape, dt=F32):
        return nc.dram_tensor(name, list(shape), dt, kind="ExternalInput").ap()

    def dscr(name, shape, dt, kind="Internal"):
        return nc.dram_tensor(name, list(shape), dt, kind=kind).ap()

    _decl = {}
    _decl["x_b"] = ("x_b", lambda: [NT, D], F32, "ExternalInput")
    _decl["x_own"] = ("x_own", lambda: [NO, D], F32, "ExternalInput")
    _decl["mem_b"] = ("mem_b", lambda: [256, D], F32, "ExternalInput")
    _decl["ln_in_g"] = ("ln_in_g", lambda: [D], F32, "ExternalInput")
    _decl["ln_in_b"] = ("ln_in_b", lambda: [D], F32, "ExternalInput")
    _decl["rel_table"] = ("rel_table", lambda: [32, 16], F32, "ExternalInput")
    _decl["w_in"] = ("w_in", lambda: [D, 6656], F32, "ExternalInput")
    _decl["b_gate"] = ("b_gate", lambda: [2048], F32, "ExternalInput")
    _decl["lam_q1"] = ("lam_q1", lambda: [64], F32, "ExternalInput")
    _decl["lam_k1"] = ("lam_k1", lambda: [64], F32, "ExternalInput")
    _decl["lam_q2"] = ("lam_q2", lambda: [64], F32, "ExternalInput")
    _decl["lam_k2"] = ("lam_k2", lambda: [64], F32, "ExternalInput")
    _decl["diff_sub_g"] = ("diff_sub_g", lambda: [128], F32, "ExternalInput")
    _decl["w_br_diff"] = ("w_br_diff", lambda: [1024, D], F32, "ExternalInput")
    _decl["w_br_moba"] = ("w_br_moba", lambda: [512, D], F32, "ExternalInput")
    _decl["w_out"] = ("w_out", lambda: [D, D], F32, "ExternalInput")
    _decl["ln1_g"] = ("ln1_g", lambda: [D], F32, "ExternalInput")
    _decl["ln1_b"] = ("ln1_b", lambda: [D], F32, "ExternalInput")
    _decl["wq_x"] = ("wq_x", lambda: [D, D], F32, "ExternalInput")
    _decl["wk_x"] = ("wk_x", lambda: [D, D], F32, "ExternalInput")
    _decl["wv_x"] = ("wv_x", lambda: [D, D], F32, "ExternalInput")
    _decl["wo_x"] = ("wo_x", lambda: [D, D], F32, "ExternalInput")
    _decl["ln2_g"] = ("ln2_g", lambda: [D], F32, "ExternalInput")
    _decl["ln2_b"] = ("ln2_b", lambda: [D], F32, "ExternalInput")
    _decl["w_ff1"] = ("w_ff1", lambda: [D, 4096], F32, "ExternalInput")
    _decl["w_ff2"] = ("w_ff2", lambda: [4096, D], F32, "ExternalInput")
    _decl["ln3_g"] = ("ln3_g", lambda: [D], F32, "ExternalInput")
    _decl["ln3_b"] = ("ln3_b", lambda: [D], F32, "ExternalInput")
    _decl["t2"] = ("t2", lambda: [16, 128, 2560], F32, "ExternalInput")
    _decl["eind"] = ("eind", lambda: [64, NT], BF16, "ExternalInput")
    _decl["pastbias"] = ("pastbias", lambda: [8, 4, 128, 64], F32, "ExternalInput")
    _decl["pastvalid"] = ("pastvalid", lambda: [8, 4, 128, 64], F32, "ExternalInput")
    _decl["ownterm"] = ("ownterm", lambda: [8, 4, 128, 64], F32, "ExternalInput")
    _decl["ident_in"] = ("ident", lambda: [128, 128], BF16, "ExternalInput")
    _decl["out_d"] = ("out", lambda: [NO, D], F32, "ExternalOutput")
    _decl["KT"] = ("KT", lambda: [12, 128, NT], BF16, "dk")
    _decl["VSd"] = ("VSd", lambda: [8, 128, 128, 129], BF16, "dk")
    _decl["VSm"] = ("VSm", lambda: [8, 128, 128, 65], BF16, "dk")
    _decl["QT"] = ("QT", lambda: [8, 128, NO], BF16, "dk")
    _decl["QA"] = ("QA", lambda: [8, 128, NO], BF16, "dk")
    _decl["Gs"] = ("Gs", lambda: [16, 128, NO], BF16, "dk")
    _decl["Hs"] = ("Hs", lambda: [NO, D], F32, "dk")
    _decl["ATT"] = ("ATT", lambda: [12, 128, NO], BF16, "dk")
    _decl["wb_brd"] = ("wb_brd", lambda: [1024, D], BF16, "Internal")
    _decl["wb_brm"] = ("wb_brm", lambda: [512, D], BF16, "Internal")
    _decl["wb_out"] = ("wb_out", lambda: [D, D], BF16, "Internal")
    _decl["wb_q"] = ("wb_q", lambda: [D, D], BF16, "Internal")
    _decl["wb_k"] = ("wb_k", lambda: [D, D], BF16, "Internal")
    _decl["wb_v"] = ("wb_v", lambda: [D, D], BF16, "Internal")
    _decl["wb_o"] = ("wb_o", lambda: [D, D], BF16, "Internal")
    _decl["wb_f1"] = ("wb_f1", lambda: [D, 4096], BF16, "Internal")
    _decl["wb_f2"] = ("wb_f2", lambda: [4096, D], BF16, "Internal")
    dk = "ExternalOutput" if dbg else "Internal"
    _made = {}
    used_inputs = []

    def T(var):
        if var not in _made:
            name, shp, dt, kind = _decl[var]
            if kind == "dk":
                kind = dk
            _made[var] = nc.dram_tensor(name, list(shp()), dt, kind=kind).ap()
            if kind == "ExternalInput":
                used_inputs.append(name)
        return _made[var]

    with ExitStack() as es:
        S = Sched(nc, es)
        op, dma = S.op, S.dma
        evq = [0]

        def ev_eng():
            evq[0] += 1
            return "act" if evq[0] % 2 else "dve"

        def sbt(st, name, shape, dt):
            return st.enter_context(nc.sbuf_tensor(name, list(shape), dt))

        def pst(st, name, shape, dt):
            return st.enter_context(nc.psum_tensor(name, list(shape), dt))

        ident = sbt(es, "ident_sb", [128, 128], BF16); identB = S.buf("ident")
        dma("sp", ident[:], T("ident_in"), writes=[identB])
        kmT = sbt(es, "kmT", [128, 4, 64], F32); kmB = S.buf("kmT")
        epsc = sbt(es, "epsc", [128, 1], F32); epsB = S.buf("epsc")
        op("dve", lambda e: e.memset(epsc[:], EPS), writes=[epsB])
        scrB = {n: S.buf(n) for n in ("KT", "VSd", "VSm", "QT", "QA", "Gs", "Hs", "ATT", "out", "wb")}
        for dst, src in () if stop_after == "A1" else ((T("wb_brd"), T("w_br_diff")), (T("wb_brm"), T("w_br_moba")), (T("wb_out"), T("w_out")), (T("wb_q"), T("wq_x")), (T("wb_k"), T("wk_x")),
                         (T("wb_v"), T("wv_x")), (T("wb_o"), T("wo_x"))):
            dma("pool", dst, src, writes=[scrB["wb"]])
        for c in range(0 if stop_after == "A1" else 4):
            dma("pool", T("wb_f1")[c * 256:(c + 1) * 256, :], T("w_ff1")[c * 256:(c + 1) * 256, :], writes=[scrB["wb"]])
            dma("pool", T("wb_f2")[c * 1024:(c + 1) * 1024, :], T("w_ff2")[c * 1024:(c + 1) * 1024, :], writes=[scrB["wb"]])

        if stop_after == "A0":
            S.barrier()
            return nc, S, used_inputs
        def layer_norm(st, xt, xtB, gt, bt, gbB, out32, out32B, outbf, outbfB, nsub=4):
            s1, s2, mu, var, rstd, nmr, junk = st
            sB = s1[1]
            op("dve", lambda e: e.reduce_sum(out=s1[0][:, 0:nsub], in_=xt[:, 0:nsub, :], axis=AX.X), reads=[xtB], writes=[sB])
            for n in range(nsub):
                op("act", lambda e: e.activation(out=junk[0][:], in_=xt[:, n, :], func=AF.Square,
                                                 accum_out=s2[0][:, n:n + 1]), reads=[xtB], writes=[junk[1], s2[1]])
            op("dve", lambda e: e.tensor_scalar_mul(out=mu[0][:, 0:nsub], in0=s1[0][:, 0:nsub], scalar1=1.0 / D), reads=[sB], writes=[mu[1]])
            op("dve", lambda e: e.tensor_tensor(out=var[0][:, 0:nsub], in0=mu[0][:, 0:nsub], in1=mu[0][:, 0:nsub], op=ALU.mult),
               reads=[mu[1]], writes=[var[1]])
            op("dve", lambda e: e.scalar_tensor_tensor(out=var[0][:, 0:nsub], in0=s2[0][:, 0:nsub], scalar=1.0 / D, in1=var[0][:, 0:nsub],
                                                       op0=ALU.mult, op1=ALU.subtract), reads=[s2[1], var[1]], writes=[var[1]])
            op("dve", lambda e: e.tensor_scalar_add(out=var[0][:, 0:nsub], in0=var[0][:, 0:nsub], scalar1=EPS), reads=[var[1]], writes=[var[1]])
            op("act", lambda e: e.activation(out=rstd[0][:, 0:nsub], in_=var[0][:, 0:nsub], func=AF.Sqrt), reads=[var[1]], writes=[rstd[1]])
            op("dve", lambda e: e.reciprocal(out=rstd[0][:, 0:nsub], in_=rstd[0][:, 0:nsub]), reads=[rstd[1]], writes=[rstd[1]])
            op("dve", lambda e: e.scalar_tensor_tensor(out=nmr[0][:, 0:nsub], in0=mu[0][:, 0:nsub], scalar=-1.0, in1=rstd[0][:, 0:nsub],
                                                       op0=ALU.mult, op1=ALU.mult), reads=[mu[1], rstd[1]], writes=[nmr[1]])
            for n in range(nsub):
                op("act", lambda e: e.activation(out=xt[:, n, :], in_=xt[:, n, :], func=AF.Identity,
                                                 bias=nmr[0][:, n:n + 1], scale=rstd[0][:, n:n + 1]),
                   reads=[xtB, nmr[1], rstd[1]], writes=[xtB])
            for n in range(nsub):
                op("dve", lambda e: e.tensor_tensor(out=xt[:, n, :], in0=xt[:, n, :], in1=gt[:], op=ALU.mult),
                   reads=[xtB, gbB], writes=[xtB])
            for n in range(nsub):
                op("pool", lambda e: e.tensor_tensor(out=out32[:, n, :], in0=xt[:, n, :], in1=bt[:], op=ALU.add),
                   reads=[xtB, gbB], writes=[out32B])
            if outbf is not None:
                for n in range(nsub):
                    op(("dve", "pool")[n % 2], lambda e: e.tensor_copy(out=outbf[:, n, :], in_=out32[:, n, :]),
                       reads=[out32B], writes=[outbfB])

        def layer_norm_ap(st, xt, xtB, gt, bt, gbB, outbf, outbfB, epsc, epsB, nsub=4, keep32=False):
            s1, s2, mu, var, rstd, nmr, junk = st
            for n in range(nsub):
                op("act", lambda e: e.activation(out=junk[0][:], in_=xt[:, n, :], func=AF.Identity,
                                                 accum_out=s1[0][:, n:n + 1]), reads=[xtB], writes=[junk[1], s1[1]])
                op("act", lambda e: e.activation(out=junk[0][:], in_=xt[:, n, :], func=AF.Square,
                                                 accum_out=s2[0][:, n:n + 1]), reads=[xtB], writes=[junk[1], s2[1]])
            op("pool", lambda e: e.tensor_scalar_mul(out=mu[0][:, 0:nsub], in0=s1[0][:, 0:nsub], scalar1=1.0 / D), reads=[s1[1]], writes=[mu[1]])
            op("pool", lambda e: e.tensor_tensor(out=var[0][:, 0:nsub], in0=mu[0][:, 0:nsub], in1=mu[0][:, 0:nsub], op=ALU.mult),
               reads=[mu[1]], writes=[var[1]])
            op("pool", lambda e: e.tensor_scalar_mul(out=s2[0][:, 0:nsub], in0=s2[0][:, 0:nsub], scalar1=1.0 / D), reads=[s2[1]], writes=[s2[1]])
            op("pool", lambda e: e.tensor_tensor(out=var[0][:, 0:nsub], in0=s2[0][:, 0:nsub], in1=var[0][:, 0:nsub], op=ALU.subtract),
               reads=[s2[1], var[1]], writes=[var[1]])
            op("act", lambda e: e.activation(out=rstd[0][:, 0:nsub], in_=var[0][:, 0:nsub], func=AF.Ln, bias=epsc[:]), reads=[var[1], epsB], writes=[rstd[1]])
            op("act", lambda e: e.activation(out=rstd[0][:, 0:nsub], in_=rstd[0][:, 0:nsub], func=AF.Exp, scale=-0.5), reads=[rstd[1]], writes=[rstd[1]])
            op("pool", lambda e: e.tensor_tensor(out=nmr[0][:, 0:nsub], in0=mu[0][:, 0:nsub], in1=rstd[0][:, 0:nsub], op=ALU.mult),
               reads=[mu[1], rstd[1]], writes=[nmr[1]])
            op("pool", lambda e: e.tensor_scalar_mul(out=nmr[0][:, 0:nsub], in0=nmr[0][:, 0:nsub], scalar1=-1.0), reads=[nmr[1]], writes=[nmr[1]])
            for n in range(nsub):
                op("act", lambda e: e.activation(out=xt[:, n, :], in_=xt[:, n, :], func=AF.Identity,
                                                 bias=nmr[0][:, n:n + 1], scale=rstd[0][:, n:n + 1]),
                   reads=[xtB, nmr[1], rstd[1]], writes=[xtB])
                op("pool", lambda e: e.tensor_tensor(out=xt[:, n, :], in0=xt[:, n, :], in1=gt[:], op=ALU.mult),
                   reads=[xtB, gbB], writes=[xtB])
                if not keep32:
                    op("pool", lambda e: e.tensor_tensor(out=outbf[:, n, :], in0=xt[:, n, :], in1=bt[:], op=ALU.add),
                       reads=[xtB, gbB], writes=[outbfB])
                else:
                    op("pool", lambda e: e.tensor_tensor(out=xt[:, n, :], in0=xt[:, n, :], in1=bt[:], op=ALU.add),
                       reads=[xtB, gbB], writes=[xtB])
                    op("act", lambda e: e.copy(out=outbf[:, n, :], in_=xt[:, n, :]), reads=[xtB], writes=[outbfB])

        def ln_state(st_, pfx):
            res = []
            for nm in ("s1", "s2", "mu", "var", "rstd", "nmr"):
                res.append((sbt(st_, pfx + nm, [128, 4], F32), S.buf(pfx + nm)))
            res.append((sbt(st_, pfx + "junk", [128, 1024], F32), S.buf(pfx + "junk")))
            return res

        def transpose_to(hb, hbB, hT, hTB, tps, tpB, tcnt, nsub=4, eng=None):
            for c in range(8):
                i = tcnt[0] % len(tps); tcnt[0] += 1
                for n in range(nsub):
                    op("pe", lambda e: e.transpose(out=tps[i][:, n * 128:(n + 1) * 128], in_=hb[:, n, c * 128:(c + 1) * 128],
                                                   identity=ident[:]),
                       reads=[hbB, identB], writes=[tpB[i]], sig=(n == nsub - 1))
                op(eng or ev_eng(), lambda e: cp(e, hT[:, c, 0:nsub * 128], tps[i][:, 0:nsub * 128]),
                   reads=[tpB[i]], writes=[hTB])

        def cp(e, out, in_):
            return e.copy(out=out, in_=in_) if e is nc.scalar else e.tensor_copy(out=out, in_=in_)

        with ExitStack() as ph:
            gt = sbt(ph, "gt", [128, D], F32); bt = sbt(ph, "bt", [128, D], F32); gbB = S.buf("gb")
            dma("sp", gt[:], T("ln_in_g").partition_broadcast(128), writes=[gbB])
            dma("sp", bt[:], T("ln_in_b").partition_broadcast(128), writes=[gbB])
            Wk = sbt(ph, "Wk", [128, 8, 1536], BF16); WkB = S.buf("Wk")
            Wv = sbt(ph, "Wv", [128, 8, 1536], BF16); WvB = S.buf("Wv")
            for c in range(8):
                rows = slice(c * 128, (c + 1) * 128)
                dma("pool", Wk[:, c, 0:1024], T("w_in")[rows, C_DK:C_DK + 1024], writes=[WkB])
                dma("pool", Wk[:, c, 1024:1536], T("w_in")[rows, C_MK:C_MK + 512], writes=[WkB])
                dma("pool", Wv[:, c, 0:1024], T("w_in")[rows, C_DV:C_DV + 1024], writes=[WvB])
                dma("pool", Wv[:, c, 1024:1536], T("w_in")[rows, C_MV:C_MV + 512], writes=[WvB])
            xts = [sbt(ph, f"xt{i}", [128, 4, D], F32) for i in range(2)]; xtBs = S.bufs("xt", 2)
            hbs = [sbt(ph, f"hb{i}", [128, 4, D], BF16) for i in range(3)]; hbBs = S.bufs("hb", 3)
            hTs = [sbt(ph, f"hT{i}", [128, 8, 512], BF16) for i in range(2)]; hTBs = S.bufs("hT", 2)
            kst = sbt(ph, "kst", [128, 12, 512], BF16); kstB = S.buf("kst")
            vstd = sbt(ph, "vstd", [128, 8, 4, 129], BF16); vstdB = S.buf("vstd")
            vstm = sbt(ph, "vstm", [128, 8, 4, 65], BF16); vstmB = S.buf("vstm")
            lst = ln_state(ph, "a_")
            tps = [pst(ph, f"tpa{i}", [128, 1024], BF16) for i in range(2)]; tpB = S.bufs("tpa", 2)
            mps = [pst(ph, f"mpa{i}", [128, 512], F32) for i in range(4)]; mpB = S.bufs("mpa", 4)
            op("dve", lambda e: e.memset(vstd[:, :, :, 128:129], 1.0), writes=[vstdB])
            op("dve", lambda e: e.memset(vstm[:, :, :, 64:65], 1.0), writes=[vstmB])
            tcnt = [0]; mcnt = [0]
            nta = 32 if stop_after != "A1" else 1
            def ln_a(Tt):
                xt, xtB = xts[Tt % 2], xtBs[Tt % 2]
                dma("sp", xt[:], T("x_b")[Tt * 512:(Tt + 1) * 512, :].rearrange("(n p) d -> p n d", p=128), writes=[xtB])
                layer_norm_ap(lst, xt, xtB, gt, bt, gbB, hbs[Tt % 3], hbBs[Tt % 3], epsc, epsB)

            def tr_a(Tt):
                transpose_to(hbs[Tt % 3], hbBs[Tt % 3], hTs[Tt % 2], hTBs[Tt % 2], tps, tpB, tcnt, eng="dve")

            ln_a(0)
            if nta > 1:
                ln_a(1)
            tr_a(0)
            for Tt in range(nta):
                if Tt + 2 < nta:
                    ln_a(Tt + 2)
                if Tt + 1 < nta:
                    tr_a(Tt + 1)
                hT, hTB = hTs[Tt % 2], hTBs[Tt % 2]
                for g in range(12):
                    i = mcnt[0] % 4; mcnt[0] += 1
                    for c in range(8):
                        op("pe", lambda e: e.matmul(mps[i][:], lhsT=Wk[:, c, g * 128:(g + 1) * 128], rhs=hT[:, c, :],
                                                    start=(c == 0), stop=(c == 7)),
                           reads=[WkB, hTB], writes=[mpB[i]], sig=(c == 7))
                    op("dve", lambda e: cp(e, kst[:, g, :], mps[i][:]), reads=[mpB[i]], writes=[kstB])
                    if g >= 8:
                        op("dve", lambda e: e.reduce_sum(out=kmT[:, g - 8, 2 * Tt:2 * Tt + 2],
                                                         in_=mps[i][:].rearrange("p (a b) -> p a b", b=256), axis=AX.X),
                           reads=[mpB[i]], writes=[kmB])
                for g4 in range(3):
                    dma("sp", T("KT")[4 * g4:4 * g4 + 4, :, Tt * 512:(Tt + 1) * 512].rearrange("g p t -> p g t"), kst[:, 4 * g4:4 * g4 + 4, :],
                        reads=[kstB], writes=[scrB["KT"]])
                for n in range(4):
                    for cg in range(3):
                        i = mcnt[0] % 4; mcnt[0] += 1
                        for c in range(8):
                            op("pe", lambda e: e.matmul(mps[i][:], lhsT=hT[:, c, n * 128:(n + 1) * 128],
                                                        rhs=Wv[:, c, cg * 512:(cg + 1) * 512], start=(c == 0), stop=(c == 7)),
                               reads=[WvB, hTB], writes=[mpB[i]], sig=(c == 7))
                        if cg < 2:
                            op("dve", lambda e: cp(e, vstd[:, 4 * cg:4 * cg + 4, n, 0:128],
                                                      mps[i][:].rearrange("p (h d) -> p h d", d=128)),
                               reads=[mpB[i]], writes=[vstdB])
                        else:
                            op("dve", lambda e: cp(e, vstm[:, :, n, 0:64], mps[i][:].rearrange("p (h d) -> p h d", d=64)),
                               reads=[mpB[i]], writes=[vstmB])
                for h in range(8):
                    dma("sp", T("VSd")[h, :, 4 * Tt:4 * Tt + 4, :], vstd[:, h, :, :], reads=[vstdB], writes=[scrB["VSd"]])
                    dma("sp", T("VSm")[h, :, 4 * Tt:4 * Tt + 4, :], vstm[:, h, :, :], reads=[vstmB], writes=[scrB["VSm"]])
            S.barrier()
        if stop_after in ("A", "A1"):
            return nc, S, used_inputs

        with ExitStack() as ph:
            gt = sbt(ph, "gtb", [128, D], F32); bt = sbt(ph, "btb", [128, D], F32); gbB = S.buf("gbb")
            dma("sp", gt[:], T("ln_in_g").partition_broadcast(128), writes=[gbB])
            dma("sp", bt[:], T("ln_in_b").partition_broadcast(128), writes=[gbB])
            Wq = sbt(ph, "Wq", [128, 8, 1536], BF16); WqB = S.buf("Wq")
            Wg = sbt(ph, "Wg", [128, 8, 2048], BF16); WgB = S.buf("Wg")
            for c in range(8):
                rows = slice(c * 128, (c + 1) * 128)
                dma("pool", Wq[:, c, 0:1024], T("w_in")[rows, C_DQ:C_DQ + 1024], writes=[WqB])
                dma("pool", Wq[:, c, 1024:1536], T("w_in")[rows, C_MQ:C_MQ + 512], writes=[WqB])
                dma("pool", Wg[:, c, :], T("w_in")[rows, C_G:C_G + 2048], writes=[WgB])
            bgT = sbt(ph, "bgT", [128, 16], F32); bgB = S.buf("bgT")
            dma("sp", bgT[:], T("b_gate").rearrange("(g p) -> p g", p=128), writes=[bgB], allow_slow_non_contiguous=True)
            kml = sbt(ph, "kml", [128, 4, 64], F32); kmh = sbt(ph, "kmh", [128, 4, 64], F32); kmlB = S.buf("kml")
            op("dve", lambda e: e.memset(kml[:], 0.0), writes=[kmlB])
            op("dve", lambda e: e.memset(kmh[:], 0.0), writes=[kmlB])
            op("dve", lambda e: e.tensor_scalar_mul(out=kml[0:64, :, :], in0=kmT[0:64, :, :], scalar1=1.0 / 256), reads=[kmB], writes=[kmlB])
            op("dve", lambda e: e.tensor_scalar_mul(out=kmh[64:128, :, :], in0=kmT[64:128, :, :], scalar1=1.0 / 256), reads=[kmB], writes=[kmlB])
            xts = [sbt(ph, f"xtb{i}", [128, 4, D], F32) for i in range(2)]; xtBs = S.bufs("xtb", 2)
            hbs = [sbt(ph, f"hbq{i}", [128, 4, D], BF16) for i in range(3)]; hbBs = S.bufs("hbb", 3)
            hTs = [sbt(ph, f"hTb{i}", [128, 8, 512], BF16) for i in range(2)]; hTBs = S.bufs("hTb", 2)
            qst = sbt(ph, "qst", [128, 8, 512], BF16); qstB = S.buf("qst")
            qf = sbt(ph, "qf", [128, 4, 512], F32); qfB = S.buf("qf")
            qa = sbt(ph, "qa", [128, 8, 512], BF16); qaB = S.buf("qa")
            gst = sbt(ph, "gst", [128, 16, 512], BF16); gstB = S.buf("gst")
            pbt = sbt(ph, "pbt", [128, 4, 64], F32); pvt = sbt(ph, "pvt", [128, 4, 64], F32); ott = sbt(ph, "ott", [128, 4, 64], F32)
            pcB = S.buf("pc")
            gm = sbt(ph, "gm", [128, 64], F32); gmB = S.buf("gm")
            m8 = sbt(ph, "m8", [128, 8], F32); m8B = S.buf("m8")
            sel = sbt(ph, "sel", [128, 64], F32); selB = S.buf("sel")
            mpad = sbt(ph, "mpad", [128, 4, 128], BF16); mpadB = S.buf("mpad")
            op("dve", lambda e: e.memset(mpad[:], 0.0), writes=[mpadB])
            lst = ln_state(ph, "b_")
            tps = [pst(ph, f"tpb{i}", [128, 1024], BF16) for i in range(2)]; tpB = S.bufs("tpb", 2)
            mps = [pst(ph, f"mpb{i}", [128, 512], F32) for i in range(4)]; mpB = S.bufs("mpb", 4)
            gps = pst(ph, "gps", [128, 512], F32); gpsB = S.buf("gps")
            tcnt = [0]; mcnt = [0]
            def ln_b(r):
                xt, xtB = xts[r % 2], xtBs[r % 2]
                tsl = slice(r * 512, (r + 1) * 512)
                dma("sp", xt[:], T("x_own")[tsl, :].rearrange("(n p) d -> p n d", p=128), writes=[xtB])
                layer_norm_ap(lst, xt, xtB, gt, bt, gbB, hbs[r % 3], hbBs[r % 3], epsc, epsB, keep32=True)
                dma("sp", T("Hs")[tsl, :].rearrange("(n p) d -> p n d", p=128), xt[:], reads=[xtB], writes=[scrB["Hs"]])

            def tr_b(r):
                transpose_to(hbs[r % 3], hbBs[r % 3], hTs[r % 2], hTBs[r % 2], tps, tpB, tcnt, eng="dve")

            ln_b(0); ln_b(1); tr_b(0)
            for r in range(8):
                if r + 2 < 8:
                    ln_b(r + 2)
                if r + 1 < 8:
                    tr_b(r + 1)
                hT, hTB = hTs[r % 2], hTBs[r % 2]
                tsl = slice(r * 512, (r + 1) * 512)
                dma("sp", pbt[:], T("pastbias")[r].rearrange("s p n -> p s n"), writes=[pcB])
                dma("sp", pvt[:], T("pastvalid")[r].rearrange("s p n -> p s n"), writes=[pcB])
                dma("sp", ott[:], T("ownterm")[r].rearrange("s p n -> p s n"), writes=[pcB])

                def proj(col0, M, dst_fn, eng=None):
                    i = mcnt[0] % 4; mcnt[0] += 1
                    for c in range(8):
                        op("pe", lambda e: e.matmul(mps[i][0:M, :], lhsT=W_[:, c, col0:col0 + M], rhs=hT[:, c, :],
                                                    start=(c == 0), stop=(c == 7)),
                           reads=[W_B, hTB], writes=[mpB[i]], sig=(c == 7))
                    dst_fn(mps[i], mpB[i])

                W_, W_B = Wq, WqB
                for g in range(8):
                    proj(g * 128, 128, lambda ps, psB: op("dve", lambda e: e.tensor_scalar_mul(out=qst[:, g, :], in0=ps[:], scalar1=0.125),
                                                          reads=[psB], writes=[qstB]))
                dma("sp", T("QT")[:, :, tsl].rearrange("g p t -> p g t"), qst[:], reads=[qstB], writes=[scrB["QT"]])
                for i4 in range(4):
                    proj(1024 + i4 * 128, 128, lambda ps, psB: op("act", lambda e: e.copy(out=qf[:, i4, :], in_=ps[:]),
                                                                  reads=[psB], writes=[qfB]))
                for h in range(8):
                    proj(1024 + h * 64, 64, lambda ps, psB: op("dve", lambda e: e.tensor_scalar_mul(out=qa[0:64, h, :], in0=ps[0:64, :], scalar1=0.125),
                                                               reads=[psB], writes=[qaB]))
                for h in range(8):
                    km = kml if h % 2 == 0 else kmh
                    for s in range(4):
                        op("pe", lambda e: e.matmul(gps[:, s * 64:(s + 1) * 64], lhsT=qf[:, h // 2, s * 128:(s + 1) * 128],
                                                    rhs=km[:, h // 2, :], start=True, stop=True),
                           reads=[qfB, kmlB], writes=[gpsB])
                    for s in range(4):
                        op("dve", lambda e: e.tensor_tensor(out=gm[:], in0=gps[:, s * 64:(s + 1) * 64], in1=pbt[:, s, :], op=ALU.add),
                           reads=[gpsB, pcB], writes=[gmB])
                        op("dve", lambda e: e.max(out=m8[:], in_=gm[:]), reads=[gmB], writes=[m8B])
                        op("dve", lambda e: e.scalar_tensor_tensor(out=sel[:], in0=gm[:], scalar=m8[:, 2:3], in1=pvt[:, s, :],
                                                                   op0=ALU.is_ge, op1=ALU.mult), reads=[gmB, m8B, pcB], writes=[selB])
                        op("dve", lambda e: e.scalar_tensor_tensor(out=mpad[:, s, 64:128], in0=sel[:], scalar=-NEG, in1=ott[:, s, :],
                                                                   op0=ALU.mult, op1=ALU.add), reads=[selB, pcB], writes=[mpadB])
                    i = tcnt[0] % 2; tcnt[0] += 1
                    for s in range(4):
                        op("pe", lambda e: e.transpose(out=tps[i][:, s * 128:(s + 1) * 128], in_=mpad[:, s, :], identity=ident[:]),
                           reads=[mpadB, identB], writes=[tpB[i]], sig=(s == 3))
                    op("act", lambda e: e.copy(out=qa[64:128, h, :], in_=tps[i][64:128, 0:512]), reads=[tpB[i]], writes=[qaB])
                dma("sp", T("QA")[:, :, tsl].rearrange("h p t -> p h t"), qa[:], reads=[qaB], writes=[scrB["QA"]])
                W_, W_B = Wg, WgB
                for gg in range(16):
                    proj(gg * 128, 128, lambda ps, psB: op("act", lambda e: e.activation(out=gst[:, gg, :], in_=ps[:], func=AF.Sigmoid,
                                                                                         bias=bgT[:, gg:gg + 1]),
                                                           reads=[psB, bgB], writes=[gstB]))
                dma("sp", T("Gs")[:, :, tsl].rearrange("g p t -> p g t"), gst[:], reads=[gstB], writes=[scrB["Gs"]])
            S.barrier()
        if stop_after == "B":
            return nc, S, used_inputs

        with ExitStack() as ph:
            KTb = sbt(ph, "KTb", [128, NT], BF16); KTB = S.bufs("KTc", 8)
            VTb = sbt(ph, "VTb", [128, 128, 129], BF16); VTB = S.bufs("VTc", 8)
            Bb = [sbt(ph, f"Bb{i}", [128, 17, 512], BF16) for i in range(2)]; BB = S.bufs("Bb", 2)
            Qb = [sbt(ph, f"Qb{i}", [128, NO], BF16) for i in range(2)]; QB = S.bufs("Qb", 2)
            chc = [sbt(ph, f"chc{i}", [128, 1], F32) for i in range(2)]; chB = S.bufs("chc", 2)
            Btmp = [sbt(ph, f"Btmp{i}", [128, 4, 512], F32) for i in range(2)]; BtB = S.bufs("Btmp", 2); bcnt = [0]
            zc = sbt(ph, "zc", [128, 1], F32); zcB = S.buf("zc")
            op("dve", lambda e: e.memset(zc[:], 0.0), writes=[zcB])
            PT = [sbt(ph, f"PT{i}", [128, 1024], BF16) for i in range(3)]; PTB = S.bufs("PT", 3)
            lq = sbt(ph, "lq", [128, 4, 64], F32); lqB = S.buf("lq")
            for i, a in enumerate((T("lam_q1"), T("lam_k1"), T("lam_q2"), T("lam_k2"))):
                dma("sp", lq[:, i, :], a.partition_broadcast(128), writes=[lqB])
            lw = sbt(ph, "lw", [128, 2, 64], F32); ls = sbt(ph, "ls", [128, 4], F32); lwB = S.buf("lw")
            op("dve", lambda e: e.tensor_tensor(out=lw[:, 0, :], in0=lq[:, 0, :], in1=lq[:, 1, :], op=ALU.mult), reads=[lqB], writes=[lwB])
            op("dve", lambda e: e.tensor_tensor(out=lw[:, 1, :], in0=lq[:, 2, :], in1=lq[:, 3, :], op=ALU.mult), reads=[lqB], writes=[lwB])
            op("dve", lambda e: e.reduce_sum(out=ls[:, 0:2], in_=lw[:], axis=AX.X), reads=[lwB], writes=[lwB])
            op("act", lambda e: e.activation(out=ls[:, 0:2], in_=ls[:, 0:2], func=AF.Exp), reads=[lwB], writes=[lwB])
            op("dve", lambda e: e.tensor_tensor(out=ls[:, 2:3], in0=ls[:, 1:2], in1=ls[:, 0:1], op=ALU.subtract), reads=[lwB], writes=[lwB])
            op("dve", lambda e: e.tensor_scalar_add(out=ls[:, 2:3], in0=ls[:, 2:3], scalar1=-LAM_INIT), reads=[lwB], writes=[lwB])
            nlam = ls[:, 2:3]
            sg = sbt(ph, "sg", [128, 128], F32); sgB = S.buf("sg")
            dma("sp", sg[:], T("diff_sub_g").partition_broadcast(128), writes=[sgB])
            op("dve", lambda e: e.tensor_scalar_mul(out=sg[:], in0=sg[:], scalar1=1.0 - LAM_INIT), reads=[sgB], writes=[sgB])
            rc = sbt(ph, "rc", [128, 8], F32); rcB = S.buf("rc")
            oa = sbt(ph, "oa", [128, 128], F32); oaB = S.buf("oa")
            od = sbt(ph, "od", [128, 128], F32); odB = S.buf("od")
            ojk = sbt(ph, "ojk", [128, 128], F32); ojB = S.buf("ojk")
            ob = sbt(ph, "ob", [128, 4, 128], BF16); obB = S.buf("ob")
            osb = sbt(ph, "osb", [128, 3, 387], F32); osbB = S.buf("osb")
            od4 = sbt(ph, "od4", [128, 4, 128], F32)
            rq = sbt(ph, "rq", [128, 16], F32); rqB = S.buf("rq")
            ast = [sbt(ph, f"ast{i}", [128, 512], BF16) for i in range(2)]; astB = S.bufs("ast", 2)
            Sps = [pst(ph, f"Sps{i}", [128, 1024], F32) for i in range(2)]; SpB = S.bufs("Sps", 2)
            Ops = [pst(ph, f"Ops{i}", [128, 512], F32) for i in range(3)]; OpB = S.buf("Ops")
            tpc = pst(ph, "tpc", [128, 1024], BF16); tpcB = S.buf("tpc")
            scnt = [0]; pcnt = [0]; acnt = [0]
            heads = range(16) if stop_after != "C1" else [0, 8]
            nslots = 8 if stop_after != "C1" else 2
            for hi, hh in enumerate(heads):
                moba = hh >= 8
                hm = hh - 8
                par = hi % 2
                dma("sp", chc[par][:], T("rel_table")[31, hh:hh + 1].partition_broadcast(128), writes=[chB[par]])
                if not moba:
                    dma("sp", Qb[par][:], T("QT")[hh], reads=[scrB["QT"]], writes=[QB[par]])
                else:
                    dma("sp", Qb[par][:], T("QA")[hm], reads=[scrB["QA"]], writes=[QB[par]])
                for ui, u0 in enumerate(range(0, 17, 4)):
                    u1 = min(17, u0 + 4)
                    src = bass.AP(tensor=T("t2").tensor, offset=T("t2")[hh, 0, u0 * 128].offset, ap=[[2560, 128], [128, u1 - u0], [1, 512]])
                    bi = bcnt[0] % 2; bcnt[0] += 1
                    dma("sp", Btmp[bi][:, 0:u1 - u0, :], src, writes=[BtB[bi]])
                    op("dve", lambda e: e.tensor_scalar(out=Btmp[bi][:, 0:u1 - u0, :], in0=Btmp[bi][:, 0:u1 - u0, :], scalar1=chc[par][:, 0:1],
                                                        scalar2=None, op0=ALU.subtract), reads=[BtB[bi], chB[par]], writes=[BtB[bi]])
                    op("act", lambda e: e.activation(out=Bb[par][:, u0:u1, :], in_=Btmp[bi][:, 0:u1 - u0, :], func=AF.Exp),
                       reads=[BtB[bi]], writes=[BB[par]])
                for c in range(nslots):
                    ksl = slice(c * 2048, (c + 1) * 2048)
                    if not moba:
                        dma("sp", KTb[:, ksl], T("KT")[hh, :, ksl], reads=[scrB["KT"]], writes=[KTB[c]])
                        dma("sp", VTb[:, 16 * c:16 * c + 16, :], T("VSd")[hh, :, 16 * c:16 * c + 16, :], reads=[scrB["VSd"]], writes=[VTB[c]])
                    else:
                        dma("sp", KTb[0:64, ksl], T("KT")[8 + hm // 2, (hm % 2) * 64:(hm % 2) * 64 + 64, ksl], reads=[scrB["KT"]], writes=[KTB[c]])
                        dma("sp", KTb[64:128, ksl], T("eind")[:, ksl], writes=[KTB[c]])
                        dma("sp", VTb[:, 16 * c:16 * c + 16, 0:65], T("VSm")[hm, :, 16 * c:16 * c + 16, :], reads=[scrB["VSm"]], writes=[VTB[c]])
                nm = 1 if moba else 2
                vw = 65 if moba else 129
                dv = vw - 1
                def acc_ap(a):
                    return Ops[a // 3][:, (a % 3) * 129:(a % 3) * 129 + vw]

                def osb_ap(a):
                    return osb[:, a // 3, (a % 3) * 129:(a % 3) * 129 + vw]

                def parts_of(r, st):
                    if not moba:
                        return [(st, slice(0, 64), 0), (st, slice(64, 128), 4)]
                    return [(2 * st, slice(0, 128), 0), (2 * st + 1, slice(0, 128), 0)]

                def emit_qk(r, st):
                    qsl = slice(r * 512, (r + 1) * 512)
                    si = scnt[0] % 2; scnt[0] += 1
                    Sp = Sps[si]
                    for p, (kt, rows, _) in enumerate(parts_of(r, st)):
                        c = kt // 16
                        op("pe", lambda e: e.matmul(Sp[:, p * 512:(p + 1) * 512], lhsT=KTb[rows, kt * 128:(kt + 1) * 128],
                                                    rhs=Qb[par][rows, qsl], start=True, stop=True),
                           reads=[KTB[c], QB[par]], writes=[SpB[si]], sig=(p == 1))
                    return (si,)

                def emit_exp_av(r, st, si):
                    nkt = 16 * r + 16
                    Sp = Sps[si]
                    pi = pcnt[0] % 3; pcnt[0] += 1
                    op("act", lambda e: e.activation(out=PT[pi][:, 0:1024], in_=Sp[:, 0:1024], func=AF.Exp, bias=chc[par][:]),
                       reads=[SpB[si], chB[par]], writes=[PTB[pi]])
                    parts = parts_of(r, st)
                    for p, (kt, rows, _) in enumerate(parts):
                        ktp = kt - (16 * r - 1)
                        if ktp >= 0:
                            u = 16 - ktp
                            op("dve", lambda e: e.tensor_tensor(out=PT[pi][:, p * 512:(p + 1) * 512], in0=PT[pi][:, p * 512:(p + 1) * 512],
                                                                in1=Bb[par][:, u, :], op=ALU.mult),
                               reads=[PTB[pi], BB[par]], writes=[PTB[pi]])
                    for p, (kt, rows, abase) in enumerate(parts):
                        c = kt // 16
                        for s in range(4):
                            a = abase + s
                            bank = a // 3
                            st_flag = (kt == 0) and (bank not in slot_banks)
                            slot_banks.add(bank)
                            op("pe", lambda e: e.matmul(acc_ap(a), lhsT=PT[pi][:, p * 512 + s * 128:p * 512 + (s + 1) * 128],
                                                        rhs=VTb[:, kt, 0:vw], start=st_flag, stop=(kt == nkt - 1),
                                                        skip_group_check=True),
                               reads=[PTB[pi], VTB[c]], writes=[OpB], sig=(p == 1 and s == 3))

                def emit_epi_dve(r):
                    if not moba:
                        for bk, wdt in ((0, 387), (1, 387), (2, 258)):
                            op("dve", lambda e: e.tensor_copy(out=osb[:, bk, 0:wdt], in_=Ops[bk][:, 0:wdt]), reads=[OpB], writes=[osbB])
                    else:
                        for a in range(4):
                            op("dve", lambda e: e.tensor_copy(out=osb_ap(a), in_=acc_ap(a)), reads=[OpB], writes=[osbB])
                    if not moba:
                        for s in range(4):
                            a0, a1 = osb_ap(s), osb_ap(4 + s)
                            op("dve", lambda e: e.reciprocal(out=rc[:, 0:1], in_=a0[:, 128:129]), reads=[osbB], writes=[rcB])
                            op("dve", lambda e: e.reciprocal(out=rc[:, 1:2], in_=a1[:, 128:129]), reads=[osbB], writes=[rcB])
                            op("dve", lambda e: e.tensor_tensor(out=rc[:, 2:3], in0=rc[:, 1:2], in1=nlam, op=ALU.mult), reads=[rcB, lwB], writes=[rcB])
                            op("dve", lambda e: e.tensor_scalar_mul(out=oa[:], in0=a0[:, 0:128], scalar1=rc[:, 0:1]), reads=[osbB, rcB], writes=[oaB])
                            op("dve", lambda e: e.scalar_tensor_tensor(out=od4[:, s, :], in0=a1[:, 0:128], scalar=rc[:, 2:3], in1=oa[:],
                                                                       op0=ALU.mult, op1=ALU.add), reads=[osbB, rcB, oaB], writes=[odB])
                            op("dve", lambda e: e.tensor_tensor(out=ojk[:], in0=od4[:, s, :], in1=od4[:, s, :], op=ALU.mult), reads=[odB], writes=[ojB])
                            op("dve", lambda e: e.reduce_sum(out=rq[:, s:s + 1], in_=ojk[:], axis=AX.X), reads=[ojB], writes=[rqB])
                        op("dve", lambda e: e.tensor_scalar(out=rq[:, 4:8], in0=rq[:, 0:4], scalar1=1.0 / 128, scalar2=EPS,
                                                            op0=ALU.mult, op1=ALU.add), reads=[rqB], writes=[rqB])
                        op("act", lambda e: e.activation(out=rq[:, 8:12], in_=rq[:, 4:8], func=AF.Sqrt), reads=[rqB], writes=[rqB])
                        op("dve", lambda e: e.reciprocal(out=rq[:, 12:16], in_=rq[:, 8:12]), reads=[rqB], writes=[rqB])
                        for s in range(4):
                            op("dve", lambda e: e.scalar_tensor_tensor(out=ob[:, s, :], in0=od4[:, s, :], scalar=rq[:, 12 + s:13 + s], in1=sg[:],
                                                                       op0=ALU.mult, op1=ALU.mult), reads=[odB, rqB, sgB], writes=[obB])
                    else:
                        for s in range(4):
                            a0 = osb_ap(s)
                            op("dve", lambda e: e.reciprocal(out=rc[:, 0:1], in_=a0[:, 64:65]), reads=[osbB], writes=[rcB])
                            op("dve", lambda e: e.tensor_scalar_mul(out=ob[:, s, 0:64], in0=a0[:, 0:64], scalar1=rc[:, 0:1]),
                               reads=[osbB, rcB], writes=[obB])

                def emit_epi_tail(r):
                    qsl = slice(r * 512, (r + 1) * 512)
                    ai = acnt[0] % 2; acnt[0] += 1
                    for s in range(4):
                        op("pe", lambda e: e.transpose(out=tpc[0:dv, s * 128:(s + 1) * 128], in_=ob[:, s, 0:dv], identity=ident[:]),
                           reads=[obB, identB], writes=[tpcB], sig=(s == 3))
                    op("dve", lambda e: e.tensor_copy(out=ast[ai][0:dv, :], in_=tpc[0:dv, 0:512]), reads=[tpcB], writes=[astB[ai]])
                    if not moba:
                        dma("pool", T("ATT")[hh, :, qsl], ast[ai][:], reads=[astB[ai]], writes=[scrB["ATT"]])
                    else:
                        dma("pool", T("ATT")[8 + hm // 2, (hm % 2) * 64:(hm % 2) * 64 + 64, qsl], ast[ai][0:64, :], reads=[astB[ai]],
                            writes=[scrB["ATT"]])

                spk = 2 if moba else 1
                flat = [(r, st) for r in range(nslots) for st in range((16 * r + 16) // spk)]
                slot_banks = set()
                nxt = emit_qk(*flat[0])
                deferred = None
                for idx, (r, st) in enumerate(flat):
                    cur = nxt
                    if idx + 1 < len(flat):
                        nxt = emit_qk(*flat[idx + 1])
                    if st == 0:
                        slot_banks.clear()
                    emit_exp_av(r, st, *cur)
                    if deferred is not None and deferred[1] == idx:
                        emit_epi_tail(deferred[0]); deferred = None
                    if st == (16 * r + 16) // spk - 1:
                        emit_epi_dve(r)
                        if idx + 1 < len(flat):
                            deferred = (r, idx + 4)
                        else:
                            emit_epi_tail(r)
            S.barrier()
        if stop_after in ("C", "C1"):
            return nc, S, used_inputs

        with ExitStack() as ph:
            lng = [sbt(ph, f"lng{i}", [128, D], F32) for i in range(6)]; lngB = S.buf("lng")
            for i, a in enumerate((T("ln1_g"), T("ln1_b"), T("ln2_g"), T("ln2_b"), T("ln3_g"), T("ln3_b"))):
                dma("sp", lng[i][:], a.partition_broadcast(128), writes=[lngB])
            ones = sbt(ph, "ones", [128, 128], BF16); onesB = S.buf("ones")
            op("dve", lambda e: e.memset(ones[:], 1.0), writes=[onesB])
            NW = 4
            wr = [sbt(ph, f"wr{i}", [128, 4096], BF16) for i in range(NW)]; wrB = S.bufs("wr", NW)
            wcnt = [0]

            def wload(src, kc, col0, ncol, row0=0):
                i = wcnt[0] % NW; wcnt[0] += 1
                v = wr[i][:, 0:kc * ncol].rearrange("p (c n) -> p c n", n=ncol)
                dma("sp", v, src[row0:row0 + kc * 128, col0:col0 + ncol].rearrange("(c p) n -> p c n", p=128),
                    reads=[scrB["wb"]], writes=[wrB[i]])
                return v, wrB[i]

            kxT = sbt(ph, "kxT", [128, 8, 256], BF16); kxB = S.buf("kxT")
            vx = sbt(ph, "vx", [128, 2, D], BF16); vxB = S.buf("vx")
            lst = ln_state(ph, "d_")
            ph2 = ExitStack()
            memf = sbt(ph2, "memf", [128, 2, D], F32); memB = S.buf("memf")
            memb = sbt(ph2, "memb", [128, 2, D], BF16); membB = S.buf("memb")
            memT = sbt(ph2, "memT", [128, 8, 256], BF16); memTB = S.buf("memT")
            tps = [pst(ph, f"tpd{i}", [128, 1024], BF16) for i in range(2)]; tpB = S.bufs("tpd", 2)
            mps = [pst(ph, f"mpd{i}", [128, 512], F32) for i in range(6)]; mpB = S.bufs("mpd", 6)
            tcnt = [0]; mcnt = [0]

            def nextps():
                i = mcnt[0] % 2; mcnt[0] += 1
                return mps[i], mpB[i]

            dma("sp", memf[:], T("mem_b").rearrange("(n p) d -> p n d", p=128), writes=[memB])
            op("dve", lambda e: e.tensor_copy(out=memb[:], in_=memf[:]), reads=[memB], writes=[membB])
            transpose_to(memb, membB, memT, memTB, tps, tpB, tcnt, nsub=2)
            for half in range(2):
                wv_, wB_ = wload(T("wb_k"), 8, half * 512, 512)
                for g4 in range(4):
                    ps, psB = nextps()
                    for c in range(8):
                        op("pe", lambda e: e.matmul(ps[:, 0:256], lhsT=wv_[:, c, g4 * 128:(g4 + 1) * 128], rhs=memT[:, c, :],
                                                    start=(c == 0), stop=(c == 7)), reads=[wB_, memTB], writes=[psB], sig=(c == 7))
                    op(ev_eng(), lambda e: cp(e, kxT[:, half * 4 + g4, :], ps[:, 0:256]), reads=[psB], writes=[kxB])
            for half in range(2):
                wv_, wB_ = wload(T("wb_v"), 8, half * 512, 512)
                for mt in range(2):
                    ps, psB = nextps()
                    for c in range(8):
                        op("pe", lambda e: e.matmul(ps[:], lhsT=memT[:, c, mt * 128:(mt + 1) * 128], rhs=wv_[:, c, :],
                                                    start=(c == 0), stop=(c == 7)), reads=[wB_, memTB], writes=[psB], sig=(c == 7))
                    op(ev_eng(), lambda e: cp(e, vx[:, mt, half * 512:(half + 1) * 512], ps[:]), reads=[psB], writes=[vxB])

            S.barrier()
            ph2.close()
            attT = sbt(ph, "attT", [128, 12, 512], BF16); attB = S.buf("attT")
            gT = sbt(ph, "gT", [128, 16, 512], BF16); gTB = S.buf("gT")
            ydT = sbt(ph, "ydT", [128, 8, 512], BF16); ydB = S.buf("ydT")
            mixT = sbt(ph, "mixT", [128, 8, 512], BF16); mixB = S.buf("mixT")
            tmpb = sbt(ph, "tmpb", [128, 512], BF16); tmpB = S.buf("tmpb")
            ha = sbt(ph, "ha", [128, 4, D], F32); haB = S.buf("ha")
            hbf = sbt(ph, "hbf", [128, 4, D], F32); hbfB = S.buf("hbf")
            hbb = sbt(ph, "hbb2", [128, 4, D], BF16); hbbB = S.buf("hbb2")
            hT = sbt(ph, "hTd", [128, 8, 512], BF16); hTB = S.buf("hTd")
            qxT, qxB = ydT, ydB
            pTs = [sbt(ph, f"pTx{i}", [128, 512], BF16) for i in range(2)]; pTB = S.bufs("pTx", 2)
            rs = sbt(ph, "rs", [128, 512], F32); rsB = S.buf("rs")
            oT, oTB = mixT, mixB
            rl = sbt(ph, "rl", [128, 512], BF16); rlB = S.buf("rl")
            actT = sbt(ph, "actT", [128, 32, 512], BF16); actB = S.buf("actT")
            def fm_proj(wsrc, kc, ncolgroups, src, srcB, evac, row0=0, gpw=None):
                gpw = gpw or (4096 // (kc * 128))
                for g0 in range(0, ncolgroups, gpw):
                    n = min(gpw, ncolgroups - g0)
                    wv_, wB_ = wload(wsrc, kc, g0 * 128, n * 128, row0=row0)
                    for gi in range(n):
                        ps, psB = nextps()
                        for c in range(kc):
                            op("pe", lambda e: e.matmul(ps[:], lhsT=wv_[:, c, gi * 128:(gi + 1) * 128], rhs=src[:, c, :],
                                                        start=(c == 0), stop=(c == kc - 1)), reads=[wB_, srcB], writes=[psB], sig=(c == kc - 1))
                        evac(g0 + gi, ps, psB)

            def tm_proj(wsrc, nk, src, srcB, evac):
                for half in range(2):
                    for kg in range(nk // 8):
                        wv_, wB_ = wload(wsrc, 8, half * 512, 512, row0=kg * 1024)
                        for sub in range(4):
                            for c in range(8):
                                k = kg * 8 + c
                                op("pe", lambda e: e.matmul(mps[2 + sub][:], lhsT=src[:, k, sub * 128:(sub + 1) * 128], rhs=wv_[:, c, :],
                                                            start=(k == 0), stop=(k == nk - 1)),
                                   reads=[wB_, srcB], writes=[mpB[2 + sub]], sig=(c == 7))
                    for sub in range(4):
                        evac(half, sub, mps[2 + sub], mpB[2 + sub])

            def resid_evac(hin, hinB, hout, houtB):
                def f(half, sub, ps, psB):
                    cs = slice(half * 512, (half + 1) * 512)
                    op("dve", lambda e: e.scalar_tensor_tensor(out=hout[:, sub, cs], in0=hin[:, sub, cs], scalar=ALPHA, in1=ps[:],
                                                               op0=ALU.mult, op1=ALU.add), reads=[hinB, psB], writes=[houtB])
                return f

            ntile = 8 if stop_after != "D1" else 1
            for t in range(ntile):
                tsl = slice(t * 512, (t + 1) * 512)
                dma("sp", attT[:], T("ATT")[:, :, tsl].rearrange("g p t -> p g t"), reads=[scrB["ATT"]], writes=[attB])
                dma("sp", gT[:], T("Gs")[:, :, tsl].rearrange("g p t -> p g t"), reads=[scrB["Gs"]], writes=[gTB])
                dma("sp", ha[:], T("Hs")[tsl, :].rearrange("(n p) d -> p n d", p=128), reads=[scrB["Hs"]], writes=[haB])
                fm_proj(T("wb_brd"), 8, 8, attT, attB,
                        lambda g, ps, psB: op("dve", lambda e: e.tensor_tensor(out=ydT[:, g, :], in0=ps[:], in1=gT[:, g, :], op=ALU.mult),
                                              reads=[psB, gTB], writes=[ydB]))

                def ev_m(g, ps, psB):
                    op("dve", lambda e: e.tensor_tensor(out=tmpb[:], in0=ps[:], in1=gT[:, 8 + g, :], op=ALU.mult),
                       reads=[psB, gTB], writes=[tmpB])
                    op("pool", lambda e: e.tensor_tensor(out=mixT[:, g, :], in0=tmpb[:], in1=ydT[:, g, :], op=ALU.add),
                       reads=[tmpB, ydB], writes=[mixB])
                fm_proj(T("wb_brm"), 4, 8, attT[:, 8:12, :], attB, ev_m)
                tm_proj(T("wb_out"), 8, mixT, mixB, resid_evac(ha, haB, hbf, hbfB))
                layer_norm(lst, hbf, hbfB, lng[0], lng[1], lngB, hbf, hbfB, hbb, hbbB)
                transpose_to(hbb, hbbB, hT, hTB, tps, tpB, tcnt)
                fm_proj(T("wb_q"), 8, 8, hT, hTB, lambda g, ps, psB: op(ev_eng(), lambda e: cp(e, qxT[:, g, :], ps[:]), reads=[psB], writes=[qxB]))
                for hd in range(4):
                    for mt in range(2):
                        ps, psB = nextps()
                        for c in range(2):
                            op("pe", lambda e: e.matmul(ps[:], lhsT=kxT[:, 2 * hd + c, mt * 128:(mt + 1) * 128], rhs=qxT[:, 2 * hd + c, :],
                                                        start=(c == 0), stop=(c == 1)), reads=[kxB, qxB], writes=[psB], sig=(c == 1))
                        op("act", lambda e: e.activation(out=pTs[mt][:], in_=ps[:], func=AF.Exp, scale=1.0 / 16), reads=[psB], writes=[pTB[mt]])
                    ps, psB = nextps()
                    for mt in range(2):
                        op("pe", lambda e: e.matmul(ps[:], lhsT=ones[:], rhs=pTs[mt][:], start=(mt == 0), stop=(mt == 1)),
                           reads=[onesB, pTB[mt]], writes=[psB], sig=(mt == 1))
                    op("dve", lambda e: e.reciprocal(out=rs[:], in_=ps[:]), reads=[psB], writes=[rsB])
                    for dc in range(2):
                        ps, psB = nextps()
                        for mt in range(2):
                            op("pe", lambda e: e.matmul(ps[:], lhsT=vx[:, mt, hd * 256 + dc * 128:hd * 256 + (dc + 1) * 128], rhs=pTs[mt][:],
                                                        start=(mt == 0), stop=(mt == 1)), reads=[vxB, pTB[mt]], writes=[psB], sig=(mt == 1))
                        op("dve", lambda e: e.tensor_tensor(out=oT[:, 2 * hd + dc, :], in0=ps[:], in1=rs[:], op=ALU.mult),
                           reads=[psB, rsB], writes=[oTB])
                tm_proj(T("wb_o"), 8, oT, oTB, resid_evac(hbf, hbfB, ha, haB))
                layer_norm(lst, ha, haB, lng[2], lng[3], lngB, ha, haB, hbb, hbbB)
                transpose_to(hbb, hbbB, hT, hTB, tps, tpB, tcnt)

                def ev_f(g, ps, psB):
                    op("act", lambda e: e.activation(out=rl[:], in_=ps[:], func=AF.Relu), reads=[psB], writes=[rlB])
                    op(("dve", "pool")[g % 2], lambda e: e.tensor_tensor(out=actT[:, g, :], in0=rl[:], in1=rl[:], op=ALU.mult),
                       reads=[rlB], writes=[actB])
                fm_proj(T("wb_f1"), 8, 32, hT, hTB, ev_f)
                tm_proj(T("wb_f2"), 32, actT, actB, resid_evac(ha, haB, hbf, hbfB))
                layer_norm(lst, hbf, hbfB, lng[4], lng[5], lngB, hbf, hbfB, None, None)
                dma("sp", T("out_d")[tsl, :].rearrange("(n p) d -> p n d", p=128), hbf[:], reads=[hbfB], writes=[scrB["out"]])
            S.barrier()
        return nc, S, used_inputs


def _rel_bucket(n):
    n = np.asarray(n)
    nf = np.maximum(n, 1).astype(np.float32)
    large = 16 + (np.log(nf / np.float32(16)) / np.float32(math.log(128 / 16)) * np.float32(16)).astype(np.int32)
    large = np.minimum(large, 31)
    return np.where(n < 16, n, large)


def make_inputs(i, inputs):
    b, j = i // 4, i % 4
    f = lambda a: np.ascontiguousarray(np.asarray(a, dtype=np.float32))
    x = np.asarray(inputs["x"], dtype=np.float32)
    qts = [4 * r + j for r in range(8)]
    m = {}
    m["x_b"] = f(x[b])
    m["x_own"] = f(np.concatenate([x[b, 512 * q:512 * q + 512] for q in qts], axis=0))
    m["mem_b"] = f(inputs["mem"][b])
    for k in ("ln_in_g", "ln_in_b", "rel_table"):
        m[k] = f(inputs[k])
    for k in ("w_in", "b_gate", "lam_q1", "lam_k1", "lam_q2", "lam_k2", "diff_sub_g", "w_br_diff", "w_br_moba", "w_out",
              "ln1_g", "ln1_b", "wq_x", "wk_x", "wv_x", "wo_x", "ln2_g", "ln2_b", "w_ff1", "w_ff2", "ln3_g", "ln3_b"):
        m[k] = f(np.asarray(inputs[k])[0])
    rel = np.asarray(inputs["rel_table"], dtype=np.float32)
    m0 = 512 * j - 1920
    mm = np.arange(2560)[None, :] + m0
    kk = np.arange(128)[:, None]
    dist = mm - kk
    bk = _rel_bucket(np.maximum(dist, 0))
    tab = rel[bk]
    tab = np.where((dist >= 0)[:, :, None], tab, np.float32(NEG))
    m["t2"] = f(np.transpose(tab, (2, 0, 1)))
    blk = np.arange(NT) // 256
    m["eind"] = (blk[None, :] == np.arange(64)[:, None]).astype(ml_dtypes.bfloat16)
    pb = np.zeros((8, 4, 128, 64), np.float32); pv = np.zeros_like(pb); ot = np.full_like(pb, NEG)
    for r in range(8):
        for s in range(4):
            cur = 2 * (4 * r + j) + s // 2
            pb[r, s, :, cur:] = -1e30
            pv[r, s, :, :cur] = 1.0
            ot[r, s, :, cur] = 0.0
    m["pastbias"], m["pastvalid"], m["ownterm"] = pb, pv, ot
    m["ident"] = np.eye(128).astype(ml_dtypes.bfloat16)
    return m


def kernel(**inputs):
    nc, S, used = build_program()
    in_maps = [{k: v for k, v in make_inputs(i, inputs).items() if k in used} for i in range(8)]
    res = run_bass_kernel_spmd(nc, in_maps, core_ids=list(range(8)))
    out = np.zeros((2, NT, D), np.float32)
    for i in range(8):
        b, j = i // 4, i % 4
        o = res.results[i]["out"]
        for r in range(8):
            q = 4 * r + j
            out[b, 512 * q:512 * q + 512] = o[r * 512:(r + 1) * 512]
    return out
```

```python
import math
from contextlib import ExitStack
import numpy as np
import ml_dtypes
import concourse.bass as bass
import concourse.mybir as mybir
from concourse.bass_utils import run_bass_kernel_spmd

F32 = mybir.dt.float32
BF16 = mybir.dt.bfloat16
AF = mybir.ActivationFunctionType
ALU = mybir.AluOpType
AX = mybir.AxisListType

D = 1024
NT = 16384
NO = 4096
NEG = -30000.0
ALPHA = 2.0 ** 0.25
LAM_INIT = 0.2
EPS = 1e-5
C_DQ, C_DK, C_DV, C_MQ, C_MK, C_MV, C_G = 0, 1024, 2048, 3072, 3584, 4096, 4608


class Buf:
    __slots__ = ("name", "w", "r", "dsem", "dcnt")

    def __init__(self, name):
        self.name = name
        self.w = None
        self.r = {}
        self.dsem = None
        self.dcnt = 0


class Sched:
    def __init__(self, nc, es):
        self.nc = nc
        self.es = es
        self.E = {}
        for name, eng in (("pe", nc.tensor), ("act", nc.scalar), ("dve", nc.vector),
                          ("pool", nc.gpsimd), ("sp", nc.sync)):
            sem = es.enter_context(nc.semaphore("s_" + name))
            self.E[name] = dict(eng=eng, sem=sem, cnt=0, seen={})
        self.dbufs = []
        self.nwaits = 0

    def buf(self, name):
        return Buf(name)

    def bufs(self, name, n):
        return [Buf(f"{name}{i}") for i in range(n)]

    def _wait(self, en, ev):
        e = self.E[en]
        if e["seen"].get(ev[0], 0) >= ev[2]:
            return
        e["eng"].wait_ge(ev[1], ev[2])
        e["seen"][ev[0]] = ev[2]
        self.nwaits += 1

    def _deps(self, en, reads, writes):
        for b in reads:
            if b.w is not None:
                self._wait(en, b.w)
        for b in writes:
            if b.w is not None:
                self._wait(en, b.w)
            for ev in b.r.values():
                self._wait(en, ev)

    def op(self, en, fn, reads=(), writes=(), sig=True):
        e = self.E[en]
        if en == "pe":
            e["seen"]["pe"] = 1 << 60
        self._deps(en, reads, writes)
        ins = fn(e["eng"])
        if sig:
            e["cnt"] += 1
            ins.then_inc(e["sem"], 1)
            ev = (en, e["sem"], e["cnt"])
        else:
            ev = (en, e["sem"], e["cnt"] + 1)
        for b in reads:
            b.r[en] = ev
        for b in writes:
            b.w = ev
            b.r = {}
        return ins

    def dma(self, qn, out, in_, reads=(), writes=(), **kw):
        q = self.E[qn]
        self._deps(qn, reads, writes)
        b = writes[0]
        if b.dsem is None:
            b.dsem = self.es.enter_context(self.nc.semaphore("d_" + b.name))
            self.dbufs.append(b)
        ins = q["eng"].dma_start(out=out, in_=in_, **kw)
        b.dcnt += 16
        ins.then_inc(b.dsem, 16)
        key = ("d", id(b))
        ev = (key, b.dsem, b.dcnt)
        for r in reads:
            r.r[key] = ev
        for w in writes:
            w.w = ev
            w.r = {}
        return ins

    def barrier(self):
        for en, e in self.E.items():
            for on, o in self.E.items():
                if on != en and o["cnt"] > 0:
                    self._wait(en, (on, o["sem"], o["cnt"]))
            for b in self.dbufs:
                self._wait(en, (("d", id(b)), b.dsem, b.dcnt))


def build_program(stop_after=None, dbg=False):
    nc = bass.Bass("TRN2", target_bir_lowering=False)

    def din(name, shape, dt=F32):
        return nc.dram_tensor(name, list(shape), dt, kind="ExternalInput").ap()

    def dscr(name, shape, dt, kind="Internal"):
        return nc.dram_tensor(name, list(shape), dt, kind=kind).ap()

    _decl = {}
    _decl["x_b"] = ("x_b", lambda: [NT, D], F32, "ExternalInput")
    _decl["x_own"] = ("x_own", lambda: [NO, D], F32, "ExternalInput")
    _decl["mem_b"] = ("mem_b", lambda: [256, D], F32, "ExternalInput")
    _decl["ln_in_g"] = ("ln_in_g", lambda: [D], F32, "ExternalInput")
    _decl["ln_in_b"] = ("ln_in_b", lambda: [D], F32, "ExternalInput")
    _decl["rel_table"] = ("rel_table", lambda: [32, 16], F32, "ExternalInput")
    _decl["w_in"] = ("w_in", lambda: [D, 6656], F32, "ExternalInput")
    _decl["b_gate"] = ("b_gate", lambda: [2048], F32, "ExternalInput")
    _decl["lam_q1"] = ("lam_q1", lambda: [64], F32, "ExternalInput")
    _decl["lam_k1"] = ("lam_k1", lambda: [64], F32, "ExternalInput")
    _decl["lam_q2"] = ("lam_q2", lambda: [64], F32, "ExternalInput")
    _decl["lam_k2"] = ("lam_k2", lambda: [64], F32, "ExternalInput")
    _decl["diff_sub_g"] = ("diff_sub_g", lambda: [128], F32, "ExternalInput")
    _decl["w_br_diff"] = ("w_br_diff", lambda: [1024, D], F32, "ExternalInput")
    _decl["w_br_moba"] = ("w_br_moba", lambda: [512, D], F32, "ExternalInput")
    _decl["w_out"] = ("w_out", lambda: [D, D], F32, "ExternalInput")
    _decl["ln1_g"] = ("ln1_g", lambda: [D], F32, "ExternalInput")
    _decl["ln1_b"] = ("ln1_b", lambda: [D], F32, "ExternalInput")
    _decl["wq_x"] = ("wq_x", lambda: [D, D], F32, "ExternalInput")
    _decl["wk_x"] = ("wk_x", lambda: [D, D], F32, "ExternalInput")
    _decl["wv_x"] = ("wv_x", lambda: [D, D], F32, "ExternalInput")
    _decl["wo_x"] = ("wo_x", lambda: [D, D], F32, "ExternalInput")
    _decl["ln2_g"] = ("ln2_g", lambda: [D], F32, "ExternalInput")
    _decl["ln2_b"] = ("ln2_b", lambda: [D], F32, "ExternalInput")
    _decl["w_ff1"] = ("w_ff1", lambda: [D, 4096], F32, "ExternalInput")
    _decl["w_ff2"] = ("w_ff2", lambda: [4096, D], F32, "ExternalInput")
    _decl["ln3_g"] = ("ln3_g", lambda: [D], F32, "ExternalInput")
    _decl["ln3_b"] = ("ln3_b", lambda: [D], F32, "ExternalInput")
    _decl["t2"] = ("t2", lambda: [16, 128, 2560], F32, "ExternalInput")
    _decl["eind"] = ("eind", lambda: [64, NT], BF16, "ExternalInput")
    _decl["pastbias"] = ("pastbias", lambda: [8, 4, 128, 64], F32, "ExternalInput")
    _decl["pastvalid"] = ("pastvalid", lambda: [8, 4, 128, 64], F32, "ExternalInput")
    _decl["ownterm"] = ("ownterm", lambda: [8, 4, 128, 64], F32, "ExternalInput")
    _decl["ident_in"] = ("ident", lambda: [128, 128], BF16, "ExternalInput")
    _decl["out_d"] = ("out", lambda: [NO, D], F32, "ExternalOutput")
    _decl["KT"] = ("KT", lambda: [12, 128, NT], BF16, "dk")
    _decl["VSd"] = ("VSd", lambda: [8, 128, 128, 129], BF16, "dk")
    _decl["VSm"] = ("VSm", lambda: [8, 128, 128, 65], BF16, "dk")
    _decl["QT"] = ("QT", lambda: [8, 128, NO], BF16, "dk")
    _decl["QA"] = ("QA", lambda: [8, 128, NO], BF16, "dk")
    _decl["Gs"] = ("Gs", lambda: [16, 128, NO], BF16, "dk")
    _decl["Hs"] = ("Hs", lambda: [NO, D], F32, "dk")
    _decl["ATT"] = ("ATT", lambda: [12, 128, NO], BF16, "dk")
    _decl["wb_brd"] = ("wb_brd", lambda: [1024, D], BF16, "Internal")
    _decl["wb_brm"] = ("wb_brm", lambda: [512, D], BF16, "Internal")
    _decl["wb_out"] = ("wb_out", lambda: [D, D], BF16, "Internal")
    _decl["wb_q"] = ("wb_q", lambda: [D, D], BF16, "Internal")
    _decl["wb_k"] = ("wb_k", lambda: [D, D], BF16, "Internal")
    _decl["wb_v"] = ("wb_v", lambda: [D, D], BF16, "Internal")
    _decl["wb_o"] = ("wb_o", lambda: [D, D], BF16, "Internal")
    _decl["wb_f1"] = ("wb_f1", lambda: [D, 4096], BF16, "Internal")
    _decl["wb_f2"] = ("wb_f2", lambda: [4096, D], BF16, "Internal")
    dk = "ExternalOutput" if dbg else "Internal"
    _made = {}
    used_inputs = []

    def T(var):
        if var not in _made:
            name, shp, dt, kind = _decl[var]
            if kind == "dk":
                kind = dk
            _made[var] = nc.dram_tensor(name, list(shp()), dt, kind=kind).ap()
            if kind == "ExternalInput":
                used_inputs.append(name)
        return _made[var]

    with ExitStack() as es:
        S = Sched(nc, es)
        op, dma = S.op, S.dma
        evq = [0]

        def ev_eng():
            evq[0] += 1
            return "act" if evq[0] % 2 else "dve"

        def sbt(st, name, shape, dt):
            return st.enter_context(nc.sbuf_tensor(name, list(shape), dt))

        def pst(st, name, shape, dt):
            return st.enter_context(nc.psum_tensor(name, list(shape), dt))

        ident = sbt(es, "ident_sb", [128, 128], BF16); identB = S.buf("ident")
        dma("sp", ident[:], T("ident_in"), writes=[identB])
        kmT = sbt(es, "kmT", [128, 4, 64], F32); kmB = S.buf("kmT")
        epsc = sbt(es, "epsc", [128, 1], F32); epsB = S.buf("epsc")
        op("dve", lambda e: e.memset(epsc[:], EPS), writes=[epsB])
        scrB = {n: S.buf(n) for n in ("KT", "VSd", "VSm", "QT", "QA", "Gs", "Hs", "ATT", "out", "wb")}
        for dst, src in () if stop_after == "A1" else ((T("wb_brd"), T("w_br_diff")), (T("wb_brm"), T("w_br_moba")), (T("wb_out"), T("w_out")), (T("wb_q"), T("wq_x")), (T("wb_k"), T("wk_x")),
                         (T("wb_v"), T("wv_x")), (T("wb_o"), T("wo_x"))):
            dma("pool", dst, src, writes=[scrB["wb"]])
        for c in range(0 if stop_after == "A1" else 4):
            dma("pool", T("wb_f1")[c * 256:(c + 1) * 256, :], T("w_ff1")[c * 256:(c + 1) * 256, :], writes=[scrB["wb"]])
            dma("pool", T("wb_f2")[c * 1024:(c + 1) * 1024, :], T("w_ff2")[c * 1024:(c + 1) * 1024, :], writes=[scrB["wb"]])

        if stop_after == "A0":
            S.barrier()
            return nc, S, used_inputs
        def layer_norm(st, xt, xtB, gt, bt, gbB, out32, out32B, outbf, outbfB, nsub=4):
            s1, s2, mu, var, rstd, nmr, junk = st
            sB = s1[1]
            op("dve", lambda e: e.reduce_sum(out=s1[0][:, 0:nsub], in_=xt[:, 0:nsub, :], axis=AX.X), reads=[xtB], writes=[sB])
            for n in range(nsub):
                op("act", lambda e: e.activation(out=junk[0][:], in_=xt[:, n, :], func=AF.Square,
                                                 accum_out=s2[0][:, n:n + 1]), reads=[xtB], writes=[junk[1], s2[1]])
            op("dve", lambda e: e.tensor_scalar_mul(out=mu[0][:, 0:nsub], in0=s1[0][:, 0:nsub], scalar1=1.0 / D), reads=[sB], writes=[mu[1]])
            op("dve", lambda e: e.tensor_tensor(out=var[0][:, 0:nsub], in0=mu[0][:, 0:nsub], in1=mu[0][:, 0:nsub], op=ALU.mult),
               reads=[mu[1]], writes=[var[1]])
            op("dve", lambda e: e.scalar_tensor_tensor(out=var[0][:, 0:nsub], in0=s2[0][:, 0:nsub], scalar=1.0 / D, in1=var[0][:, 0:nsub],
                                                       op0=ALU.mult, op1=ALU.subtract), reads=[s2[1], var[1]], writes=[var[1]])
            op("dve", lambda e: e.tensor_scalar_add(out=var[0][:, 0:nsub], in0=var[0][:, 0:nsub], scalar1=EPS), reads=[var[1]], writes=[var[1]])
            op("act", lambda e: e.activation(out=rstd[0][:, 0:nsub], in_=var[0][:, 0:nsub], func=AF.Sqrt), reads=[var[1]], writes=[rstd[1]])
            op("dve", lambda e: e.reciprocal(out=rstd[0][:, 0:nsub], in_=rstd[0][:, 0:nsub]), reads=[rstd[1]], writes=[rstd[1]])
            op("dve", lambda e: e.scalar_tensor_tensor(out=nmr[0][:, 0:nsub], in0=mu[0][:, 0:nsub], scalar=-1.0, in1=rstd[0][:, 0:nsub],
                                                       op0=ALU.mult, op1=ALU.mult), reads=[mu[1], rstd[1]], writes=[nmr[1]])
            for n in range(nsub):
                op("act", lambda e: e.activation(out=xt[:, n, :], in_=xt[:, n, :], func=AF.Identity,
                                                 bias=nmr[0][:, n:n + 1], scale=rstd[0][:, n:n + 1]),
                   reads=[xtB, nmr[1], rstd[1]], writes=[xtB])
            for n in range(nsub):
                op("dve", lambda e: e.tensor_tensor(out=xt[:, n, :], in0=xt[:, n, :], in1=gt[:], op=ALU.mult),
                   reads=[xtB, gbB], writes=[xtB])
            for n in range(nsub):
                op("pool", lambda e: e.tensor_tensor(out=out32[:, n, :], in0=xt[:, n, :], in1=bt[:], op=ALU.add),
                   reads=[xtB, gbB], writes=[out32B])
            if outbf is not None:
                for n in range(nsub):
                    op(("dve", "pool")[n % 2], lambda e: e.tensor_copy(out=outbf[:, n, :], in_=out32[:, n, :]),
                       reads=[out32B], writes=[outbfB])

        def layer_norm_ap(st, xt, xtB, gt, bt, gbB, outbf, outbfB, epsc, epsB, nsub=4, keep32=False):
            s1, s2, mu, var, rstd, nmr, junk = st
            for n in range(nsub):
                op("act", lambda e: e.activation(out=junk[0][:], in_=xt[:, n, :], func=AF.Identity,
                                                 accum_out=s1[0][:, n:n + 1]), reads=[xtB], writes=[junk[1], s1[1]])
                op("act", lambda e: e.activation(out=junk[0][:], in_=xt[:, n, :], func=AF.Square,
                                                 accum_out=s2[0][:, n:n + 1]), reads=[xtB], writes=[junk[1], s2[1]])
            op("pool", lambda e: e.tensor_scalar_mul(out=mu[0][:, 0:nsub], in0=s1[0][:, 0:nsub], scalar1=1.0 / D), reads=[s1[1]], writes=[mu[1]])
            op("pool", lambda e: e.tensor_tensor(out=var[0][:, 0:nsub], in0=mu[0][:, 0:nsub], in1=mu[0][:, 0:nsub], op=ALU.mult),
               reads=[mu[1]], writes=[var[1]])
            op("pool", lambda e: e.tensor_scalar_mul(out=s2[0][:, 0:nsub], in0=s2[0][:, 0:nsub], scalar1=1.0 / D), reads=[s2[1]], writes=[s2[1]])
            op("pool", lambda e: e.tensor_tensor(out=var[0][:, 0:nsub], in0=s2[0][:, 0:nsub], in1=var[0][:, 0:nsub], op=ALU.subtract),
               reads=[s2[1], var[1]], writes=[var[1]])
            op("act", lambda e: e.activation(out=rstd[0][:, 0:nsub], in_=var[0][:, 0:nsub], func=AF.Ln, bias=epsc[:]), reads=[var[1], epsB], writes=[rstd[1]])
            op("act", lambda e: e.activation(out=rstd[0][:, 0:nsub], in_=rstd[0][:, 0:nsub], func=AF.Exp, scale=-0.5), reads=[rstd[1]], writes=[rstd[1]])
            op("pool", lambda e: e.tensor_tensor(out=nmr[0][:, 0:nsub], in0=mu[0][:, 0:nsub], in1=rstd[0][:, 0:nsub], op=ALU.mult),
               reads=[mu[1], rstd[1]], writes=[nmr[1]])
            op("pool", lambda e: e.tensor_scalar_mul(out=nmr[0][:, 0:nsub], in0=nmr[0][:, 0:nsub], scalar1=-1.0), reads=[nmr[1]], writes=[nmr[1]])
            for n in range(nsub):
                op("act", lambda e: e.activation(out=xt[:, n, :], in_=xt[:, n, :], func=AF.Identity,
                                                 bias=nmr[0][:, n:n + 1], scale=rstd[0][:, n:n + 1]),
                   reads=[xtB, nmr[1], rstd[1]], writes=[xtB])
                op("pool", lambda e: e.tensor_tensor(out=xt[:, n, :], in0=xt[:, n, :], in1=gt[:], op=ALU.mult),
                   reads=[xtB, gbB], writes=[xtB])
                if not keep32:
                    op("pool", lambda e: e.tensor_tensor(out=outbf[:, n, :], in0=xt[:, n, :], in1=bt[:], op=ALU.add),
                       reads=[xtB, gbB], writes=[outbfB])
                else:
                    op("pool", lambda e: e.tensor_tensor(out=xt[:, n, :], in0=xt[:, n, :], in1=bt[:], op=ALU.add),
                       reads=[xtB, gbB], writes=[xtB])
                    op("act", lambda e: e.copy(out=outbf[:, n, :], in_=xt[:, n, :]), reads=[xtB], writes=[outbfB])

        def ln_state(st_, pfx):
            res = []
            for nm in ("s1", "s2", "mu", "var", "rstd", "nmr"):
                res.append((sbt(st_, pfx + nm, [128, 4], F32), S.buf(pfx + nm)))
            res.append((sbt(st_, pfx + "junk", [128, 1024], F32), S.buf(pfx + "junk")))
            return res

        def transpose_to(hb, hbB, hT, hTB, tps, tpB, tcnt, nsub=4, eng=None):
            for c in range(8):
                i = tcnt[0] % len(tps); tcnt[0] += 1
                for n in range(nsub):
                    op("pe", lambda e: e.transpose(out=tps[i][:, n * 128:(n + 1) * 128], in_=hb[:, n, c * 128:(c + 1) * 128],
                                                   identity=ident[:]),
                       reads=[hbB, identB], writes=[tpB[i]], sig=(n == nsub - 1))
                op(eng or ev_eng(), lambda e: cp(e, hT[:, c, 0:nsub * 128], tps[i][:, 0:nsub * 128]),
                   reads=[tpB[i]], writes=[hTB])

        def cp(e, out, in_):
            return e.copy(out=out, in_=in_) if e is nc.scalar else e.tensor_copy(out=out, in_=in_)

        with ExitStack() as ph:
            gt = sbt(ph, "gt", [128, D], F32); bt = sbt(ph, "bt", [128, D], F32); gbB = S.buf("gb")
            dma("sp", gt[:], T("ln_in_g").partition_broadcast(128), writes=[gbB])
            dma("sp", bt[:], T("ln_in_b").partition_broadcast(128), writes=[gbB])
            Wk = sbt(ph, "Wk", [128, 8, 1536], BF16); WkB = S.buf("Wk")
            Wv = sbt(ph, "Wv", [128, 8, 1536], BF16); WvB = S.buf("Wv")
            for c in range(8):
                rows = slice(c * 128, (c + 1) * 128)
                dma("pool", Wk[:, c, 0:1024], T("w_in")[rows, C_DK:C_DK + 1024], writes=[WkB])
                dma("pool", Wk[:, c, 1024:1536], T("w_in")[rows, C_MK:C_MK + 512], writes=[WkB])
                dma("pool", Wv[:, c, 0:1024], T("w_in")[rows, C_DV:C_DV + 1024], writes=[WvB])
                dma("pool", Wv[:, c, 1024:1536], T("w_in")[rows, C_MV:C_MV + 512], writes=[WvB])
            xts = [sbt(ph, f"xt{i}", [128, 4, D], F32) for i in range(2)]; xtBs = S.bufs("xt", 2)
            hbs = [sbt(ph, f"hb{i}", [128, 4, D], BF16) for i in range(3)]; hbBs = S.bufs("hb", 3)
            hTs = [sbt(ph, f"hT{i}", [128, 8, 512], BF16) for i in range(2)]; hTBs = S.bufs("hT", 2)
            kst = sbt(ph, "kst", [128, 12, 512], BF16); kstB = S.buf("kst")
            vstd = sbt(ph, "vstd", [128, 8, 4, 129], BF16); vstdB = S.buf("vstd")
            vstm = sbt(ph, "vstm", [128, 8, 4, 65], BF16); vstmB = S.buf("vstm")
            lst = ln_state(ph, "a_")
            tps = [pst(ph, f"tpa{i}", [128, 1024], BF16) for i in range(2)]; tpB = S.bufs("tpa", 2)
            mps = [pst(ph, f"mpa{i}", [128, 512], F32) for i in range(4)]; mpB = S.bufs("mpa", 4)
            op("dve", lambda e: e.memset(vstd[:, :, :, 128:129], 1.0), writes=[vstdB])
            op("dve", lambda e: e.memset(vstm[:, :, :, 64:65], 1.0), writes=[vstmB])
            tcnt = [0]; mcnt = [0]
            nta = 32 if stop_after != "A1" else 1
            def ln_a(Tt):
                xt, xtB = xts[Tt % 2], xtBs[Tt % 2]
                dma("sp", xt[:], T("x_b")[Tt * 512:(Tt + 1) * 512, :].rearrange("(n p) d -> p n d", p=128), writes=[xtB])
                layer_norm_ap(lst, xt, xtB, gt, bt, gbB, hbs[Tt % 3], hbBs[Tt % 3], epsc, epsB)

            def tr_a(Tt):
                transpose_to(hbs[Tt % 3], hbBs[Tt % 3], hTs[Tt % 2], hTBs[Tt % 2], tps, tpB, tcnt, eng="dve")

            ln_a(0)
            if nta > 1:
                ln_a(1)
            tr_a(0)
            for Tt in range(nta):
                if Tt + 2 < nta:
                    ln_a(Tt + 2)
                if Tt + 1 < nta:
                    tr_a(Tt + 1)
                hT, hTB = hTs[Tt % 2], hTBs[Tt % 2]
                for g in range(12):
                    i = mcnt[0] % 4; mcnt[0] += 1
                    for c in range(8):
                        op("pe", lambda e: e.matmul(mps[i][:], lhsT=Wk[:, c, g * 128:(g + 1) * 128], rhs=hT[:, c, :],
                                                    start=(c == 0), stop=(c == 7)),
                           reads=[WkB, hTB], writes=[mpB[i]], sig=(c == 7))
                    op("dve", lambda e: cp(e, kst[:, g, :], mps[i][:]), reads=[mpB[i]], writes=[kstB])
                    if g >= 8:
                        op("dve", lambda e: e.reduce_sum(out=kmT[:, g - 8, 2 * Tt:2 * Tt + 2],
                                                         in_=mps[i][:].rearrange("p (a b) -> p a b", b=256), axis=AX.X),
                           reads=[mpB[i]], writes=[kmB])
                for g4 in range(3):
                    dma("sp", T("KT")[4 * g4:4 * g4 + 4, :, Tt * 512:(Tt + 1) * 512].rearrange("g p t -> p g t"), kst[:, 4 * g4:4 * g4 + 4, :],
                        reads=[kstB], writes=[scrB["KT"]])
                for n in range(4):
                    for cg in range(3):
                        i = mcnt[0] % 4; mcnt[0] += 1
                        for c in range(8):
                            op("pe", lambda e: e.matmul(mps[i][:], lhsT=hT[:, c, n * 128:(n + 1) * 128],
                                                        rhs=Wv[:, c, cg * 512:(cg + 1) * 512], start=(c == 0), stop=(c == 7)),
                               reads=[WvB, hTB], writes=[mpB[i]], sig=(c == 7))
                        if cg < 2:
                            op("dve", lambda e: cp(e, vstd[:, 4 * cg:4 * cg + 4, n, 0:128],
                                                      mps[i][:].rearrange("p (h d) -> p h d", d=128)),
                               reads=[mpB[i]], writes=[vstdB])
                        else:
                            op("dve", lambda e: cp(e, vstm[:, :, n, 0:64], mps[i][:].rearrange("p (h d) -> p h d", d=64)),
                               reads=[mpB[i]], writes=[vstmB])
                for h in range(8):
                    dma("sp", T("VSd")[h, :, 4 * Tt:4 * Tt + 4, :], vstd[:, h, :, :], reads=[vstdB], writes=[scrB["VSd"]])
                    dma("sp", T("VSm")[h, :, 4 * Tt:4 * Tt + 4, :], vstm[:, h, :, :], reads=[vstmB], writes=[scrB["VSm"]])
            S.barrier()
        if stop_after in ("A", "A1"):
            return nc, S, used_inputs

        with ExitStack() as ph:
            gt = sbt(ph, "gtb", [128, D], F32); bt = sbt(ph, "btb", [128, D], F32); gbB = S.buf("gbb")
            dma("sp", gt[:], T("ln_in_g").partition_broadcast(128), writes=[gbB])
            dma("sp", bt[:], T("ln_in_b").partition_broadcast(128), writes=[gbB])
            Wq = sbt(ph, "Wq", [128, 8, 1536], BF16); WqB = S.buf("Wq")
            Wg = sbt(ph, "Wg", [128, 8, 2048], BF16); WgB = S.buf("Wg")
            for c in range(8):
                rows = slice(c * 128, (c + 1) * 128)
                dma("pool", Wq[:, c, 0:1024], T("w_in")[rows, C_DQ:C_DQ + 1024], writes=[WqB])
                dma("pool", Wq[:, c, 1024:1536], T("w_in")[rows, C_MQ:C_MQ + 512], writes=[WqB])
                dma("pool", Wg[:, c, :], T("w_in")[rows, C_G:C_G + 2048], writes=[WgB])
            bgT = sbt(ph, "bgT", [128, 16], F32); bgB = S.buf("bgT")
            dma("sp", bgT[:], T("b_gate").rearrange("(g p) -> p g", p=128), writes=[bgB], allow_slow_non_contiguous=True)
            kml = sbt(ph, "kml", [128, 4, 64], F32); kmh = sbt(ph, "kmh", [128, 4, 64], F32); kmlB = S.buf("kml")
            op("dve", lambda e: e.memset(kml[:], 0.0), writes=[kmlB])
            op("dve", lambda e: e.memset(kmh[:], 0.0), writes=[kmlB])
            op("dve", lambda e: e.tensor_scalar_mul(out=kml[0:64, :, :], in0=kmT[0:64, :, :], scalar1=1.0 / 256), reads=[kmB], writes=[kmlB])
            op("dve", lambda e: e.tensor_scalar_mul(out=kmh[64:128, :, :], in0=kmT[64:128, :, :], scalar1=1.0 / 256), reads=[kmB], writes=[kmlB])
            xts = [sbt(ph, f"xtb{i}", [128, 4, D], F32) for i in range(2)]; xtBs = S.bufs("xtb", 2)
            hbs = [sbt(ph, f"hbq{i}", [128, 4, D], BF16) for i in range(3)]; hbBs = S.bufs("hbb", 3)
            hTs = [sbt(ph, f"hTb{i}", [128, 8, 512], BF16) for i in range(2)]; hTBs = S.bufs("hTb", 2)
            qst = sbt(ph, "qst", [128, 8, 512], BF16); qstB = S.buf("qst")
            qf = sbt(ph, "qf", [128, 4, 512], F32); qfB = S.buf("qf")
            qa = sbt(ph, "qa", [128, 8, 512], BF16); qaB = S.buf("qa")
            gst = sbt(ph, "gst", [128, 16, 512], BF16); gstB = S.buf("gst")
            pbt = sbt(ph, "pbt", [128, 4, 64], F32); pvt = sbt(ph, "pvt", [128, 4, 64], F32); ott = sbt(ph, "ott", [128, 4, 64], F32)
            pcB = S.buf("pc")
            gm = sbt(ph, "gm", [128, 64], F32); gmB = S.buf("gm")
            m8 = sbt(ph, "m8", [128, 8], F32); m8B = S.buf("m8")
            sel = sbt(ph, "sel", [128, 64], F32); selB = S.buf("sel")
            mpad = sbt(ph, "mpad", [128, 4, 128], BF16); mpadB = S.buf("mpad")
            op("dve", lambda e: e.memset(mpad[:], 0.0), writes=[mpadB])
            lst = ln_state(ph, "b_")
            tps = [pst(ph, f"tpb{i}", [128, 1024], BF16) for i in range(2)]; tpB = S.bufs("tpb", 2)
            mps = [pst(ph, f"mpb{i}", [128, 512], F32) for i in range(4)]; mpB = S.bufs("mpb", 4)
            gps = pst(ph, "gps", [128, 512], F32); gpsB = S.buf("gps")
            tcnt = [0]; mcnt = [0]
            def ln_b(r):
                xt, xtB = xts[r % 2], xtBs[r % 2]
                tsl = slice(r * 512, (r + 1) * 512)
                dma("sp", xt[:], T("x_own")[tsl, :].rearrange("(n p) d -> p n d", p=128), writes=[xtB])
                layer_norm_ap(lst, xt, xtB, gt, bt, gbB, hbs[r % 3], hbBs[r % 3], epsc, epsB, keep32=True)
                dma("sp", T("Hs")[tsl, :].rearrange("(n p) d -> p n d", p=128), xt[:], reads=[xtB], writes=[scrB["Hs"]])

            def tr_b(r):
                transpose_to(hbs[r % 3], hbBs[r % 3], hTs[r % 2], hTBs[r % 2], tps, tpB, tcnt, eng="dve")

            ln_b(0); ln_b(1); tr_b(0)
            for r in range(8):
                if r + 2 < 8:
                    ln_b(r + 2)
                if r + 1 < 8:
                    tr_b(r + 1)
                hT, hTB = hTs[r % 2], hTBs[r % 2]
                tsl = slice(r * 512, (r + 1) * 512)
                dma("sp", pbt[:], T("pastbias")[r].rearrange("s p n -> p s n"), writes=[pcB])
                dma("sp", pvt[:], T("pastvalid")[r].rearrange("s p n -> p s n"), writes=[pcB])
                dma("sp", ott[:], T("ownterm")[r].rearrange("s p n -> p s n"), writes=[pcB])

                def proj(col0, M, dst_fn, eng=None):
                    i = mcnt[0] % 4; mcnt[0] += 1
                    for c in range(8):
                        op("pe", lambda e: e.matmul(mps[i][0:M, :], lhsT=W_[:, c, col0:col0 + M], rhs=hT[:, c, :],
                                                    start=(c == 0), stop=(c == 7)),
                           reads=[W_B, hTB], writes=[mpB[i]], sig=(c == 7))
                    dst_fn(mps[i], mpB[i])

                W_, W_B = Wq, WqB
                for g in range(8):
                    proj(g * 128, 128, lambda ps, psB: op("dve", lambda e: e.tensor_scalar_mul(out=qst[:, g, :], in0=ps[:], scalar1=0.125),
                                                          reads=[psB], writes=[qstB]))
                dma("sp", T("QT")[:, :, tsl].rearrange("g p t -> p g t"), qst[:], reads=[qstB], writes=[scrB["QT"]])
                for i4 in range(4):
                    proj(1024 + i4 * 128, 128, lambda ps, psB: op("act", lambda e: e.copy(out=qf[:, i4, :], in_=ps[:]),
                                                                  reads=[psB], writes=[qfB]))
                for h in range(8):
                    proj(1024 + h * 64, 64, lambda ps, psB: op("dve", lambda e: e.tensor_scalar_mul(out=qa[0:64, h, :], in0=ps[0:64, :], scalar1=0.125),
                                                               reads=[psB], writes=[qaB]))
                for h in range(8):
                    km = kml if h % 2 == 0 else kmh
                    for s in range(4):
                        op("pe", lambda e: e.matmul(gps[:, s * 64:(s + 1) * 64], lhsT=qf[:, h // 2, s * 128:(s + 1) * 128],
                                                    rhs=km[:, h // 2, :], start=True, stop=True),
                           reads=[qfB, kmlB], writes=[gpsB])
                    for s in range(4):
                        op("dve", lambda e: e.tensor_tensor(out=gm[:], in0=gps[:, s * 64:(s + 1) * 64], in1=pbt[:, s, :], op=ALU.add),
                           reads=[gpsB, pcB], writes=[gmB])
                        op("dve", lambda e: e.max(out=m8[:], in_=gm[:]), reads=[gmB], writes=[m8B])
                        op("dve", lambda e: e.scalar_tensor_tensor(out=sel[:], in0=gm[:], scalar=m8[:, 2:3], in1=pvt[:, s, :],
                                                                   op0=ALU.is_ge, op1=ALU.mult), reads=[gmB, m8B, pcB], writes=[selB])
                        op("dve", lambda e: e.scalar_tensor_tensor(out=mpad[:, s, 64:128], in0=sel[:], scalar=-NEG, in1=ott[:, s, :],
                                                                   op0=ALU.mult, op1=ALU.add), reads=[selB, pcB], writes=[mpadB])
                    i = tcnt[0] % 2; tcnt[0] += 1
                    for s in range(4):
                        op("pe", lambda e: e.transpose(out=tps[i][:, s * 128:(s + 1) * 128], in_=mpad[:, s, :], identity=ident[:]),
                           reads=[mpadB, identB], writes=[tpB[i]], sig=(s == 3))
                    op("act", lambda e: e.copy(out=qa[64:128, h, :], in_=tps[i][64:128, 0:512]), reads=[tpB[i]], writes=[qaB])
                dma("sp", T("QA")[:, :, tsl].rearrange("h p t -> p h t"), qa[:], reads=[qaB], writes=[scrB["QA"]])
                W_, W_B = Wg, WgB
                for gg in range(16):
                    proj(gg * 128, 128, lambda ps, psB: op("act", lambda e: e.activation(out=gst[:, gg, :], in_=ps[:], func=AF.Sigmoid,
                                                                                         bias=bgT[:, gg:gg + 1]),
                                                           reads=[psB, bgB], writes=[gstB]))
                dma("sp", T("Gs")[:, :, tsl].rearrange("g p t -> p g t"), gst[:], reads=[gstB], writes=[scrB["Gs"]])
            S.barrier()
        if stop_after == "B":
            return nc, S, used_inputs

        with ExitStack() as ph:
            KTb = sbt(ph, "KTb", [128, NT], BF16); KTB = S.bufs("KTc", 8)
            VTb = sbt(ph, "VTb", [128, 128, 129], BF16); VTB = S.bufs("VTc", 8)
            Bb = [sbt(ph, f"Bb{i}", [128, 17, 512], BF16) for i in range(2)]; BB = S.bufs("Bb", 2)
            Qb = [sbt(ph, f"Qb{i}", [128, NO], BF16) for i in range(2)]; QB = S.bufs("Qb", 2)
            chc = [sbt(ph, f"chc{i}", [128, 1], F32) for i in range(2)]; chB = S.bufs("chc", 2)
            Btmp = [sbt(ph, f"Btmp{i}", [128, 4, 512], F32) for i in range(2)]; BtB = S.bufs("Btmp", 2); bcnt = [0]
            zc = sbt(ph, "zc", [128, 1], F32); zcB = S.buf("zc")
            op("dve", lambda e: e.memset(zc[:], 0.0), writes=[zcB])
            PT = [sbt(ph, f"PT{i}", [128, 1024], BF16) for i in range(3)]; PTB = S.bufs("PT", 3)
            lq = sbt(ph, "lq", [128, 4, 64], F32); lqB = S.buf("lq")
            for i, a in enumerate((T("lam_q1"), T("lam_k1"), T("lam_q2"), T("lam_k2"))):
                dma("sp", lq[:, i, :], a.partition_broadcast(128), writes=[lqB])
            lw = sbt(ph, "lw", [128, 2, 64], F32); ls = sbt(ph, "ls", [128, 4], F32); lwB = S.buf("lw")
            op("dve", lambda e: e.tensor_tensor(out=lw[:, 0, :], in0=lq[:, 0, :], in1=lq[:, 1, :], op=ALU.mult), reads=[lqB], writes=[lwB])
            op("dve", lambda e: e.tensor_tensor(out=lw[:, 1, :], in0=lq[:, 2, :], in1=lq[:, 3, :], op=ALU.mult), reads=[lqB], writes=[lwB])
            op("dve", lambda e: e.reduce_sum(out=ls[:, 0:2], in_=lw[:], axis=AX.X), reads=[lwB], writes=[lwB])
            op("act", lambda e: e.activation(out=ls[:, 0:2], in_=ls[:, 0:2], func=AF.Exp), reads=[lwB], writes=[lwB])
            op("dve", lambda e: e.tensor_tensor(out=ls[:, 2:3], in0=ls[:, 1:2], in1=ls[:, 0:1], op=ALU.subtract), reads=[lwB], writes=[lwB])
            op("dve", lambda e: e.tensor_scalar_add(out=ls[:, 2:3], in0=ls[:, 2:3], scalar1=-LAM_INIT), reads=[lwB], writes=[lwB])
            nlam = ls[:, 2:3]
            sg = sbt(ph, "sg", [128, 128], F32); sgB = S.buf("sg")
            dma("sp", sg[:], T("diff_sub_g").partition_broadcast(128), writes=[sgB])
            op("dve", lambda e: e.tensor_scalar_mul(out=sg[:], in0=sg[:], scalar1=1.0 - LAM_INIT), reads=[sgB], writes=[sgB])
            rc = sbt(ph, "rc", [128, 8], F32); rcB = S.buf("rc")
            oa = sbt(ph, "oa", [128, 128], F32); oaB = S.buf("oa")
            od = sbt(ph, "od", [128, 128], F32); odB = S.buf("od")
            ojk = sbt(ph, "ojk", [128, 128], F32); ojB = S.buf("ojk")
            ob = sbt(ph, "ob", [128, 4, 128], BF16); obB = S.buf("ob")
            osb = sbt(ph, "osb", [128, 3, 387], F32); osbB = S.buf("osb")
            od4 = sbt(ph, "od4", [128, 4, 128], F32)
            rq = sbt(ph, "rq", [128, 16], F32); rqB = S.buf("rq")
            ast = [sbt(ph, f"ast{i}", [128, 512], BF16) for i in range(2)]; astB = S.bufs("ast", 2)
            Sps = [pst(ph, f"Sps{i}", [128, 1024], F32) for i in range(2)]; SpB = S.bufs("Sps", 2)
            Ops = [pst(ph, f"Ops{i}", [128, 512], F32) for i in range(3)]; OpB = S.buf("Ops")
            tpc = pst(ph, "tpc", [128, 1024], BF16); tpcB = S.buf("tpc")
            scnt = [0]; pcnt = [0]; acnt = [0]
            heads = range(16) if stop_after != "C1" else [0, 8]
            nslots = 8 if stop_after != "C1" else 2
            for hi, hh in enumerate(heads):
                moba = hh >= 8
                hm = hh - 8
                par = hi % 2
                dma("sp", chc[par][:], T("rel_table")[31, hh:hh + 1].partition_broadcast(128), writes=[chB[par]])
                if not moba:
                    dma("sp", Qb[par][:], T("QT")[hh], reads=[scrB["QT"]], writes=[QB[par]])
                else:
                    dma("sp", Qb[par][:], T("QA")[hm], reads=[scrB["QA"]], writes=[QB[par]])
                for ui, u0 in enumerate(range(0, 17, 4)):
                    u1 = min(17, u0 + 4)
                    src = bass.AP(tensor=T("t2").tensor, offset=T("t2")[hh, 0, u0 * 128].offset, ap=[[2560, 128], [128, u1 - u0], [1, 512]])
                    bi = bcnt[0] % 2; bcnt[0] += 1
                    dma("sp", Btmp[bi][:, 0:u1 - u0, :], src, writes=[BtB[bi]])
                    op("dve", lambda e: e.tensor_scalar(out=Bb[par][:, u0:u1, :], in0=Btmp[bi][:, 0:u1 - u0, :], scalar1=chc[par][:, 0:1],
                                                        scalar2=None, op0=ALU.subtract), reads=[BtB[bi], chB[par]], writes=[BB[par]])
                for c in range(nslots):
                    ksl = slice(c * 2048, (c + 1) * 2048)
                    if not moba:
                        dma("sp", KTb[:, ksl], T("KT")[hh, :, ksl], reads=[scrB["KT"]], writes=[KTB[c]])
                        dma("sp", VTb[:, 16 * c:16 * c + 16, :], T("VSd")[hh, :, 16 * c:16 * c + 16, :], reads=[scrB["VSd"]], writes=[VTB[c]])
                    else:
                        dma("sp", KTb[0:64, ksl], T("KT")[8 + hm // 2, (hm % 2) * 64:(hm % 2) * 64 + 64, ksl], reads=[scrB["KT"]], writes=[KTB[c]])
                        dma("sp", KTb[64:128, ksl], T("eind")[:, ksl], writes=[KTB[c]])
                        dma("sp", VTb[:, 16 * c:16 * c + 16, 0:65], T("VSm")[hm, :, 16 * c:16 * c + 16, :], reads=[scrB["VSm"]], writes=[VTB[c]])
                nm = 1 if moba else 2
                vw = 65 if moba else 129
                dv = vw - 1
                def acc_ap(a):
                    return Ops[a // 3][:, (a % 3) * 129:(a % 3) * 129 + vw]

                def osb_ap(a):
                    return osb[:, a // 3, (a % 3) * 129:(a % 3) * 129 + vw]

                def parts_of(r, st):
                    if not moba:
                        return [(st, slice(0, 64), 0), (st, slice(64, 128), 4)]
                    return [(2 * st, slice(0, 128), 0), (2 * st + 1, slice(0, 128), 0)]

                def emit_qk(r, st):
                    qsl = slice(r * 512, (r + 1) * 512)
                    si = scnt[0] % 2; scnt[0] += 1
                    Sp = Sps[si]
                    parts = parts_of(r, st)
                    nears = [kt - (16 * r - 1) >= 0 for kt, _, _ in parts]
                    for p, (kt, rows, _) in enumerate(parts):
                        c = kt // 16
                        op("pe", lambda e: e.matmul(Sp[:, p * 512:(p + 1) * 512], lhsT=KTb[rows, kt * 128:(kt + 1) * 128],
                                                    rhs=Qb[par][rows, qsl], start=True, stop=not nears[p]),
                           reads=[KTB[c], QB[par]], writes=[SpB[si]], sig=(p == 1 and not any(nears)))
                    lastn = max([p for p in range(2) if nears[p]], default=-1)
                    for p, (kt, rows, _) in enumerate(parts):
                        if nears[p]:
                            u = 16 - (kt - (16 * r - 1))
                            op("pe", lambda e: e.matmul(Sp[:, p * 512:(p + 1) * 512], lhsT=ident[:], rhs=Bb[par][:, u, :],
                                                        start=False, stop=True),
                               reads=[identB, BB[par]], writes=[SpB[si]], sig=(p == lastn))
                    return (si,)

                def emit_exp_av(r, st, si):
                    nkt = 16 * r + 16
                    Sp = Sps[si]
                    pi = pcnt[0] % 3; pcnt[0] += 1
                    op("act", lambda e: e.activation(out=PT[pi][:, 0:1024], in_=Sp[:, 0:1024], func=AF.Exp, bias=chc[par][:]),
                       reads=[SpB[si], chB[par]], writes=[PTB[pi]])
                    parts = parts_of(r, st)
                    for p, (kt, rows, abase) in enumerate(parts):
                        c = kt // 16
                        for s in range(4):
                            a = abase + s
                            bank = a // 3
                            st_flag = (kt == 0) and (bank not in slot_banks)
                            slot_banks.add(bank)
                            op("pe", lambda e: e.matmul(acc_ap(a), lhsT=PT[pi][:, p * 512 + s * 128:p * 512 + (s + 1) * 128],
                                                        rhs=VTb[:, kt, 0:vw], start=st_flag, stop=(kt == nkt - 1),
                                                        skip_group_check=True),
                               reads=[PTB[pi], VTB[c]], writes=[OpB], sig=(p == 1 and s == 3))

                def emit_epi_dve(r):
                    if not moba:
                        for bk, wdt in ((0, 387), (1, 387), (2, 258)):
                            op("dve", lambda e: e.tensor_copy(out=osb[:, bk, 0:wdt], in_=Ops[bk][:, 0:wdt]), reads=[OpB], writes=[osbB])
                    else:
                        for a in range(4):
                            op("dve", lambda e: e.tensor_copy(out=osb_ap(a), in_=acc_ap(a)), reads=[OpB], writes=[osbB])
                    if not moba:
                        for s in range(4):
                            a0, a1 = osb_ap(s), osb_ap(4 + s)
                            op("dve", lambda e: e.reciprocal(out=rc[:, 0:1], in_=a0[:, 128:129]), reads=[osbB], writes=[rcB])
                            op("dve", lambda e: e.reciprocal(out=rc[:, 1:2], in_=a1[:, 128:129]), reads=[osbB], writes=[rcB])
                            op("dve", lambda e: e.tensor_tensor(out=rc[:, 2:3], in0=rc[:, 1:2], in1=nlam, op=ALU.mult), reads=[rcB, lwB], writes=[rcB])
                            op("dve", lambda e: e.tensor_scalar_mul(out=oa[:], in0=a0[:, 0:128], scalar1=rc[:, 0:1]), reads=[osbB, rcB], writes=[oaB])
                            op("dve", lambda e: e.scalar_tensor_tensor(out=od4[:, s, :], in0=a1[:, 0:128], scalar=rc[:, 2:3], in1=oa[:],
                                                                       op0=ALU.mult, op1=ALU.add), reads=[osbB, rcB, oaB], writes=[odB])
                            op("dve", lambda e: e.tensor_tensor(out=ojk[:], in0=od4[:, s, :], in1=od4[:, s, :], op=ALU.mult), reads=[odB], writes=[ojB])
                            op("dve", lambda e: e.reduce_sum(out=rq[:, s:s + 1], in_=ojk[:], axis=AX.X), reads=[ojB], writes=[rqB])
                    else:
                        for s in range(4):
                            a0 = osb_ap(s)
                            op("dve", lambda e: e.reciprocal(out=rc[:, 0:1], in_=a0[:, 64:65]), reads=[osbB], writes=[rcB])
                            op("dve", lambda e: e.tensor_scalar_mul(out=ob[:, s, 0:64], in0=a0[:, 0:64], scalar1=rc[:, 0:1]),
                               reads=[osbB, rcB], writes=[obB])

                def emit_epi_tail(r):
                    qsl = slice(r * 512, (r + 1) * 512)
                    ai = acnt[0] % 2; acnt[0] += 1
                    if not moba:
                        op("dve", lambda e: e.tensor_scalar(out=rq[:, 4:8], in0=rq[:, 0:4], scalar1=1.0 / 128, scalar2=EPS,
                                                            op0=ALU.mult, op1=ALU.add), reads=[rqB], writes=[rqB])
                        op("act", lambda e: e.activation(out=rq[:, 8:12], in_=rq[:, 4:8], func=AF.Sqrt), reads=[rqB], writes=[rqB])
                        op("dve", lambda e: e.reciprocal(out=rq[:, 12:16], in_=rq[:, 8:12]), reads=[rqB], writes=[rqB])
                        for s in range(4):
                            op("dve", lambda e: e.scalar_tensor_tensor(out=ob[:, s, :], in0=od4[:, s, :], scalar=rq[:, 12 + s:13 + s], in1=sg[:],
                                                                       op0=ALU.mult, op1=ALU.mult), reads=[odB, rqB, sgB], writes=[obB])
                    for s in range(4):
                        op("pe", lambda e: e.transpose(out=tpc[0:dv, s * 128:(s + 1) * 128], in_=ob[:, s, 0:dv], identity=ident[:]),
                           reads=[obB, identB], writes=[tpcB], sig=(s == 3))
                    op("dve", lambda e: e.tensor_copy(out=ast[ai][0:dv, :], in_=tpc[0:dv, 0:512]), reads=[tpcB], writes=[astB[ai]])
                    if not moba:
                        dma("pool", T("ATT")[hh, :, qsl], ast[ai][:], reads=[astB[ai]], writes=[scrB["ATT"]])
                    else:
                        dma("pool", T("ATT")[8 + hm // 2, (hm % 2) * 64:(hm % 2) * 64 + 64, qsl], ast[ai][0:64, :], reads=[astB[ai]],
                            writes=[scrB["ATT"]])

                spk = 2 if moba else 1
                flat = [(r, st) for r in range(nslots) for st in range((16 * r + 16) // spk)]
                slot_banks = set()
                nxt = emit_qk(*flat[0])
                deferred = None
                for idx, (r, st) in enumerate(flat):
                    cur = nxt
                    if idx + 1 < len(flat):
                        nxt = emit_qk(*flat[idx + 1])
                    if st == 0:
                        slot_banks.clear()
                    emit_exp_av(r, st, *cur)
                    if deferred is not None and deferred[1] == idx:
                        emit_epi_tail(deferred[0]); deferred = None
                    if st == (16 * r + 16) // spk - 1:
                        emit_epi_dve(r)
                        if idx + 1 < len(flat):
                            deferred = (r, idx + 10)
                        else:
                            emit_epi_tail(r)
            S.barrier()
        if stop_after in ("C", "C1"):
            return nc, S, used_inputs

        with ExitStack() as ph:
            lng = [sbt(ph, f"lng{i}", [128, D], F32) for i in range(6)]; lngB = S.buf("lng")
            for i, a in enumerate((T("ln1_g"), T("ln1_b"), T("ln2_g"), T("ln2_b"), T("ln3_g"), T("ln3_b"))):
                dma("sp", lng[i][:], a.partition_broadcast(128), writes=[lngB])
            ones = sbt(ph, "ones", [128, 128], BF16); onesB = S.buf("ones")
            op("dve", lambda e: e.memset(ones[:], 1.0), writes=[onesB])
            NW = 4
            wr = [sbt(ph, f"wr{i}", [128, 4096], BF16) for i in range(NW)]; wrB = S.bufs("wr", NW)
            wcnt = [0]

            def wload(src, kc, col0, ncol, row0=0):
                i = wcnt[0] % NW; wcnt[0] += 1
                v = wr[i][:, 0:kc * ncol].rearrange("p (c n) -> p c n", n=ncol)
                dma("sp", v, src[row0:row0 + kc * 128, col0:col0 + ncol].rearrange("(c p) n -> p c n", p=128),
                    reads=[scrB["wb"]], writes=[wrB[i]])
                return v, wrB[i]

            kxT = sbt(ph, "kxT", [128, 8, 256], BF16); kxB = S.buf("kxT")
            vx = sbt(ph, "vx", [128, 2, D], BF16); vxB = S.buf("vx")
            lst = ln_state(ph, "d_")
            ph2 = ExitStack()
            memf = sbt(ph2, "memf", [128, 2, D], F32); memB = S.buf("memf")
            memb = sbt(ph2, "memb", [128, 2, D], BF16); membB = S.buf("memb")
            memT = sbt(ph2, "memT", [128, 8, 256], BF16); memTB = S.buf("memT")
            tps = [pst(ph, f"tpd{i}", [128, 1024], BF16) for i in range(2)]; tpB = S.bufs("tpd", 2)
            mps = [pst(ph, f"mpd{i}", [128, 512], F32) for i in range(6)]; mpB = S.bufs("mpd", 6)
            tcnt = [0]; mcnt = [0]

            def nextps():
                i = mcnt[0] % 2; mcnt[0] += 1
                return mps[i], mpB[i]

            dma("sp", memf[:], T("mem_b").rearrange("(n p) d -> p n d", p=128), writes=[memB])
            op("dve", lambda e: e.tensor_copy(out=memb[:], in_=memf[:]), reads=[memB], writes=[membB])
            transpose_to(memb, membB, memT, memTB, tps, tpB, tcnt, nsub=2)
            for half in range(2):
                wv_, wB_ = wload(T("wb_k"), 8, half * 512, 512)
                for g4 in range(4):
                    ps, psB = nextps()
                    for c in range(8):
                        op("pe", lambda e: e.matmul(ps[:, 0:256], lhsT=wv_[:, c, g4 * 128:(g4 + 1) * 128], rhs=memT[:, c, :],
                                                    start=(c == 0), stop=(c == 7)), reads=[wB_, memTB], writes=[psB], sig=(c == 7))
                    op(ev_eng(), lambda e: cp(e, kxT[:, half * 4 + g4, :], ps[:, 0:256]), reads=[psB], writes=[kxB])
            for half in range(2):
                wv_, wB_ = wload(T("wb_v"), 8, half * 512, 512)
                for mt in range(2):
                    ps, psB = nextps()
                    for c in range(8):
                        op("pe", lambda e: e.matmul(ps[:], lhsT=memT[:, c, mt * 128:(mt + 1) * 128], rhs=wv_[:, c, :],
                                                    start=(c == 0), stop=(c == 7)), reads=[wB_, memTB], writes=[psB], sig=(c == 7))
                    op(ev_eng(), lambda e: cp(e, vx[:, mt, half * 512:(half + 1) * 512], ps[:]), reads=[psB], writes=[vxB])

            S.barrier()
            ph2.close()
            attT = sbt(ph, "attT", [128, 12, 512], BF16); attB = S.buf("attT")
            gT = sbt(ph, "gT", [128, 16, 512], BF16); gTB = S.buf("gT")
            ydT = sbt(ph, "ydT", [128, 8, 512], BF16); ydB = S.buf("ydT")
            mixT = sbt(ph, "mixT", [128, 8, 512], BF16); mixB = S.buf("mixT")
            tmpb = sbt(ph, "tmpb", [128, 512], BF16); tmpB = S.buf("tmpb")
            ha = sbt(ph, "ha", [128, 4, D], F32); haB = S.buf("ha")
            hbf = sbt(ph, "hbf", [128, 4, D], F32); hbfB = S.buf("hbf")
            hbb = sbt(ph, "hbb2", [128, 4, D], BF16); hbbB = S.buf("hbb2")
            hT = sbt(ph, "hTd", [128, 8, 512], BF16); hTB = S.buf("hTd")
            qxT, qxB = ydT, ydB
            pTs = [sbt(ph, f"pTx{i}", [128, 512], BF16) for i in range(2)]; pTB = S.bufs("pTx", 2)
            rs = sbt(ph, "rs", [128, 512], F32); rsB = S.buf("rs")
            oT, oTB = mixT, mixB
            rl = sbt(ph, "rl", [128, 512], BF16); rlB = S.buf("rl")
            actT = sbt(ph, "actT", [128, 32, 512], BF16); actB = S.buf("actT")
            def fm_proj(wsrc, kc, ncolgroups, src, srcB, evac, row0=0, gpw=None):
                gpw = gpw or (4096 // (kc * 128))
                for g0 in range(0, ncolgroups, gpw):
                    n = min(gpw, ncolgroups - g0)
                    wv_, wB_ = wload(wsrc, kc, g0 * 128, n * 128, row0=row0)
                    for gi in range(n):
                        ps, psB = nextps()
                        for c in range(kc):
                            op("pe", lambda e: e.matmul(ps[:], lhsT=wv_[:, c, gi * 128:(gi + 1) * 128], rhs=src[:, c, :],
                                                        start=(c == 0), stop=(c == kc - 1)), reads=[wB_, srcB], writes=[psB], sig=(c == kc - 1))
                        evac(g0 + gi, ps, psB)

            def tm_proj(wsrc, nk, src, srcB, evac):
                for half in range(2):
                    for kg in range(nk // 8):
                        wv_, wB_ = wload(wsrc, 8, half * 512, 512, row0=kg * 1024)
                        for sub in range(4):
                            for c in range(8):
                                k = kg * 8 + c
                                op("pe", lambda e: e.matmul(mps[2 + sub][:], lhsT=src[:, k, sub * 128:(sub + 1) * 128], rhs=wv_[:, c, :],
                                                            start=(k == 0), stop=(k == nk - 1)),
                                   reads=[wB_, srcB], writes=[mpB[2 + sub]], sig=(c == 7))
                    for sub in range(4):
                        evac(half, sub, mps[2 + sub], mpB[2 + sub])

            def resid_evac(hin, hinB, hout, houtB):
                def f(half, sub, ps, psB):
                    cs = slice(half * 512, (half + 1) * 512)
                    op("dve", lambda e: e.scalar_tensor_tensor(out=hout[:, sub, cs], in0=hin[:, sub, cs], scalar=ALPHA, in1=ps[:],
                                                               op0=ALU.mult, op1=ALU.add), reads=[hinB, psB], writes=[houtB])
                return f

            ntile = 8 if stop_after != "D1" else 1
            for t in range(ntile):
                tsl = slice(t * 512, (t + 1) * 512)
                dma("sp", attT[:], T("ATT")[:, :, tsl].rearrange("g p t -> p g t"), reads=[scrB["ATT"]], writes=[attB])
                dma("sp", gT[:], T("Gs")[:, :, tsl].rearrange("g p t -> p g t"), reads=[scrB["Gs"]], writes=[gTB])
                dma("sp", ha[:], T("Hs")[tsl, :].rearrange("(n p) d -> p n d", p=128), reads=[scrB["Hs"]], writes=[haB])
                fm_proj(T("wb_brd"), 8, 8, attT, attB,
                        lambda g, ps, psB: op("dve", lambda e: e.tensor_tensor(out=ydT[:, g, :], in0=ps[:], in1=gT[:, g, :], op=ALU.mult),
                                              reads=[psB, gTB], writes=[ydB]))

                def ev_m(g, ps, psB):
                    op("dve", lambda e: e.tensor_tensor(out=tmpb[:], in0=ps[:], in1=gT[:, 8 + g, :], op=ALU.mult),
                       reads=[psB, gTB], writes=[tmpB])
                    op("pool", lambda e: e.tensor_tensor(out=mixT[:, g, :], in0=tmpb[:], in1=ydT[:, g, :], op=ALU.add),
                       reads=[tmpB, ydB], writes=[mixB])
                fm_proj(T("wb_brm"), 4, 8, attT[:, 8:12, :], attB, ev_m)
                tm_proj(T("wb_out"), 8, mixT, mixB, resid_evac(ha, haB, hbf, hbfB))
                layer_norm(lst, hbf, hbfB, lng[0], lng[1], lngB, hbf, hbfB, hbb, hbbB)
                transpose_to(hbb, hbbB, hT, hTB, tps, tpB, tcnt)
                fm_proj(T("wb_q"), 8, 8, hT, hTB, lambda g, ps, psB: op(ev_eng(), lambda e: cp(e, qxT[:, g, :], ps[:]), reads=[psB], writes=[qxB]))
                for hd in range(4):
                    for mt in range(2):
                        ps, psB = nextps()
                        for c in range(2):
                            op("pe", lambda e: e.matmul(ps[:], lhsT=kxT[:, 2 * hd + c, mt * 128:(mt + 1) * 128], rhs=qxT[:, 2 * hd + c, :],
                                                        start=(c == 0), stop=(c == 1)), reads=[kxB, qxB], writes=[psB], sig=(c == 1))
                        op("act", lambda e: e.activation(out=pTs[mt][:], in_=ps[:], func=AF.Exp, scale=1.0 / 16), reads=[psB], writes=[pTB[mt]])
                    ps, psB = nextps()
                    for mt in range(2):
                        op("pe", lambda e: e.matmul(ps[:], lhsT=ones[:], rhs=pTs[mt][:], start=(mt == 0), stop=(mt == 1)),
                           reads=[onesB, pTB[mt]], writes=[psB], sig=(mt == 1))
                    op("dve", lambda e: e.reciprocal(out=rs[:], in_=ps[:]), reads=[psB], writes=[rsB])
                    for dc in range(2):
                        ps, psB = nextps()
                        for mt in range(2):
                            op("pe", lambda e: e.matmul(ps[:], lhsT=vx[:, mt, hd * 256 + dc * 128:hd * 256 + (dc + 1) * 128], rhs=pTs[mt][:],
                                                        start=(mt == 0), stop=(mt == 1)), reads=[vxB, pTB[mt]], writes=[psB], sig=(mt == 1))
                        op("dve", lambda e: e.tensor_tensor(out=oT[:, 2 * hd + dc, :], in0=ps[:], in1=rs[:], op=ALU.mult),
                           reads=[psB, rsB], writes=[oTB])
                tm_proj(T("wb_o"), 8, oT, oTB, resid_evac(hbf, hbfB, ha, haB))
                layer_norm(lst, ha, haB, lng[2], lng[3], lngB, ha, haB, hbb, hbbB)
                transpose_to(hbb, hbbB, hT, hTB, tps, tpB, tcnt)

                def ev_f(g, ps, psB):
                    op("act", lambda e: e.activation(out=rl[:], in_=ps[:], func=AF.Relu), reads=[psB], writes=[rlB])
                    op(("dve", "pool")[g % 2], lambda e: e.tensor_tensor(out=actT[:, g, :], in0=rl[:], in1=rl[:], op=ALU.mult),
                       reads=[rlB], writes=[actB])
                fm_proj(T("wb_f1"), 8, 32, hT, hTB, ev_f)
                tm_proj(T("wb_f2"), 32, actT, actB, resid_evac(ha, haB, hbf, hbfB))
                layer_norm(lst, hbf, hbfB, lng[4], lng[5], lngB, hbf, hbfB, None, None)
                dma("sp", T("out_d")[tsl, :].rearrange("(n p) d -> p n d", p=128), hbf[:], reads=[hbfB], writes=[scrB["out"]])
            S.barrier()
        return nc, S, used_inputs


def _rel_bucket(n):
    n = np.asarray(n)
    nf = np.maximum(n, 1).astype(np.float32)
    large = 16 + (np.log(nf / np.float32(16)) / np.float32(math.log(128 / 16)) * np.float32(16)).astype(np.int32)
    large = np.minimum(large, 31)
    return np.where(n < 16, n, large)


def make_inputs(i, inputs):
    b, j = i // 4, i % 4
    f = lambda a: np.ascontiguousarray(np.asarray(a, dtype=np.float32))
    x = np.asarray(inputs["x"], dtype=np.float32)
    qts = [4 * r + j for r in range(8)]
    m = {}
    m["x_b"] = f(x[b])
    m["x_own"] = f(np.concatenate([x[b, 512 * q:512 * q + 512] for q in qts], axis=0))
    m["mem_b"] = f(inputs["mem"][b])
    for k in ("ln_in_g", "ln_in_b", "rel_table"):
        m[k] = f(inputs[k])
    for k in ("w_in", "b_gate", "lam_q1", "lam_k1", "lam_q2", "lam_k2", "diff_sub_g", "w_br_diff", "w_br_moba", "w_out",
              "ln1_g", "ln1_b", "wq_x", "wk_x", "wv_x", "wo_x", "ln2_g", "ln2_b", "w_ff1", "w_ff2", "ln3_g", "ln3_b"):
        m[k] = f(np.asarray(inputs[k])[0])
    rel = np.asarray(inputs["rel_table"], dtype=np.float32)
    m0 = 512 * j - 1920
    mm = np.arange(2560)[None, :] + m0
    kk = np.arange(128)[:, None]
    dist = mm - kk
    bk = _rel_bucket(np.maximum(dist, 0))
    tab = rel[bk]
    tab = np.where((dist >= 0)[:, :, None], tab, np.float32(NEG))
    m["t2"] = f(np.transpose(tab, (2, 0, 1)))
    blk = np.arange(NT) // 256
    m["eind"] = (blk[None, :] == np.arange(64)[:, None]).astype(ml_dtypes.bfloat16)
    pb = np.zeros((8, 4, 128, 64), np.float32); pv = np.zeros_like(pb); ot = np.full_like(pb, NEG)
    for r in range(8):
        for s in range(4):
            cur = 2 * (4 * r + j) + s // 2
            pb[r, s, :, cur:] = -1e30
            pv[r, s, :, :cur] = 1.0
            ot[r, s, :, cur] = 0.0
    m["pastbias"], m["pastvalid"], m["ownterm"] = pb, pv, ot
    m["ident"] = np.eye(128).astype(ml_dtypes.bfloat16)
    return m


def kernel(**inputs):
    nc, S, used = build_program()
    in_maps = [{k: v for k, v in make_inputs(i, inputs).items() if k in used} for i in range(8)]
    res = run_bass_kernel_spmd(nc, in_maps, core_ids=list(range(8)))
    out = np.zeros((2, NT, D), np.float32)
    for i in range(8):
        b, j = i // 4, i % 4
        o = res.results[i]["out"]
        for r in range(8):
            q = 4 * r + j
            out[b, 512 * q:512 * q + 512] = o[r * 512:(r + 1) * 512]
    return out
```

```python
import math
from contextlib import ExitStack
import numpy as np
import ml_dtypes
import concourse.bass as bass
import concourse.mybir as mybir
from concourse.bass_utils import run_bass_kernel_spmd

F32 = mybir.dt.float32
BF16 = mybir.dt.bfloat16
AF = mybir.ActivationFunctionType
ALU = mybir.AluOpType
AX = mybir.AxisListType

D = 1024
NT = 16384
NO = 4096
NEG = -30000.0
ALPHA = 2.0 ** 0.25
LAM_INIT = 0.2
EPS = 1e-5
C_DQ, C_DK, C_DV, C_MQ, C_MK, C_MV, C_G = 0, 1024, 2048, 3072, 3584, 4096, 4608


class Buf:
    __slots__ = ("name", "w", "r", "dsem", "dcnt")

    def __init__(self, name):
        self.name = name
        self.w = None
        self.r = {}
        self.dsem = None
        self.dcnt = 0


class Sched:
    def __init__(self, nc, es):
        self.nc = nc
        self.es = es
        self.E = {}
        for name, eng in (("pe", nc.tensor), ("act", nc.scalar), ("dve", nc.vector),
                          ("pool", nc.gpsimd), ("sp", nc.sync)):
            sem = es.enter_context(nc.semaphore("s_" + name))
            self.E[name] = dict(eng=eng, sem=sem, cnt=0, seen={})
        self.dbufs = []
        self.nwaits = 0

    def buf(self, name):
        return Buf(name)

    def bufs(self, name, n):
        return [Buf(f"{name}{i}") for i in range(n)]

    def _wait(self, en, ev):
        e = self.E[en]
        if e["seen"].get(ev[0], 0) >= ev[2]:
            return
        e["eng"].wait_ge(ev[1], ev[2])
        e["seen"][ev[0]] = ev[2]
        self.nwaits += 1

    def _deps(self, en, reads, writes):
        for b in reads:
            if b.w is not None:
                self._wait(en, b.w)
        for b in writes:
            if b.w is not None:
                self._wait(en, b.w)
            for ev in b.r.values():
                self._wait(en, ev)

    def op(self, en, fn, reads=(), writes=(), sig=True):
        e = self.E[en]
        if en == "pe":
            e["seen"]["pe"] = 1 << 60
        self._deps(en, reads, writes)
        ins = fn(e["eng"])
        if sig:
            e["cnt"] += 1
            ins.then_inc(e["sem"], 1)
            ev = (en, e["sem"], e["cnt"])
        else:
            ev = (en, e["sem"], e["cnt"] + 1)
        for b in reads:
            b.r[en] = ev
        for b in writes:
            b.w = ev
            b.r = {}
        return ins

    def dma(self, qn, out, in_, reads=(), writes=(), **kw):
        q = self.E[qn]
        self._deps(qn, reads, writes)
        b = writes[0]
        if b.dsem is None:
            b.dsem = self.es.enter_context(self.nc.semaphore("d_" + b.name))
            self.dbufs.append(b)
        ins = q["eng"].dma_start(out=out, in_=in_, **kw)
        b.dcnt += 16
        ins.then_inc(b.dsem, 16)
        key = ("d", id(b))
        ev = (key, b.dsem, b.dcnt)
        for r in reads:
            r.r[key] = ev
        for w in writes:
            w.w = ev
            w.r = {}
        return ins

    def barrier(self):
        for en, e in self.E.items():
            for on, o in self.E.items():
                if on != en and o["cnt"] > 0:
                    self._wait(en, (on, o["sem"], o["cnt"]))
            for b in self.dbufs:
                self._wait(en, (("d", id(b)), b.dsem, b.dcnt))


def build_program(stop_after=None, dbg=False):
    nc = bass.Bass("TRN2", target_bir_lowering=False)

    def din(name, shape, dt=F32):
        return nc.dram_tensor(name, list(shape), dt, kind="ExternalInput").ap()

    def dscr(name, shape, dt, kind="Internal"):
        return nc.dram_tensor(name, list(shape), dt, kind=kind).ap()

    _decl = {}
    _decl["x_b"] = ("x_b", lambda: [NT, D], F32, "ExternalInput")
    _decl["x_own"] = ("x_own", lambda: [NO, D], F32, "ExternalInput")
    _decl["mem_b"] = ("mem_b", lambda: [256, D], F32, "ExternalInput")
    _decl["ln_in_g"] = ("ln_in_g", lambda: [D], F32, "ExternalInput")
    _decl["ln_in_b"] = ("ln_in_b", lambda: [D], F32, "ExternalInput")
    _decl["rel_table"] = ("rel_table", lambda: [32, 16], F32, "ExternalInput")
    _decl["w_in"] = ("w_in", lambda: [D, 6656], F32, "ExternalInput")
    _decl["b_gate"] = ("b_gate", lambda: [2048], F32, "ExternalInput")
    _decl["lam_q1"] = ("lam_q1", lambda: [64], F32, "ExternalInput")
    _decl["lam_k1"] = ("lam_k1", lambda: [64], F32, "ExternalInput")
    _decl["lam_q2"] = ("lam_q2", lambda: [64], F32, "ExternalInput")
    _decl["lam_k2"] = ("lam_k2", lambda: [64], F32, "ExternalInput")
    _decl["diff_sub_g"] = ("diff_sub_g", lambda: [128], F32, "ExternalInput")
    _decl["w_br_diff"] = ("w_br_diff", lambda: [1024, D], F32, "ExternalInput")
    _decl["w_br_moba"] = ("w_br_moba", lambda: [512, D], F32, "ExternalInput")
    _decl["w_out"] = ("w_out", lambda: [D, D], F32, "ExternalInput")
    _decl["ln1_g"] = ("ln1_g", lambda: [D], F32, "ExternalInput")
    _decl["ln1_b"] = ("ln1_b", lambda: [D], F32, "ExternalInput")
    _decl["wq_x"] = ("wq_x", lambda: [D, D], F32, "ExternalInput")
    _decl["wk_x"] = ("wk_x", lambda: [D, D], F32, "ExternalInput")
    _decl["wv_x"] = ("wv_x", lambda: [D, D], F32, "ExternalInput")
    _decl["wo_x"] = ("wo_x", lambda: [D, D], F32, "ExternalInput")
    _decl["ln2_g"] = ("ln2_g", lambda: [D], F32, "ExternalInput")
    _decl["ln2_b"] = ("ln2_b", lambda: [D], F32, "ExternalInput")
    _decl["w_ff1"] = ("w_ff1", lambda: [D, 4096], F32, "ExternalInput")
    _decl["w_ff2"] = ("w_ff2", lambda: [4096, D], F32, "ExternalInput")
    _decl["ln3_g"] = ("ln3_g", lambda: [D], F32, "ExternalInput")
    _decl["ln3_b"] = ("ln3_b", lambda: [D], F32, "ExternalInput")
    _decl["t2"] = ("t2", lambda: [16, 128, 2560], F32, "ExternalInput")
    _decl["eind"] = ("eind", lambda: [64, NT], BF16, "ExternalInput")
    _decl["pastbias"] = ("pastbias", lambda: [8, 4, 128, 64], F32, "ExternalInput")
    _decl["pastvalid"] = ("pastvalid", lambda: [8, 4, 128, 64], F32, "ExternalInput")
    _decl["ownterm"] = ("ownterm", lambda: [8, 4, 128, 64], F32, "ExternalInput")
    _decl["ident_in"] = ("ident", lambda: [128, 128], BF16, "ExternalInput")
    _decl["out_d"] = ("out", lambda: [NO, D], F32, "ExternalOutput")
    _decl["KT"] = ("KT", lambda: [12, 128, NT], BF16, "dk")
    _decl["VSd"] = ("VSd", lambda: [8, 128, 128, 129], BF16, "dk")
    _decl["VSm"] = ("VSm", lambda: [8, 128, 128, 65], BF16, "dk")
    _decl["QT"] = ("QT", lambda: [8, 128, NO], BF16, "dk")
    _decl["QA"] = ("QA", lambda: [8, 128, NO], BF16, "dk")
    _decl["Gs"] = ("Gs", lambda: [16, 128, NO], BF16, "dk")
    _decl["Hs"] = ("Hs", lambda: [NO, D], F32, "dk")
    _decl["ATT"] = ("ATT", lambda: [12, 128, NO], BF16, "dk")
    _decl["wb_brd"] = ("wb_brd", lambda: [1024, D], BF16, "Internal")
    _decl["wb_brm"] = ("wb_brm", lambda: [512, D], BF16, "Internal")
    _decl["wb_out"] = ("wb_out", lambda: [D, D], BF16, "Internal")
    _decl["wb_q"] = ("wb_q", lambda: [D, D], BF16, "Internal")
    _decl["wb_k"] = ("wb_k", lambda: [D, D], BF16, "Internal")
    _decl["wb_v"] = ("wb_v", lambda: [D, D], BF16, "Internal")
    _decl["wb_o"] = ("wb_o", lambda: [D, D], BF16, "Internal")
    _decl["wb_f1"] = ("wb_f1", lambda: [D, 4096], BF16, "Internal")
    _decl["wb_f2"] = ("wb_f2", lambda: [4096, D], BF16, "Internal")
    dk = "ExternalOutput" if dbg else "Internal"
    _made = {}
    used_inputs = []

    def T(var):
        if var not in _made:
            name, shp, dt, kind = _decl[var]
            if kind == "dk":
                kind = dk
            _made[var] = nc.dram_tensor(name, list(shp()), dt, kind=kind).ap()
            if kind == "ExternalInput":
                used_inputs.append(name)
        return _made[var]

    with ExitStack() as es:
        S = Sched(nc, es)
        op, dma = S.op, S.dma
        evq = [0]

        def ev_eng():
            evq[0] += 1
            return "act" if evq[0] % 2 else "dve"

        def sbt(st, name, shape, dt):
            return st.enter_context(nc.sbuf_tensor(name, list(shape), dt))

        def pst(st, name, shape, dt):
            return st.enter_context(nc.psum_tensor(name, list(shape), dt))

        ident = sbt(es, "ident_sb", [128, 128], BF16); identB = S.buf("ident")
        dma("sp", ident[:], T("ident_in"), writes=[identB])
        kmT = sbt(es, "kmT", [128, 4, 64], F32); kmB = S.buf("kmT")
        epsc = sbt(es, "epsc", [128, 1], F32); epsB = S.buf("epsc")
        op("dve", lambda e: e.memset(epsc[:], EPS), writes=[epsB])
        scrB = {n: S.buf(n) for n in ("KT", "VSd", "VSm", "QT", "QA", "Gs", "Hs", "ATT", "out", "wb")}
        for dst, src in () if stop_after == "A1" else ((T("wb_brd"), T("w_br_diff")), (T("wb_brm"), T("w_br_moba")), (T("wb_out"), T("w_out")), (T("wb_q"), T("wq_x")), (T("wb_k"), T("wk_x")),
                         (T("wb_v"), T("wv_x")), (T("wb_o"), T("wo_x"))):
            dma("pool", dst, src, writes=[scrB["wb"]])
        for c in range(0 if stop_after == "A1" else 4):
            dma("pool", T("wb_f1")[c * 256:(c + 1) * 256, :], T("w_ff1")[c * 256:(c + 1) * 256, :], writes=[scrB["wb"]])
            dma("pool", T("wb_f2")[c * 1024:(c + 1) * 1024, :], T("w_ff2")[c * 1024:(c + 1) * 1024, :], writes=[scrB["wb"]])

        if stop_after == "A0":
            S.barrier()
            return nc, S, used_inputs
        def layer_norm(st, xt, xtB, gt, bt, gbB, out32, out32B, outbf, outbfB, nsub=4):
            s1, s2, mu, var, rstd, nmr, junk = st
            sB = s1[1]
            op("dve", lambda e: e.reduce_sum(out=s1[0][:, 0:nsub], in_=xt[:, 0:nsub, :], axis=AX.X), reads=[xtB], writes=[sB])
            for n in range(nsub):
                op("act", lambda e: e.activation(out=junk[0][:], in_=xt[:, n, :], func=AF.Square,
                                                 accum_out=s2[0][:, n:n + 1]), reads=[xtB], writes=[junk[1], s2[1]])
            op("dve", lambda e: e.tensor_scalar_mul(out=mu[0][:, 0:nsub], in0=s1[0][:, 0:nsub], scalar1=1.0 / D), reads=[sB], writes=[mu[1]])
            op("dve", lambda e: e.tensor_tensor(out=var[0][:, 0:nsub], in0=mu[0][:, 0:nsub], in1=mu[0][:, 0:nsub], op=ALU.mult),
               reads=[mu[1]], writes=[var[1]])
            op("dve", lambda e: e.scalar_tensor_tensor(out=var[0][:, 0:nsub], in0=s2[0][:, 0:nsub], scalar=1.0 / D, in1=var[0][:, 0:nsub],
                                                       op0=ALU.mult, op1=ALU.subtract), reads=[s2[1], var[1]], writes=[var[1]])
            op("act", lambda e: e.activation(out=rstd[0][:, 0:nsub], in_=var[0][:, 0:nsub], func=AF.Ln, bias=epsc[:]), reads=[var[1], epsB], writes=[rstd[1]])
            op("act", lambda e: e.activation(out=rstd[0][:, 0:nsub], in_=rstd[0][:, 0:nsub], func=AF.Exp, scale=-0.5), reads=[rstd[1]], writes=[rstd[1]])
            op("dve", lambda e: e.scalar_tensor_tensor(out=nmr[0][:, 0:nsub], in0=mu[0][:, 0:nsub], scalar=-1.0, in1=rstd[0][:, 0:nsub],
                                                       op0=ALU.mult, op1=ALU.mult), reads=[mu[1], rstd[1]], writes=[nmr[1]])
            for n in range(nsub):
                op("act", lambda e: e.activation(out=xt[:, n, :], in_=xt[:, n, :], func=AF.Identity,
                                                 bias=nmr[0][:, n:n + 1], scale=rstd[0][:, n:n + 1]),
                   reads=[xtB, nmr[1], rstd[1]], writes=[xtB])
                op("dve", lambda e: e.tensor_tensor(out=xt[:, n, :], in0=xt[:, n, :], in1=gt[:], op=ALU.mult),
                   reads=[xtB, gbB], writes=[xtB])
                op("dve", lambda e: e.tensor_tensor(out=out32[:, n, :], in0=xt[:, n, :], in1=bt[:], op=ALU.add),
                   reads=[xtB, gbB], writes=[out32B])
                if outbf is not None:
                    op("act", lambda e: e.copy(out=outbf[:, n, :], in_=out32[:, n, :]), reads=[out32B], writes=[outbfB])

        def layer_norm_ap(st, xt, xtB, gt, bt, gbB, outbf, outbfB, epsc, epsB, nsub=4, keep32=False):
            s1, s2, mu, var, rstd, nmr, junk = st
            for n in range(nsub):
                op("act", lambda e: e.activation(out=junk[0][:], in_=xt[:, n, :], func=AF.Identity,
                                                 accum_out=s1[0][:, n:n + 1]), reads=[xtB], writes=[junk[1], s1[1]])
                op("act", lambda e: e.activation(out=junk[0][:], in_=xt[:, n, :], func=AF.Square,
                                                 accum_out=s2[0][:, n:n + 1]), reads=[xtB], writes=[junk[1], s2[1]])
            op("pool", lambda e: e.tensor_scalar_mul(out=mu[0][:, 0:nsub], in0=s1[0][:, 0:nsub], scalar1=1.0 / D), reads=[s1[1]], writes=[mu[1]])
            op("pool", lambda e: e.tensor_tensor(out=var[0][:, 0:nsub], in0=mu[0][:, 0:nsub], in1=mu[0][:, 0:nsub], op=ALU.mult),
               reads=[mu[1]], writes=[var[1]])
            op("pool", lambda e: e.tensor_scalar_mul(out=s2[0][:, 0:nsub], in0=s2[0][:, 0:nsub], scalar1=1.0 / D), reads=[s2[1]], writes=[s2[1]])
            op("pool", lambda e: e.tensor_tensor(out=var[0][:, 0:nsub], in0=s2[0][:, 0:nsub], in1=var[0][:, 0:nsub], op=ALU.subtract),
               reads=[s2[1], var[1]], writes=[var[1]])
            op("act", lambda e: e.activation(out=rstd[0][:, 0:nsub], in_=var[0][:, 0:nsub], func=AF.Ln, bias=epsc[:]), reads=[var[1], epsB], writes=[rstd[1]])
            op("act", lambda e: e.activation(out=rstd[0][:, 0:nsub], in_=rstd[0][:, 0:nsub], func=AF.Exp, scale=-0.5), reads=[rstd[1]], writes=[rstd[1]])
            op("pool", lambda e: e.tensor_tensor(out=nmr[0][:, 0:nsub], in0=mu[0][:, 0:nsub], in1=rstd[0][:, 0:nsub], op=ALU.mult),
               reads=[mu[1], rstd[1]], writes=[nmr[1]])
            op("pool", lambda e: e.tensor_scalar_mul(out=nmr[0][:, 0:nsub], in0=nmr[0][:, 0:nsub], scalar1=-1.0), reads=[nmr[1]], writes=[nmr[1]])
            for n in range(nsub):
                op("act", lambda e: e.activation(out=xt[:, n, :], in_=xt[:, n, :], func=AF.Identity,
                                                 bias=nmr[0][:, n:n + 1], scale=rstd[0][:, n:n + 1]),
                   reads=[xtB, nmr[1], rstd[1]], writes=[xtB])
                op("pool", lambda e: e.tensor_tensor(out=xt[:, n, :], in0=xt[:, n, :], in1=gt[:], op=ALU.mult),
                   reads=[xtB, gbB], writes=[xtB])
                if not keep32:
                    op("pool", lambda e: e.tensor_tensor(out=outbf[:, n, :], in0=xt[:, n, :], in1=bt[:], op=ALU.add),
                       reads=[xtB, gbB], writes=[outbfB])
                else:
                    op("pool", lambda e: e.tensor_tensor(out=xt[:, n, :], in0=xt[:, n, :], in1=bt[:], op=ALU.add),
                       reads=[xtB, gbB], writes=[xtB])
                    op("act", lambda e: e.copy(out=outbf[:, n, :], in_=xt[:, n, :]), reads=[xtB], writes=[outbfB])

        def ln_state(st_, pfx):
            res = []
            for nm in ("s1", "s2", "mu", "var", "rstd", "nmr"):
                res.append((sbt(st_, pfx + nm, [128, 4], F32), S.buf(pfx + nm)))
            res.append((sbt(st_, pfx + "junk", [128, 1024], F32), S.buf(pfx + "junk")))
            return res

        def transpose_to(hb, hbB, hT, hTB, tps, tpB, tcnt, nsub=4, eng=None):
            for c in range(8):
                i = tcnt[0] % len(tps); tcnt[0] += 1
                for n in range(nsub):
                    op("pe", lambda e: e.transpose(out=tps[i][:, n * 128:(n + 1) * 128], in_=hb[:, n, c * 128:(c + 1) * 128],
                                                   identity=ident[:]),
                       reads=[hbB, identB], writes=[tpB[i]], sig=(n == nsub - 1))
                op(eng or ev_eng(), lambda e: cp(e, hT[:, c, 0:nsub * 128], tps[i][:, 0:nsub * 128]),
                   reads=[tpB[i]], writes=[hTB])

        def cp(e, out, in_):
            return e.copy(out=out, in_=in_) if e is nc.scalar else e.tensor_copy(out=out, in_=in_)

        with ExitStack() as ph:
            gt = sbt(ph, "gt", [128, D], F32); bt = sbt(ph, "bt", [128, D], F32); gbB = S.buf("gb")
            dma("sp", gt[:], T("ln_in_g").partition_broadcast(128), writes=[gbB])
            dma("sp", bt[:], T("ln_in_b").partition_broadcast(128), writes=[gbB])
            Wk = sbt(ph, "Wk", [128, 8, 1536], BF16); WkB = S.buf("Wk")
            Wv = sbt(ph, "Wv", [128, 8, 1536], BF16); WvB = S.buf("Wv")
            for c in range(8):
                rows = slice(c * 128, (c + 1) * 128)
                dma("pool", Wk[:, c, 0:1024], T("w_in")[rows, C_DK:C_DK + 1024], writes=[WkB])
                dma("pool", Wk[:, c, 1024:1536], T("w_in")[rows, C_MK:C_MK + 512], writes=[WkB])
                dma("pool", Wv[:, c, 0:1024], T("w_in")[rows, C_DV:C_DV + 1024], writes=[WvB])
                dma("pool", Wv[:, c, 1024:1536], T("w_in")[rows, C_MV:C_MV + 512], writes=[WvB])
            xts = [sbt(ph, f"xt{i}", [128, 4, D], F32) for i in range(2)]; xtBs = S.bufs("xt", 2)
            hbs = [sbt(ph, f"hb{i}", [128, 4, D], BF16) for i in range(3)]; hbBs = S.bufs("hb", 3)
            hTs = [sbt(ph, f"hT{i}", [128, 8, 512], BF16) for i in range(2)]; hTBs = S.bufs("hT", 2)
            kst = sbt(ph, "kst", [128, 12, 512], BF16); kstB = S.buf("kst")
            vstd = sbt(ph, "vstd", [128, 8, 4, 129], BF16); vstdB = S.buf("vstd")
            vstm = sbt(ph, "vstm", [128, 8, 4, 65], BF16); vstmB = S.buf("vstm")
            lst = ln_state(ph, "a_")
            tps = [pst(ph, f"tpa{i}", [128, 1024], BF16) for i in range(2)]; tpB = S.bufs("tpa", 2)
            mps = [pst(ph, f"mpa{i}", [128, 512], F32) for i in range(4)]; mpB = S.bufs("mpa", 4)
            op("dve", lambda e: e.memset(vstd[:, :, :, 128:129], 1.0), writes=[vstdB])
            op("dve", lambda e: e.memset(vstm[:, :, :, 64:65], 1.0), writes=[vstmB])
            tcnt = [0]; mcnt = [0]
            nta = 32 if stop_after != "A1" else 1
            def ln_a(Tt):
                xt, xtB = xts[Tt % 2], xtBs[Tt % 2]
                dma("sp", xt[:], T("x_b")[Tt * 512:(Tt + 1) * 512, :].rearrange("(n p) d -> p n d", p=128), writes=[xtB])
                layer_norm_ap(lst, xt, xtB, gt, bt, gbB, hbs[Tt % 3], hbBs[Tt % 3], epsc, epsB)

            def tr_a(Tt):
                transpose_to(hbs[Tt % 3], hbBs[Tt % 3], hTs[Tt % 2], hTBs[Tt % 2], tps, tpB, tcnt, eng="dve")

            ln_a(0)
            if nta > 1:
                ln_a(1)
            tr_a(0)
            for Tt in range(nta):
                if Tt + 2 < nta:
                    ln_a(Tt + 2)
                if Tt + 1 < nta:
                    tr_a(Tt + 1)
                hT, hTB = hTs[Tt % 2], hTBs[Tt % 2]
                for g in range(12):
                    i = mcnt[0] % 4; mcnt[0] += 1
                    for c in range(8):
                        op("pe", lambda e: e.matmul(mps[i][:], lhsT=Wk[:, c, g * 128:(g + 1) * 128], rhs=hT[:, c, :],
                                                    start=(c == 0), stop=(c == 7)),
                           reads=[WkB, hTB], writes=[mpB[i]], sig=(c == 7))
                    op("dve", lambda e: cp(e, kst[:, g, :], mps[i][:]), reads=[mpB[i]], writes=[kstB])
                    if g >= 8:
                        op("dve", lambda e: e.reduce_sum(out=kmT[:, g - 8, 2 * Tt:2 * Tt + 2],
                                                         in_=mps[i][:].rearrange("p (a b) -> p a b", b=256), axis=AX.X),
                           reads=[mpB[i]], writes=[kmB])
                for g4 in range(3):
                    dma("sp", T("KT")[4 * g4:4 * g4 + 4, :, Tt * 512:(Tt + 1) * 512].rearrange("g p t -> p g t"), kst[:, 4 * g4:4 * g4 + 4, :],
                        reads=[kstB], writes=[scrB["KT"]])
                for n in range(4):
                    for cg in range(3):
                        i = mcnt[0] % 4; mcnt[0] += 1
                        for c in range(8):
                            op("pe", lambda e: e.matmul(mps[i][:], lhsT=hT[:, c, n * 128:(n + 1) * 128],
                                                        rhs=Wv[:, c, cg * 512:(cg + 1) * 512], start=(c == 0), stop=(c == 7)),
                               reads=[WvB, hTB], writes=[mpB[i]], sig=(c == 7))
                        if cg < 2:
                            op("dve", lambda e: cp(e, vstd[:, 4 * cg:4 * cg + 4, n, 0:128],
                                                      mps[i][:].rearrange("p (h d) -> p h d", d=128)),
                               reads=[mpB[i]], writes=[vstdB])
                        else:
                            op("dve", lambda e: cp(e, vstm[:, :, n, 0:64], mps[i][:].rearrange("p (h d) -> p h d", d=64)),
                               reads=[mpB[i]], writes=[vstmB])
                for h in range(8):
                    dma("sp", T("VSd")[h, :, 4 * Tt:4 * Tt + 4, :], vstd[:, h, :, :], reads=[vstdB], writes=[scrB["VSd"]])
                    dma("sp", T("VSm")[h, :, 4 * Tt:4 * Tt + 4, :], vstm[:, h, :, :], reads=[vstmB], writes=[scrB["VSm"]])
            S.barrier()
        if stop_after in ("A", "A1"):
            return nc, S, used_inputs

        with ExitStack() as ph:
            gt = sbt(ph, "gtb", [128, D], F32); bt = sbt(ph, "btb", [128, D], F32); gbB = S.buf("gbb")
            dma("sp", gt[:], T("ln_in_g").partition_broadcast(128), writes=[gbB])
            dma("sp", bt[:], T("ln_in_b").partition_broadcast(128), writes=[gbB])
            Wq = sbt(ph, "Wq", [128, 8, 1536], BF16); WqB = S.buf("Wq")
            Wg = sbt(ph, "Wg", [128, 8, 2048], BF16); WgB = S.buf("Wg")
            for c in range(8):
                rows = slice(c * 128, (c + 1) * 128)
                dma("pool", Wq[:, c, 0:1024], T("w_in")[rows, C_DQ:C_DQ + 1024], writes=[WqB])
                dma("pool", Wq[:, c, 1024:1536], T("w_in")[rows, C_MQ:C_MQ + 512], writes=[WqB])
                dma("pool", Wg[:, c, :], T("w_in")[rows, C_G:C_G + 2048], writes=[WgB])
            bgT = sbt(ph, "bgT", [128, 16], F32); bgB = S.buf("bgT")
            dma("sp", bgT[:], T("b_gate").rearrange("(g p) -> p g", p=128), writes=[bgB], allow_slow_non_contiguous=True)
            kml = sbt(ph, "kml", [128, 4, 64], F32); kmh = sbt(ph, "kmh", [128, 4, 64], F32); kmlB = S.buf("kml")
            op("dve", lambda e: e.memset(kml[:], 0.0), writes=[kmlB])
            op("dve", lambda e: e.memset(kmh[:], 0.0), writes=[kmlB])
            op("dve", lambda e: e.tensor_scalar_mul(out=kml[0:64, :, :], in0=kmT[0:64, :, :], scalar1=1.0 / 256), reads=[kmB], writes=[kmlB])
            op("dve", lambda e: e.tensor_scalar_mul(out=kmh[64:128, :, :], in0=kmT[64:128, :, :], scalar1=1.0 / 256), reads=[kmB], writes=[kmlB])
            xts = [sbt(ph, f"xtb{i}", [128, 4, D], F32) for i in range(2)]; xtBs = S.bufs("xtb", 2)
            hbs = [sbt(ph, f"hbq{i}", [128, 4, D], BF16) for i in range(3)]; hbBs = S.bufs("hbb", 3)
            hTs = [sbt(ph, f"hTb{i}", [128, 8, 512], BF16) for i in range(2)]; hTBs = S.bufs("hTb", 2)
            qst = sbt(ph, "qst", [128, 8, 512], BF16); qstB = S.buf("qst")
            qf = sbt(ph, "qf", [128, 4, 512], F32); qfB = S.buf("qf")
            qa = sbt(ph, "qa", [128, 8, 512], BF16); qaB = S.buf("qa")
            gst = sbt(ph, "gst", [128, 16, 512], BF16); gstB = S.buf("gst")
            pbt = sbt(ph, "pbt", [128, 4, 64], F32); pvt = sbt(ph, "pvt", [128, 4, 64], F32); ott = sbt(ph, "ott", [128, 4, 64], F32)
            pcB = S.buf("pc")
            gm4 = sbt(ph, "gm4", [128, 4, 64], F32); gmB = S.buf("gm")
            m8 = sbt(ph, "m8", [128, 4, 8], F32); m8B = S.buf("m8")
            sel4 = sbt(ph, "sel4", [128, 4, 64], F32); selB = S.buf("sel")
            mpads = [sbt(ph, f"mpad{i}", [128, 4, 128], BF16) for i in range(8)]; mpadBs = S.bufs("mpad", 8)
            for i in range(8):
                op("dve", lambda e: e.memset(mpads[i][:], 0.0), writes=[mpadBs[i]])
            lst = ln_state(ph, "b_")
            tps = [pst(ph, f"tpb{i}", [128, 1024], BF16) for i in range(2)]; tpB = S.bufs("tpb", 2)
            mps = [pst(ph, f"mpb{i}", [128, 512], F32) for i in range(4)]; mpB = S.bufs("mpb", 4)
            gpss = [pst(ph, f"gps{i}", [128, 512], F32) for i in range(2)]; gpsBs = S.bufs("gps", 2)
            tcnt = [0]; mcnt = [0]
            def ln_b(r):
                xt, xtB = xts[r % 2], xtBs[r % 2]
                tsl = slice(r * 512, (r + 1) * 512)
                dma("sp", xt[:], T("x_own")[tsl, :].rearrange("(n p) d -> p n d", p=128), writes=[xtB])
                layer_norm_ap(lst, xt, xtB, gt, bt, gbB, hbs[r % 3], hbBs[r % 3], epsc, epsB, keep32=True)
                dma("sp", T("Hs")[tsl, :].rearrange("(n p) d -> p n d", p=128), xt[:], reads=[xtB], writes=[scrB["Hs"]])

            def tr_b(r):
                transpose_to(hbs[r % 3], hbBs[r % 3], hTs[r % 2], hTBs[r % 2], tps, tpB, tcnt, eng="dve")

            ln_b(0); ln_b(1); tr_b(0)
            for r in range(8):
                if r + 2 < 8:
                    ln_b(r + 2)
                if r + 1 < 8:
                    tr_b(r + 1)
                hT, hTB = hTs[r % 2], hTBs[r % 2]
                tsl = slice(r * 512, (r + 1) * 512)
                dma("sp", pbt[:], T("pastbias")[r].rearrange("s p n -> p s n"), writes=[pcB])
                dma("sp", pvt[:], T("pastvalid")[r].rearrange("s p n -> p s n"), writes=[pcB])
                dma("sp", ott[:], T("ownterm")[r].rearrange("s p n -> p s n"), writes=[pcB])

                def proj(col0, M, dst_fn, eng=None):
                    i = mcnt[0] % 4; mcnt[0] += 1
                    for c in range(8):
                        op("pe", lambda e: e.matmul(mps[i][0:M, :], lhsT=W_[:, c, col0:col0 + M], rhs=hT[:, c, :],
                                                    start=(c == 0), stop=(c == 7)),
                           reads=[W_B, hTB], writes=[mpB[i]], sig=(c == 7))
                    dst_fn(mps[i], mpB[i])

                W_, W_B = Wq, WqB
                for g in range(8):
                    proj(g * 128, 128, lambda ps, psB: op("dve", lambda e: e.tensor_scalar_mul(out=qst[:, g, :], in0=ps[:], scalar1=0.125),
                                                          reads=[psB], writes=[qstB]))
                dma("sp", T("QT")[:, :, tsl].rearrange("g p t -> p g t"), qst[:], reads=[qstB], writes=[scrB["QT"]])
                for i4 in range(4):
                    proj(1024 + i4 * 128, 128, lambda ps, psB: op("act", lambda e: e.copy(out=qf[:, i4, :], in_=ps[:]),
                                                                  reads=[psB], writes=[qfB]))
                for h in range(8):
                    proj(1024 + h * 64, 64, lambda ps, psB: op("dve", lambda e: e.tensor_scalar_mul(out=qa[0:64, h, :], in0=ps[0:64, :], scalar1=0.125),
                                                               reads=[psB], writes=[qaB]))
                for h in range(8):
                    km = kml if h % 2 == 0 else kmh
                    gp, gpB = gpss[h % 2], gpsBs[h % 2]
                    for s in range(4):
                        op("pe", lambda e: e.matmul(gp[:, s * 64:(s + 1) * 64], lhsT=qf[:, h // 2, s * 128:(s + 1) * 128],
                                                    rhs=km[:, h // 2, :], start=True, stop=True),
                           reads=[qfB, kmlB], writes=[gpB], sig=(s == 3))
                    op("dve", lambda e: e.tensor_tensor(out=gm4[:], in0=gp[:, 0:256].rearrange("p (s n) -> p s n", n=64), in1=pbt[:], op=ALU.add),
                       reads=[gpB, pcB], writes=[gmB])
                    for s in range(4):
                        op("dve", lambda e: e.max(out=m8[:, s, :], in_=gm4[:, s, :]), reads=[gmB], writes=[m8B])
                    for s in range(4):
                        op("dve", lambda e: e.scalar_tensor_tensor(out=sel4[:, s, :], in0=gm4[:, s, :], scalar=m8[:, s, 2:3], in1=pvt[:, s, :],
                                                                   op0=ALU.is_ge, op1=ALU.mult), reads=[gmB, m8B, pcB], writes=[selB])
                    op("dve", lambda e: e.scalar_tensor_tensor(out=mpads[h][:, :, 64:128], in0=sel4[:], scalar=-NEG, in1=ott[:],
                                                               op0=ALU.mult, op1=ALU.add), reads=[selB, pcB], writes=[mpadBs[h]])
                W_, W_B = Wg, WgB
                for gg in range(16):
                    proj(gg * 128, 128, lambda ps, psB: op("act", lambda e: e.activation(out=gst[:, gg, :], in_=ps[:], func=AF.Sigmoid,
                                                                                         bias=bgT[:, gg:gg + 1]),
                                                           reads=[psB, bgB], writes=[gstB]))
                for h in range(8):
                    i = tcnt[0] % 2; tcnt[0] += 1
                    for s in range(4):
                        op("pe", lambda e: e.transpose(out=tps[i][:, s * 128:(s + 1) * 128], in_=mpads[h][:, s, :], identity=ident[:]),
                           reads=[mpadBs[h], identB], writes=[tpB[i]], sig=(s == 3))
                    op("act", lambda e: e.copy(out=qa[64:128, h, :], in_=tps[i][64:128, 0:512]), reads=[tpB[i]], writes=[qaB])
                dma("sp", T("QA")[:, :, tsl].rearrange("h p t -> p h t"), qa[:], reads=[qaB], writes=[scrB["QA"]])
                dma("sp", T("Gs")[:, :, tsl].rearrange("g p t -> p g t"), gst[:], reads=[gstB], writes=[scrB["Gs"]])
            S.barrier()
        if stop_after == "B":
            return nc, S, used_inputs

        with ExitStack() as ph:
            KTb = sbt(ph, "KTb", [128, NT], BF16); KTB = S.bufs("KTc", 8)
            VTb = sbt(ph, "VTb", [128, 128, 129], BF16); VTB = S.bufs("VTc", 8)
            Bb = [sbt(ph, f"Bb{i}", [128, 17, 512], BF16) for i in range(2)]; BB = S.bufs("Bb", 2)
            Qb = [sbt(ph, f"Qb{i}", [128, NO], BF16) for i in range(2)]; QB = S.bufs("Qb", 2)
            chc = [sbt(ph, f"chc{i}", [128, 1], F32) for i in range(2)]; chB = S.bufs("chc", 2)
            Btmp = [sbt(ph, f"Btmp{i}", [128, 4, 512], F32) for i in range(2)]; BtB = S.bufs("Btmp", 2); bcnt = [0]
            zc = sbt(ph, "zc", [128, 1], F32); zcB = S.buf("zc")
            op("dve", lambda e: e.memset(zc[:], 0.0), writes=[zcB])
            PT = [sbt(ph, f"PT{i}", [128, 1024], BF16) for i in range(3)]; PTB = S.bufs("PT", 3)
            lq = sbt(ph, "lq", [128, 4, 64], F32); lqB = S.buf("lq")
            for i, a in enumerate((T("lam_q1"), T("lam_k1"), T("lam_q2"), T("lam_k2"))):
                dma("sp", lq[:, i, :], a.partition_broadcast(128), writes=[lqB])
            lw = sbt(ph, "lw", [128, 2, 64], F32); ls = sbt(ph, "ls", [128, 4], F32); lwB = S.buf("lw")
            op("dve", lambda e: e.tensor_tensor(out=lw[:, 0, :], in0=lq[:, 0, :], in1=lq[:, 1, :], op=ALU.mult), reads=[lqB], writes=[lwB])
            op("dve", lambda e: e.tensor_tensor(out=lw[:, 1, :], in0=lq[:, 2, :], in1=lq[:, 3, :], op=ALU.mult), reads=[lqB], writes=[lwB])
            op("dve", lambda e: e.reduce_sum(out=ls[:, 0:2], in_=lw[:], axis=AX.X), reads=[lwB], writes=[lwB])
            op("act", lambda e: e.activation(out=ls[:, 0:2], in_=ls[:, 0:2], func=AF.Exp), reads=[lwB], writes=[lwB])
            op("dve", lambda e: e.tensor_tensor(out=ls[:, 2:3], in0=ls[:, 1:2], in1=ls[:, 0:1], op=ALU.subtract), reads=[lwB], writes=[lwB])
            op("dve", lambda e: e.tensor_scalar_add(out=ls[:, 2:3], in0=ls[:, 2:3], scalar1=-LAM_INIT), reads=[lwB], writes=[lwB])
            nlam = ls[:, 2:3]
            sg = sbt(ph, "sg", [128, 128], F32); sgB = S.buf("sg")
            dma("sp", sg[:], T("diff_sub_g").partition_broadcast(128), writes=[sgB])
            op("dve", lambda e: e.tensor_scalar_mul(out=sg[:], in0=sg[:], scalar1=1.0 - LAM_INIT), reads=[sgB], writes=[sgB])
            rc = sbt(ph, "rc", [128, 8], F32); rcB = S.buf("rc")
            oa = sbt(ph, "oa", [128, 128], F32); oaB = S.buf("oa")
            od = sbt(ph, "od", [128, 128], F32); odB = S.buf("od")
            ojk = sbt(ph, "ojk", [128, 128], F32); ojB = S.buf("ojk")
            ob = sbt(ph, "ob", [128, 4, 128], BF16); obB = S.buf("ob")
            osb = sbt(ph, "osb", [128, 3, 387], F32); osbB = S.buf("osb")
            od4 = sbt(ph, "od4", [128, 4, 128], F32)
            rq = sbt(ph, "rq", [128, 16], F32); rqB = S.buf("rq")
            ast = [sbt(ph, f"ast{i}", [128, 512], BF16) for i in range(2)]; astB = S.bufs("ast", 2)
            Sps = [pst(ph, f"Sps{i}", [128, 1024], F32) for i in range(2)]; SpB = S.bufs("Sps", 2)
            Ops = [pst(ph, f"Ops{i}", [128, 512], F32) for i in range(3)]; OpB = S.buf("Ops")
            tpc = pst(ph, "tpc", [128, 1024], BF16); tpcB = S.buf("tpc")
            scnt = [0]; pcnt = [0]; acnt = [0]
            heads = range(16) if stop_after != "C1" else [0, 8]
            nslots = 8 if stop_after != "C1" else 2
            for hi, hh in enumerate(heads):
                moba = hh >= 8
                hm = hh - 8
                par = hi % 2
                dma("sp", chc[par][:], T("rel_table")[31, hh:hh + 1].partition_broadcast(128), writes=[chB[par]])
                if not moba:
                    dma("sp", Qb[par][:], T("QT")[hh], reads=[scrB["QT"]], writes=[QB[par]])
                else:
                    dma("sp", Qb[par][:], T("QA")[hm], reads=[scrB["QA"]], writes=[QB[par]])
                for ui, u0 in enumerate(range(0, 17, 4)):
                    u1 = min(17, u0 + 4)
                    src = bass.AP(tensor=T("t2").tensor, offset=T("t2")[hh, 0, u0 * 128].offset, ap=[[2560, 128], [128, u1 - u0], [1, 512]])
                    bi = bcnt[0] % 2; bcnt[0] += 1
                    dma("sp", Btmp[bi][:, 0:u1 - u0, :], src, writes=[BtB[bi]])
                    op("dve", lambda e: e.tensor_scalar(out=Bb[par][:, u0:u1, :], in0=Btmp[bi][:, 0:u1 - u0, :], scalar1=chc[par][:, 0:1],
                                                        scalar2=None, op0=ALU.subtract), reads=[BtB[bi], chB[par]], writes=[BB[par]])
                for c in range(nslots):
                    ksl = slice(c * 2048, (c + 1) * 2048)
                    if not moba:
                        dma("sp", KTb[:, ksl], T("KT")[hh, :, ksl], reads=[scrB["KT"]], writes=[KTB[c]])
                        dma("sp", VTb[:, 16 * c:16 * c + 16, :], T("VSd")[hh, :, 16 * c:16 * c + 16, :], reads=[scrB["VSd"]], writes=[VTB[c]])
                    else:
                        dma("sp", KTb[0:64, ksl], T("KT")[8 + hm // 2, (hm % 2) * 64:(hm % 2) * 64 + 64, ksl], reads=[scrB["KT"]], writes=[KTB[c]])
                        dma("sp", KTb[64:128, ksl], T("eind")[:, ksl], writes=[KTB[c]])
                        dma("sp", VTb[:, 16 * c:16 * c + 16, 0:65], T("VSm")[hm, :, 16 * c:16 * c + 16, :], reads=[scrB["VSm"]], writes=[VTB[c]])
                nm = 1 if moba else 2
                vw = 65 if moba else 129
                dv = vw - 1
                def acc_ap(a):
                    return Ops[a // 3][:, (a % 3) * 129:(a % 3) * 129 + vw]

                def osb_ap(a):
                    return osb[:, a // 3, (a % 3) * 129:(a % 3) * 129 + vw]

                def parts_of(r, st):
                    if not moba:
                        return [(st, slice(0, 64), 0), (st, slice(64, 128), 4)]
                    return [(2 * st, slice(0, 128), 0), (2 * st + 1, slice(0, 128), 0)]

                def emit_qk(r, st):
                    qsl = slice(r * 512, (r + 1) * 512)
                    si = scnt[0] % 2; scnt[0] += 1
                    Sp = Sps[si]
                    parts = parts_of(r, st)
                    nears = [kt - (16 * r - 1) >= 0 for kt, _, _ in parts]
                    for p, (kt, rows, _) in enumerate(parts):
                        c = kt // 16
                        op("pe", lambda e: e.matmul(Sp[:, p * 512:(p + 1) * 512], lhsT=KTb[rows, kt * 128:(kt + 1) * 128],
                                                    rhs=Qb[par][rows, qsl], start=True, stop=not nears[p]),
                           reads=[KTB[c], QB[par]], writes=[SpB[si]], sig=(p == 1 and not any(nears)))
                    lastn = max([p for p in range(2) if nears[p]], default=-1)
                    for p, (kt, rows, _) in enumerate(parts):
                        if nears[p]:
                            u = 16 - (kt - (16 * r - 1))
                            op("pe", lambda e: e.matmul(Sp[:, p * 512:(p + 1) * 512], lhsT=ident[:], rhs=Bb[par][:, u, :],
                                                        start=False, stop=True),
                               reads=[identB, BB[par]], writes=[SpB[si]], sig=(p == lastn))
                    return (si,)

                def emit_exp_av(r, st, si):
                    nkt = 16 * r + 16
                    Sp = Sps[si]
                    pi = pcnt[0] % 3; pcnt[0] += 1
                    op("act", lambda e: e.activation(out=PT[pi][:, 0:1024], in_=Sp[:, 0:1024], func=AF.Exp, bias=chc[par][:]),
                       reads=[SpB[si], chB[par]], writes=[PTB[pi]])
                    parts = parts_of(r, st)
                    for p, (kt, rows, abase) in enumerate(parts):
                        c = kt // 16
                        for s in range(4):
                            a = abase + s
                            bank = a // 3
                            st_flag = (kt == 0) and (bank not in slot_banks)
                            slot_banks.add(bank)
                            op("pe", lambda e: e.matmul(acc_ap(a), lhsT=PT[pi][:, p * 512 + s * 128:p * 512 + (s + 1) * 128],
                                                        rhs=VTb[:, kt, 0:vw], start=st_flag, stop=(kt == nkt - 1),
                                                        skip_group_check=True),
                               reads=[PTB[pi], VTB[c]], writes=[OpB], sig=(p == 1 and s == 3))

                def emit_epi_dve(r):
                    if not moba:
                        for bk, wdt in ((0, 387), (1, 387), (2, 258)):
                            op("dve", lambda e: e.tensor_copy(out=osb[:, bk, 0:wdt], in_=Ops[bk][:, 0:wdt]), reads=[OpB], writes=[osbB])
                    else:
                        for a in range(4):
                            op("dve", lambda e: e.tensor_copy(out=osb_ap(a), in_=acc_ap(a)), reads=[OpB], writes=[osbB])
                    if not moba:
                        for s in range(4):
                            a0, a1 = osb_ap(s), osb_ap(4 + s)
                            op("dve", lambda e: e.reciprocal(out=rc[:, 0:1], in_=a0[:, 128:129]), reads=[osbB], writes=[rcB])
                            op("dve", lambda e: e.reciprocal(out=rc[:, 1:2], in_=a1[:, 128:129]), reads=[osbB], writes=[rcB])
                            op("dve", lambda e: e.tensor_tensor(out=rc[:, 2:3], in0=rc[:, 1:2], in1=nlam, op=ALU.mult), reads=[rcB, lwB], writes=[rcB])
                            op("dve", lambda e: e.tensor_scalar_mul(out=oa[:], in0=a0[:, 0:128], scalar1=rc[:, 0:1]), reads=[osbB, rcB], writes=[oaB])
                            op("dve", lambda e: e.scalar_tensor_tensor(out=od4[:, s, :], in0=a1[:, 0:128], scalar=rc[:, 2:3], in1=oa[:],
                                                                       op0=ALU.mult, op1=ALU.add), reads=[osbB, rcB, oaB], writes=[odB])
                            op("dve", lambda e: e.tensor_tensor(out=ojk[:], in0=od4[:, s, :], in1=od4[:, s, :], op=ALU.mult), reads=[odB], writes=[ojB])
                            op("dve", lambda e: e.reduce_sum(out=rq[:, s:s + 1], in_=ojk[:], axis=AX.X), reads=[ojB], writes=[rqB])
                    else:
                        for s in range(4):
                            a0 = osb_ap(s)
                            op("dve", lambda e: e.reciprocal(out=rc[:, 0:1], in_=a0[:, 64:65]), reads=[osbB], writes=[rcB])
                            op("dve", lambda e: e.tensor_scalar_mul(out=ob[:, s, 0:64], in0=a0[:, 0:64], scalar1=rc[:, 0:1]),
                               reads=[osbB, rcB], writes=[obB])

                def emit_epi_tail(r):
                    qsl = slice(r * 512, (r + 1) * 512)
                    ai = acnt[0] % 2; acnt[0] += 1
                    if not moba:
                        op("dve", lambda e: e.tensor_scalar(out=rq[:, 4:8], in0=rq[:, 0:4], scalar1=1.0 / 128, scalar2=EPS,
                                                            op0=ALU.mult, op1=ALU.add), reads=[rqB], writes=[rqB])
                        op("act", lambda e: e.activation(out=rq[:, 8:12], in_=rq[:, 4:8], func=AF.Sqrt), reads=[rqB], writes=[rqB])
                        op("dve", lambda e: e.reciprocal(out=rq[:, 12:16], in_=rq[:, 8:12]), reads=[rqB], writes=[rqB])
                        for s in range(4):
                            op("dve", lambda e: e.scalar_tensor_tensor(out=ob[:, s, :], in0=od4[:, s, :], scalar=rq[:, 12 + s:13 + s], in1=sg[:],
                                                                       op0=ALU.mult, op1=ALU.mult), reads=[odB, rqB, sgB], writes=[obB])
                    for s in range(4):
                        op("pe", lambda e: e.transpose(out=tpc[0:dv, s * 128:(s + 1) * 128], in_=ob[:, s, 0:dv], identity=ident[:]),
                           reads=[obB, identB], writes=[tpcB], sig=(s == 3))
                    op("dve", lambda e: e.tensor_copy(out=ast[ai][0:dv, :], in_=tpc[0:dv, 0:512]), reads=[tpcB], writes=[astB[ai]])
                    if not moba:
                        dma("pool", T("ATT")[hh, :, qsl], ast[ai][:], reads=[astB[ai]], writes=[scrB["ATT"]])
                    else:
                        dma("pool", T("ATT")[8 + hm // 2, (hm % 2) * 64:(hm % 2) * 64 + 64, qsl], ast[ai][0:64, :], reads=[astB[ai]],
                            writes=[scrB["ATT"]])

                spk = 2 if moba else 1
                flat = [(r, st) for r in range(nslots) for st in range((16 * r + 16) // spk)]
                slot_banks = set()
                nxt = emit_qk(*flat[0])
                deferred = None
                for idx, (r, st) in enumerate(flat):
                    cur = nxt
                    if idx + 1 < len(flat):
                        nxt = emit_qk(*flat[idx + 1])
                    if st == 0:
                        slot_banks.clear()
                    emit_exp_av(r, st, *cur)
                    if deferred is not None and deferred[1] == idx:
                        emit_epi_tail(deferred[0]); deferred = None
                    if st == (16 * r + 16) // spk - 1:
                        emit_epi_dve(r)
                        if idx + 1 < len(flat):
                            deferred = (r, idx + 10)
                        else:
                            emit_epi_tail(r)
            S.barrier()
        if stop_after in ("C", "C1"):
            return nc, S, used_inputs

        with ExitStack() as ph:
            lng = [sbt(ph, f"lng{i}", [128, D], F32) for i in range(6)]; lngB = S.buf("lng")
            for i, a in enumerate((T("ln1_g"), T("ln1_b"), T("ln2_g"), T("ln2_b"), T("ln3_g"), T("ln3_b"))):
                dma("sp", lng[i][:], a.partition_broadcast(128), writes=[lngB])
            ones = sbt(ph, "ones", [128, 128], BF16); onesB = S.buf("ones")
            op("dve", lambda e: e.memset(ones[:], 1.0), writes=[onesB])
            NW = 4
            wr = [sbt(ph, f"wr{i}", [128, 4096], BF16) for i in range(NW)]; wrB = S.bufs("wr", NW)
            wcnt = [0]

            def wload(src, kc, col0, ncol, row0=0):
                i = wcnt[0] % NW; wcnt[0] += 1
                v = wr[i][:, 0:kc * ncol].rearrange("p (c n) -> p c n", n=ncol)
                dma("sp", v, src[row0:row0 + kc * 128, col0:col0 + ncol].rearrange("(c p) n -> p c n", p=128),
                    reads=[scrB["wb"]], writes=[wrB[i]])
                return v, wrB[i]

            kxT = sbt(ph, "kxT", [128, 8, 256], BF16); kxB = S.buf("kxT")
            vx = sbt(ph, "vx", [128, 2, D], BF16); vxB = S.buf("vx")
            lst = ln_state(ph, "d_")
            ph2 = ExitStack()
            memf = sbt(ph2, "memf", [128, 2, D], F32); memB = S.buf("memf")
            memb = sbt(ph2, "memb", [128, 2, D], BF16); membB = S.buf("memb")
            memT = sbt(ph2, "memT", [128, 8, 256], BF16); memTB = S.buf("memT")
            tps = [pst(ph, f"tpd{i}", [128, 1024], BF16) for i in range(2)]; tpB = S.bufs("tpd", 2)
            mps = [pst(ph, f"mpd{i}", [128, 512], F32) for i in range(6)]; mpB = S.bufs("mpd", 6)
            tcnt = [0]; mcnt = [0]

            def nextps():
                i = mcnt[0] % 2; mcnt[0] += 1
                return mps[i], mpB[i]

            dma("sp", memf[:], T("mem_b").rearrange("(n p) d -> p n d", p=128), writes=[memB])
            op("dve", lambda e: e.tensor_copy(out=memb[:], in_=memf[:]), reads=[memB], writes=[membB])
            transpose_to(memb, membB, memT, memTB, tps, tpB, tcnt, nsub=2)
            for half in range(2):
                wv_, wB_ = wload(T("wb_k"), 8, half * 512, 512)
                for g4 in range(4):
                    ps, psB = nextps()
                    for c in range(8):
                        op("pe", lambda e: e.matmul(ps[:, 0:256], lhsT=wv_[:, c, g4 * 128:(g4 + 1) * 128], rhs=memT[:, c, :],
                                                    start=(c == 0), stop=(c == 7)), reads=[wB_, memTB], writes=[psB], sig=(c == 7))
                    op(ev_eng(), lambda e: cp(e, kxT[:, half * 4 + g4, :], ps[:, 0:256]), reads=[psB], writes=[kxB])
            for half in range(2):
                wv_, wB_ = wload(T("wb_v"), 8, half * 512, 512)
                for mt in range(2):
                    ps, psB = nextps()
                    for c in range(8):
                        op("pe", lambda e: e.matmul(ps[:], lhsT=memT[:, c, mt * 128:(mt + 1) * 128], rhs=wv_[:, c, :],
                                                    start=(c == 0), stop=(c == 7)), reads=[wB_, memTB], writes=[psB], sig=(c == 7))
                    op(ev_eng(), lambda e: cp(e, vx[:, mt, half * 512:(half + 1) * 512], ps[:]), reads=[psB], writes=[vxB])

            S.barrier()
            ph2.close()
            attT = sbt(ph, "attT", [128, 12, 512], BF16); attB = S.buf("attT")
            gT = sbt(ph, "gT", [128, 16, 512], BF16); gTB = S.buf("gT")
            ydT = sbt(ph, "ydT", [128, 8, 512], BF16); ydB = S.buf("ydT")
            mixT = sbt(ph, "mixT", [128, 8, 512], BF16); mixB = S.buf("mixT")
            tmpb = sbt(ph, "tmpb", [128, 512], BF16); tmpB = S.buf("tmpb")
            ha = sbt(ph, "ha", [128, 4, D], F32); haB = S.buf("ha")
            hbf = sbt(ph, "hbf", [128, 4, D], F32); hbfB = S.buf("hbf")
            hbb = sbt(ph, "hbb2", [128, 4, D], BF16); hbbB = S.buf("hbb2")
            hT = sbt(ph, "hTd", [128, 8, 512], BF16); hTB = S.buf("hTd")
            qxT, qxB = ydT, ydB
            pTs = [sbt(ph, f"pTx{i}", [128, 512], BF16) for i in range(2)]; pTB = S.bufs("pTx", 2)
            rs = sbt(ph, "rs", [128, 512], F32); rsB = S.buf("rs")
            oT, oTB = mixT, mixB
            rl = sbt(ph, "rl", [128, 512], BF16); rlB = S.buf("rl")
            actT = sbt(ph, "actT", [128, 32, 512], BF16); actB = S.buf("actT")
            def fm_proj(wsrc, kc, ncolgroups, src, srcB, evac, row0=0, gpw=None):
                gpw = gpw or (4096 // (kc * 128))
                for g0 in range(0, ncolgroups, gpw):
                    n = min(gpw, ncolgroups - g0)
                    wv_, wB_ = wload(wsrc, kc, g0 * 128, n * 128, row0=row0)
                    for gi in range(n):
                        ps, psB = nextps()
                        for c in range(kc):
                            op("pe", lambda e: e.matmul(ps[:], lhsT=wv_[:, c, gi * 128:(gi + 1) * 128], rhs=src[:, c, :],
                                                        start=(c == 0), stop=(c == kc - 1)), reads=[wB_, srcB], writes=[psB], sig=(c == kc - 1))
                        evac(g0 + gi, ps, psB)

            def tm_proj(wsrc, nk, src, srcB, evac):
                for half in range(2):
                    for kg in range(nk // 8):
                        wv_, wB_ = wload(wsrc, 8, half * 512, 512, row0=kg * 1024)
                        for sub in range(4):
                            for c in range(8):
                                k = kg * 8 + c
                                op("pe", lambda e: e.matmul(mps[2 + sub][:], lhsT=src[:, k, sub * 128:(sub + 1) * 128], rhs=wv_[:, c, :],
                                                            start=(k == 0), stop=(k == nk - 1)),
                                   reads=[wB_, srcB], writes=[mpB[2 + sub]], sig=(c == 7))
                    for sub in range(4):
                        evac(half, sub, mps[2 + sub], mpB[2 + sub])

            def resid_evac(hin, hinB, hout, houtB):
                def f(half, sub, ps, psB):
                    cs = slice(half * 512, (half + 1) * 512)
                    op("dve", lambda e: e.scalar_tensor_tensor(out=hout[:, sub, cs], in0=hin[:, sub, cs], scalar=ALPHA, in1=ps[:],
                                                               op0=ALU.mult, op1=ALU.add), reads=[hinB, psB], writes=[houtB])
                return f

            ntile = 8 if stop_after != "D1" else 1
            for t in range(ntile):
                tsl = slice(t * 512, (t + 1) * 512)
                dma("sp", attT[:], T("ATT")[:, :, tsl].rearrange("g p t -> p g t"), reads=[scrB["ATT"]], writes=[attB])
                dma("sp", gT[:], T("Gs")[:, :, tsl].rearrange("g p t -> p g t"), reads=[scrB["Gs"]], writes=[gTB])
                dma("sp", ha[:], T("Hs")[tsl, :].rearrange("(n p) d -> p n d", p=128), reads=[scrB["Hs"]], writes=[haB])
                fm_proj(T("wb_brd"), 8, 8, attT, attB,
                        lambda g, ps, psB: op("dve", lambda e: e.tensor_tensor(out=ydT[:, g, :], in0=ps[:], in1=gT[:, g, :], op=ALU.mult),
                                              reads=[psB, gTB], writes=[ydB]))

                def ev_m(g, ps, psB):
                    op("dve", lambda e: e.tensor_tensor(out=tmpb[:], in0=ps[:], in1=gT[:, 8 + g, :], op=ALU.mult),
                       reads=[psB, gTB], writes=[tmpB])
                    op("pool", lambda e: e.tensor_tensor(out=mixT[:, g, :], in0=tmpb[:], in1=ydT[:, g, :], op=ALU.add),
                       reads=[tmpB, ydB], writes=[mixB])
                fm_proj(T("wb_brm"), 4, 8, attT[:, 8:12, :], attB, ev_m)
                tm_proj(T("wb_out"), 8, mixT, mixB, resid_evac(ha, haB, hbf, hbfB))
                layer_norm(lst, hbf, hbfB, lng[0], lng[1], lngB, hbf, hbfB, hbb, hbbB)
                transpose_to(hbb, hbbB, hT, hTB, tps, tpB, tcnt)
                fm_proj(T("wb_q"), 8, 8, hT, hTB, lambda g, ps, psB: op(ev_eng(), lambda e: cp(e, qxT[:, g, :], ps[:]), reads=[psB], writes=[qxB]))
                for hd in range(4):
                    for mt in range(2):
                        ps, psB = nextps()
                        for c in range(2):
                            op("pe", lambda e: e.matmul(ps[:], lhsT=kxT[:, 2 * hd + c, mt * 128:(mt + 1) * 128], rhs=qxT[:, 2 * hd + c, :],
                                                        start=(c == 0), stop=(c == 1)), reads=[kxB, qxB], writes=[psB], sig=(c == 1))
                        op("act", lambda e: e.activation(out=pTs[mt][:], in_=ps[:], func=AF.Exp, scale=1.0 / 16), reads=[psB], writes=[pTB[mt]])
                    ps, psB = nextps()
                    for mt in range(2):
                        op("pe", lambda e: e.matmul(ps[:], lhsT=ones[:], rhs=pTs[mt][:], start=(mt == 0), stop=(mt == 1)),
                           reads=[onesB, pTB[mt]], writes=[psB], sig=(mt == 1))
                    op("dve", lambda e: e.reciprocal(out=rs[:], in_=ps[:]), reads=[psB], writes=[rsB])
                    for dc in range(2):
                        ps, psB = nextps()
                        for mt in range(2):
                            op("pe", lambda e: e.matmul(ps[:], lhsT=vx[:, mt, hd * 256 + dc * 128:hd * 256 + (dc + 1) * 128], rhs=pTs[mt][:],
                                                        start=(mt == 0), stop=(mt == 1)), reads=[vxB, pTB[mt]], writes=[psB], sig=(mt == 1))
                        op("dve", lambda e: e.tensor_tensor(out=oT[:, 2 * hd + dc, :], in0=ps[:], in1=rs[:], op=ALU.mult),
                           reads=[psB, rsB], writes=[oTB])
                tm_proj(T("wb_o"), 8, oT, oTB, resid_evac(hbf, hbfB, ha, haB))
                layer_norm(lst, ha, haB, lng[2], lng[3], lngB, ha, haB, hbb, hbbB)
                transpose_to(hbb, hbbB, hT, hTB, tps, tpB, tcnt)

                def ev_f(g, ps, psB):
                    op("act", lambda e: e.activation(out=rl[:], in_=ps[:], func=AF.Relu), reads=[psB], writes=[rlB])
                    op(("dve", "pool")[g % 2], lambda e: e.tensor_tensor(out=actT[:, g, :], in0=rl[:], in1=rl[:], op=ALU.mult),
                       reads=[rlB], writes=[actB])
                fm_proj(T("wb_f1"), 8, 32, hT, hTB, ev_f)
                tm_proj(T("wb_f2"), 32, actT, actB, resid_evac(ha, haB, hbf, hbfB))
                layer_norm(lst, hbf, hbfB, lng[4], lng[5], lngB, hbf, hbfB, None, None)
                dma("sp", T("out_d")[tsl, :].rearrange("(n p) d -> p n d", p=128), hbf[:], reads=[hbfB], writes=[scrB["out"]])
            S.barrier()
        return nc, S, used_inputs


def _rel_bucket(n):
    n = np.asarray(n)
    nf = np.maximum(n, 1).astype(np.float32)
    large = 16 + (np.log(nf / np.float32(16)) / np.float32(math.log(128 / 16)) * np.float32(16)).astype(np.int32)
    large = np.minimum(large, 31)
    return np.where(n < 16, n, large)


def make_inputs(i, inputs):
    b, j = i // 4, i % 4
    f = lambda a: np.ascontiguousarray(np.asarray(a, dtype=np.float32))
    x = np.asarray(inputs["x"], dtype=np.float32)
    qts = [4 * r + j for r in range(8)]
    m = {}
    m["x_b"] = f(x[b])
    m["x_own"] = f(np.concatenate([x[b, 512 * q:512 * q + 512] for q in qts], axis=0))
    m["mem_b"] = f(inputs["mem"][b])
    for k in ("ln_in_g", "ln_in_b", "rel_table"):
        m[k] = f(inputs[k])
    for k in ("w_in", "b_gate", "lam_q1", "lam_k1", "lam_q2", "lam_k2", "diff_sub_g", "w_br_diff", "w_br_moba", "w_out",
              "ln1_g", "ln1_b", "wq_x", "wk_x", "wv_x", "wo_x", "ln2_g", "ln2_b", "w_ff1", "w_ff2", "ln3_g", "ln3_b"):
        m[k] = f(np.asarray(inputs[k])[0])
    rel = np.asarray(inputs["rel_table"], dtype=np.float32)
    m0 = 512 * j - 1920
    mm = np.arange(2560)[None, :] + m0
    kk = np.arange(128)[:, None]
    dist = mm - kk
    bk = _rel_bucket(np.maximum(dist, 0))
    tab = rel[bk]
    tab = np.where((dist >= 0)[:, :, None], tab, np.float32(NEG))
    m["t2"] = f(np.transpose(tab, (2, 0, 1)))
    blk = np.arange(NT) // 256
    m["eind"] = (blk[None, :] == np.arange(64)[:, None]).astype(ml_dtypes.bfloat16)
    pb = np.zeros((8, 4, 128, 64), np.float32); pv = np.zeros_like(pb); ot = np.full_like(pb, NEG)
    for r in range(8):
        for s in range(4):
            cur = 2 * (4 * r + j) + s // 2
            pb[r, s, :, cur:] = -1e30
            pv[r, s, :, :cur] = 1.0
            ot[r, s, :, cur] = 0.0
    m["pastbias"], m["pastvalid"], m["ownterm"] = pb, pv, ot
    m["ident"] = np.eye(128).astype(ml_dtypes.bfloat16)
    return m


def kernel(**inputs):
    nc, S, used = build_program()
    in_maps = [{k: v for k, v in make_inputs(i, inputs).items() if k in used} for i in range(8)]
    res = run_bass_kernel_spmd(nc, in_maps, core_ids=list(range(8)))
    out = np.zeros((2, NT, D), np.float32)
    for i in range(8):
        b, j = i // 4, i % 4
        o = res.results[i]["out"]
        for r in range(8):
            q = 4 * r + j
            out[b, 512 * q:512 * q + 512] = o[r * 512:(r + 1) * 512]
    return out
```
